# Optimizing a Trainium2 kernel written in Bass

```python
import jax, jax.numpy as jnp
from jax import lax
import numpy as np

D_MODEL = 4096
BATCH = 2
SEQ = 8192
DEPTH = 4

BLOCK = 128
EPS = 1e-6
NEG = -1e30
A_GROUPS = ((128, 1), (512, 4), (2048, 16))
A_HEADS = 6
A_HEAD_DIM = 128
A_WIDTH = A_HEADS * A_HEAD_DIM
A_QKV = len(A_GROUPS) * A_WIDTH
B_Q_HEADS = 16
B_KV_HEADS = 2
B_HEAD_DIM = 64
B_WINDOW = 128
B_WIDTH = B_Q_HEADS * B_HEAD_DIM
B_KV = B_KV_HEADS * B_HEAD_DIM
C_HEADS = 4
C_QK_DIM = 128
C_V_DIM = 256
C_QK = C_HEADS * C_QK_DIM
C_WIDTH = C_HEADS * C_V_DIM
C_CHUNK = 128
ROT_BASE = 10000.0
GATE_RANK = 256
N_BRANCH = 3
MIX_WIDTH = A_WIDTH + B_WIDTH + C_WIDTH
IN_WIDTH = 3 * A_QKV + B_WIDTH + 2 * B_KV + 2 * C_QK + 2 * C_WIDTH + GATE_RANK
D_FF = 4 * D_MODEL

kernel_name = "hybrid_dilated_swa_retention_block"


def rmsnorm(x, g):
    xf = x.astype(jnp.float32)
    y = xf * lax.rsqrt(jnp.mean(xf * xf, axis=-1, keepdims=True) + EPS)
    return (y * g.astype(jnp.float32)).astype(x.dtype)


def banded_attention(q, k, v, max_dist, sinks=None):
    n, L, hq, dh = q.shape
    hk = k.shape[2]
    grp = hq // hk
    nb = -(-L // BLOCK)
    padw = ((0, 0), (0, nb * BLOCK - L), (0, 0), (0, 0))
    qb = jnp.pad(q, padw).astype(jnp.float32).reshape(n, nb, BLOCK, hk, grp, dh)
    kb = jnp.pad(k, padw).astype(jnp.float32).reshape(n, nb, BLOCK, hk, dh)
    vb = jnp.pad(v, padw).astype(jnp.float32).reshape(n, nb, BLOCK, hk, dh)
    prev = lambda t: jnp.pad(t, ((0, 0), (1, 0), (0, 0), (0, 0), (0, 0)))[:, :-1]
    kw = jnp.concatenate([prev(kb), kb], axis=2)
    vw = jnp.concatenate([prev(vb), vb], axis=2)
    s = jnp.einsum('nbqhgd,nbkhd->nbhgqk', qb, kw) * (dh ** -0.5)
    qi = jnp.arange(BLOCK)[:, None]
    kj = jnp.arange(2 * BLOCK)[None, :]
    dist = BLOCK + qi - kj
    band = (dist >= 0) & (dist <= max_dist)
    valid = band[None] & ((jnp.arange(nb)[:, None, None] > 0) | (kj >= BLOCK)[None])
    s = jnp.where(valid[None, :, None, None], s, NEG)
    m = s.max(-1)
    if sinks is not None:
        sk = sinks.astype(jnp.float32).reshape(hk, grp)[None, None, :, :, None]
        m = jnp.maximum(m, sk)
    p = jnp.exp(s - m[..., None])
    den = p.sum(-1)
    if sinks is not None:
        den = den + jnp.exp(sk - m)
    o = jnp.einsum('nbhgqk,nbkhd->nbqhgd', p, vw)
    den_t = den.transpose(0, 1, 4, 2, 3)
    o = (o / den_t[..., None]).reshape(n, nb * BLOCK, hq, dh)[:, :L]
    lse = (m.transpose(0, 1, 4, 2, 3) + jnp.log(den_t)).reshape(n, nb * BLOCK, hq)[:, :L]
    return o, lse


def to_residues(t, r):
    b, s = t.shape[:2]
    t = jnp.swapaxes(t.reshape(b, s // r, r, *t.shape[2:]), 1, 2)
    return t.reshape(b * r, s // r, *t.shape[3:])


def from_residues(t, b):
    br, L = t.shape[:2]
    r = br // b
    t = jnp.swapaxes(t.reshape(b, r, L, *t.shape[2:]), 1, 2)
    return t.reshape(b, L * r, *t.shape[3:])


def dilated_attention(q, k, v):
    b = q.shape[0]
    outs, lses = [], []
    for g, (w, r) in enumerate(A_GROUPS):
        o, lse = banded_attention(to_residues(q[:, :, g], r), to_residues(k[:, :, g], r),
                                  to_residues(v[:, :, g], r), w // r)
        outs.append(from_residues(o, b))
        lses.append(from_residues(lse, b))
    o = jnp.stack(outs, axis=2)
    wgt = jax.nn.softmax(jnp.stack(lses, axis=2), axis=2)
    return jnp.einsum('bsghd,bsgh->bshd', o, wgt)


def xpos_rotate(t, cos, sin):
    t1 = t[..., ::2]
    t2 = t[..., 1::2]
    rot = jnp.stack([-t2, t1], axis=-1).reshape(t.shape)
    return t * cos[:, None, :] + rot * sin[:, None, :]


def retention(q, k, v):
    b, s, h, dk = q.shape
    dv = v.shape[-1]
    nc = s // C_CHUNK
    log_g = jnp.log1p(-(2.0 ** (-5.0 - jnp.arange(h, dtype=jnp.float32))))
    qc = q.astype(jnp.float32).reshape(b, nc, C_CHUNK, h, dk)
    kc = k.astype(jnp.float32).reshape(b, nc, C_CHUNK, h, dk)
    vc = v.astype(jnp.float32).reshape(b, nc, C_CHUNK, h, dv)
    idx = jnp.arange(C_CHUNK, dtype=jnp.float32)
    rel = idx[:, None] - idx[None, :]
    decay = jnp.where(rel >= 0, jnp.exp(log_g[:, None, None] * jnp.maximum(rel, 0.0)), 0.0)
    inner = jnp.einsum('bcihd,bcjhd->bchij', qc, kc) * decay
    o_in = jnp.einsum('bchij,bcjhe->bcihe', inner, vc)
    k_dec = kc * jnp.exp(log_g * (C_CHUNK - 1 - idx)[:, None])[..., None]
    kv = jnp.einsum('bcjhd,bcjhe->bchde', k_dec, vc)
    chunk_decay = jnp.exp(log_g * C_CHUNK)[:, None, None]

    def step(state, kv_c):
        return chunk_decay * state + kv_c, state

    _, prev = lax.scan(step, jnp.zeros((b, h, dk, dv), jnp.float32), jnp.moveaxis(kv, 1, 0))
    prev = jnp.moveaxis(prev, 0, 1)
    q_dec = qc * jnp.exp(log_g * (idx + 1.0)[:, None])[..., None]
    o_x = jnp.einsum('bcihd,bchde->bcihe', q_dec, prev)
    return (o_in + o_x).reshape(b, s, h, dv)


def hybrid_layer(x, norm1_g, w_in, qn_a, kn_a, qn_b, kn_b, sinks, w_branch, w_gate_up, b_gate,
                 w_out, norm2_g, w_ff1, w_ff2, cos, sin):
    b, s, _ = x.shape
    dt = x.dtype
    hn = rmsnorm(x, norm1_g)
    proj = jnp.einsum('bsd,de->bse', hn, w_in)
    sizes = (A_QKV, A_QKV, A_QKV, B_WIDTH, B_KV, B_KV, C_QK, C_QK, C_WIDTH, C_WIDTH, GATE_RANK)
    cuts = np.cumsum(sizes)[:-1].tolist()
    qa, ka, va, qb, kb, vb, qc, kc, vc, gc, gl = jnp.split(proj, cuts, axis=-1)

    shp_a = (b, s, len(A_GROUPS), A_HEADS, A_HEAD_DIM)
    o_a = dilated_attention(rmsnorm(qa.reshape(shp_a), qn_a), rmsnorm(ka.reshape(shp_a), kn_a),
                            va.reshape(shp_a)).reshape(b, s, A_WIDTH)

    qb = rmsnorm(qb.reshape(b, s, B_Q_HEADS, B_HEAD_DIM), qn_b)
    kb = rmsnorm(kb.reshape(b, s, B_KV_HEADS, B_HEAD_DIM), kn_b)
    vb = vb.reshape(b, s, B_KV_HEADS, B_HEAD_DIM)
    o_b, _ = banded_attention(qb, kb, vb, B_WINDOW - 1, sinks)
    o_b = o_b.reshape(b, s, B_WIDTH)

    qc = xpos_rotate(qc.reshape(b, s, C_HEADS, C_QK_DIM), cos, sin)
    kc = xpos_rotate(kc.reshape(b, s, C_HEADS, C_QK_DIM), cos, sin) * (C_QK_DIM ** -0.5)
    r = retention(qc, kc, vc.reshape(b, s, C_HEADS, C_V_DIM))
    r = r * lax.rsqrt(jnp.mean(r * r, axis=-1, keepdims=True) + EPS)
    o_c = r.reshape(b, s, C_WIDTH) * jax.nn.silu(gc.astype(jnp.float32))

    w_rows = jnp.split(w_branch, [A_WIDTH, A_WIDTH + B_WIDTH], axis=0)
    mixed = jnp.zeros((b, s, D_MODEL), jnp.float32)
    for i, o_br in enumerate((o_a, o_b, o_c)):
        gate = jax.nn.sigmoid((jnp.einsum('bsr,re->bse', gl, w_gate_up[:, i * D_MODEL:(i + 1) * D_MODEL])
                               + b_gate[i * D_MODEL:(i + 1) * D_MODEL]).astype(jnp.float32))
        mixed = mixed + gate * jnp.einsum('bse,ed->bsd', o_br.astype(dt), w_rows[i]).astype(jnp.float32)
    x = x + jnp.einsum('bsd,de->bse', mixed.astype(dt), w_out)

    hn = rmsnorm(x, norm2_g)
    hid = jnp.square(jax.nn.relu(jnp.einsum('bsd,df->bsf', hn, w_ff1)))
    return x + jnp.einsum('bsf,fd->bsd', hid, w_ff2)


def setup_inputs(seed: int = 0) -> dict:
    key = jax.random.key(seed)
    ks = jax.random.split(key, 20)
    f = jnp.float32
    nrm = lambda k, shape, scale: jax.random.normal(k, shape, f) * scale
    w_branch = jnp.concatenate([
        nrm(ks[9], (DEPTH, A_WIDTH, D_MODEL), A_WIDTH ** -0.5),
        nrm(ks[10], (DEPTH, B_WIDTH, D_MODEL), B_WIDTH ** -0.5),
        nrm(ks[11], (DEPTH, C_WIDTH, D_MODEL), C_WIDTH ** -0.5)], axis=1)
    return {
        "x": nrm(ks[0], (BATCH, SEQ, D_MODEL), 1.0),
        "norm1_g": 1.0 + nrm(ks[1], (DEPTH, D_MODEL), 0.02),
        "w_in": nrm(ks[2], (DEPTH, D_MODEL, IN_WIDTH), D_MODEL ** -0.5),
        "qn_a": 1.0 + nrm(ks[3], (DEPTH, A_HEAD_DIM), 0.02),
        "kn_a": 1.0 + nrm(ks[4], (DEPTH, A_HEAD_DIM), 0.02),
        "qn_b": 1.0 + nrm(ks[5], (DEPTH, B_HEAD_DIM), 0.02),
        "kn_b": 1.0 + nrm(ks[6], (DEPTH, B_HEAD_DIM), 0.02),
        "sinks": nrm(ks[7], (DEPTH, B_Q_HEADS), 0.5),
        "w_branch": w_branch,
        "w_gate_up": nrm(ks[12], (DEPTH, GATE_RANK, N_BRANCH * D_MODEL), GATE_RANK ** -0.5),
        "b_gate": nrm(ks[13], (DEPTH, N_BRANCH * D_MODEL), 0.1),
        "w_out": nrm(ks[14], (DEPTH, D_MODEL, D_MODEL), D_MODEL ** -0.5),
        "norm2_g": 1.0 + nrm(ks[15], (DEPTH, D_MODEL), 0.02),
        "w_ff1": nrm(ks[16], (DEPTH, D_MODEL, D_FF), D_MODEL ** -0.5),
        "w_ff2": nrm(ks[17], (DEPTH, D_FF, D_MODEL), D_FF ** -0.5),
    }


def reference(x, norm1_g, w_in, qn_a, kn_a, qn_b, kn_b, sinks, w_branch, w_gate_up, b_gate,
              w_out, norm2_g, w_ff1, w_ff2):
    s = x.shape[1]
    pos = jnp.arange(s, dtype=jnp.float32)
    inv = 1.0 / (ROT_BASE ** jnp.linspace(0.0, 1.0, C_QK_DIM // 2, dtype=jnp.float32))
    ang = pos[:, None] * jnp.repeat(inv, 2)[None, :]
    cos, sin = jnp.cos(ang), jnp.sin(ang)
    for i in range(DEPTH):
        x = hybrid_layer(x, norm1_g[i], w_in[i], qn_a[i], kn_a[i], qn_b[i], kn_b[i], sinks[i],
                         w_branch[i], w_gate_up[i], b_gate[i], w_out[i], norm2_g[i],
                         w_ff1[i], w_ff2[i], cos, sin)
    return x
```

```python
import contextlib
import numpy as np
import concourse.bass as bass
import concourse.mybir as mybir
from concourse.bass_utils import run_bass_kernel_spmd

F32 = mybir.dt.float32
BF16 = mybir.dt.bfloat16
AF = mybir.ActivationFunctionType
ALU = mybir.AluOpType

EPS = 1e-6
NEG = -30000.0
IN_W = 11520
A_GROUPS = ((128, 1), (512, 4), (2048, 16))
C_QA, C_KA, C_VA = 0, 18, 36
C_QB, C_KB, C_VB = 54, 62, 63
C_QC, C_KC, C_VC, C_GC, C_GL = 64, 68, 72, 80, 88
N_CH = 90


class Cfg:
    def __init__(self, D=4096, DFF=16384, S=8192, DEPTH=4, NCORE=2):
        self.D, self.DFF, self.S, self.DEPTH, self.NCORE = D, DFF, S, DEPTH, NCORE
        self.KC = D // 128
        self.T = 512
        self.NTT = S // 512
        self.FB = min(2048, DFF)
        self.SB = 2048


def _const_layout():
    names = [("ident", 128), ("ones", 128), ("blk64", 128), ("rmatT", 128),
             ("e0", 128), ("e1", 128), ("e2", 128), ("e3", 128),
             ("ones_lo", 128), ("ones_hi", 128),
             ("maskA", 256), ("maskAf", 256), ("maskB", 512), ("maskBf", 512),
             ("decay0", 128), ("decay1", 128), ("decay2", 128), ("decay3", 128),
             ("qd0", 128), ("qd1", 128), ("qd2", 128), ("qd3", 128), ("kd", 4)]
    off = {}
    o = 0
    for n, w in names:
        off[n] = (o, w)
        o += w
    return off, o


CL, NCONST = _const_layout()
NBF = CL["decay0"][0]


def make_consts():
    c = np.zeros((128, NCONST), np.float32)

    def put(n, a):
        o, w = CL[n]
        c[:, o:o + w] = a

    i = np.arange(128)
    put("ident", np.eye(128, dtype=np.float32))
    put("ones", np.ones((128, 128), np.float32))
    blk = np.zeros((128, 128), np.float32)
    blk[:64, :64] = 1
    blk[64:, 64:] = 1
    put("blk64", blk)
    R = np.zeros((128, 128), np.float32)
    for a in range(64):
        R[2 * a, 2 * a + 1] = -1.0
        R[2 * a + 1, 2 * a] = 1.0
    put("rmatT", R.T.copy())
    e0 = np.zeros((128, 128), np.float32)
    e1 = np.zeros((128, 128), np.float32)
    e2 = np.zeros((128, 128), np.float32)
    e3 = np.zeros((128, 128), np.float32)
    for m in range(64):
        e0[m, m] = 1
        e1[m, m + 64] = 1
        e2[m + 64, m] = 1
        e3[m + 64, m + 64] = 1
    put("e0", e0), put("e1", e1), put("e2", e2), put("e3", e3)
    lo = np.zeros((128, 128), np.float32)
    lo[:, :64] = 1
    hi = np.zeros((128, 128), np.float32)
    hi[:, 64:] = 1
    put("ones_lo", lo), put("ones_hi", hi)
    j = i[:, None]
    q = i[None, :]
    cur = np.where(j <= q, 0.0, NEG).astype(np.float32)
    prevA = np.where(j >= q, 0.0, NEG).astype(np.float32)
    prevB = np.where(j >= q + 1, 0.0, NEG).astype(np.float32)
    negs = np.full((128, 128), NEG, np.float32)
    put("maskA", np.concatenate([prevA, cur], 1))
    put("maskAf", np.concatenate([negs, cur], 1))
    put("maskB", np.concatenate([prevB, cur, prevB, cur], 1))
    put("maskBf", np.concatenate([negs, cur, negs, cur], 1))
    for h in range(4):
        lg = np.log1p(-(2.0 ** (-5.0 - h)))
        rel = (q - j).astype(np.float64)
        dec = np.where(rel >= 0, np.exp(lg * np.maximum(rel, 0)), 0.0)
        put(f"decay{h}", dec.astype(np.float32))
        put(f"qd{h}", np.broadcast_to(np.exp(lg * (i + 1.0))[None, :], (128, 128)).astype(np.float32))
        c[:, CL["kd"][0] + h] = np.exp(lg * (127.0 - i)).astype(np.float32)
    return c


def chunk_decay(h):
    return float(np.exp(np.log1p(-(2.0 ** (-5.0 - h))) * 128.0))


def make_rot(S):
    inv = (1.0 / (np.float32(10000.0) ** np.linspace(0.0, 1.0, 64, dtype=np.float32))).astype(np.float32)
    pos = np.arange(S, dtype=np.float32)
    ang = (pos[:, None] * np.repeat(inv, 2)[None, :]).astype(np.float32)
    cos = np.cos(ang.astype(np.float64)).astype(np.float32).T
    sin = np.sin(ang.astype(np.float64)).astype(np.float32).T
    ksc = np.float32(128 ** -0.5)
    return np.ascontiguousarray(np.stack([cos, sin, cos * ksc, sin * ksc], 0))


def layout_params(cfg, p):
    L, KC, D = cfg.DEPTH, cfg.KC, cfg.D
    g1 = p["norm1_g"].reshape(L, KC, 128).transpose(2, 0, 1).reshape(128, L * KC)
    g2 = p["norm2_g"].reshape(L, KC, 128).transpose(2, 0, 1).reshape(128, L * KC)
    qka = np.stack([p["qn_a"], p["kn_a"]], 1).transpose(2, 0, 1).reshape(128, L * 2)
    qb2 = np.concatenate([p["qn_b"], p["qn_b"]], 1)
    kb2 = np.concatenate([p["kn_b"], p["kn_b"]], 1)
    qkb = np.stack([qb2, kb2], 1).transpose(2, 0, 1).reshape(128, L * 2)
    sk = p["sinks"].reshape(L, 8, 2)
    sinkt = np.repeat(sk.transpose(2, 0, 1), 64, axis=0).reshape(128, L * 8)
    bg = p["b_gate"].reshape(L, 3, KC, 128).transpose(3, 0, 1, 2).reshape(128, L * 3 * KC)
    tab = np.concatenate([g1, g2, qka, qkb, sinkt, bg], 1).astype(np.float32)
    return np.ascontiguousarray(tab)


def ptab_offsets(cfg):
    L, KC = cfg.DEPTH, cfg.KC
    o = {}
    o["g1"] = 0
    o["g2"] = L * KC
    o["qka"] = 2 * L * KC
    o["qkb"] = o["qka"] + 2 * L
    o["sink"] = o["qkb"] + 2 * L
    o["bg"] = o["sink"] + 8 * L
    o["n"] = o["bg"] + 3 * KC * L
    return o


class Eng:
    def __init__(self, name, obj, sem):
        self.name, self.o, self.sem = name, obj, sem
        self.cnt = 0
        self.waited = {}


class TT:
    __slots__ = ("ap", "w", "r", "sem", "dcnt", "name", "excl")

    def __init__(self, ap, name, sem=None):
        self.ap, self.name, self.sem = ap, name, sem
        self.w = None
        self.r = []
        self.dcnt = 0
        self.excl = False


class Kern:
    def __init__(self, nc, cfg):
        self.nc, self.cfg = nc, cfg
        self.E = {}
        for n, o in (("pe", nc.tensor), ("act", nc.scalar), ("dve", nc.vector),
                     ("pool", nc.gpsimd), ("sp", nc.sync)):
            self.E[n] = Eng(n, o, nc.alloc_semaphore(name="sem_" + n))
        self.dsems = []
        self.free_recs = {}
        self.phase_recs = []
        self.nsem = 0
        self.alt = 0

    def dsem(self):
        s = self.nc.alloc_semaphore(name=f"dsem{self.nsem}")
        self.nsem += 1
        return s

    def sb(self, es, name, shape, dt, dma=False):
        self.uid = getattr(self, "uid", 0) + 1
        name = f"{name}_u{self.uid}"
        h = es.enter_context(self.nc.sbuf_tensor(name, list(shape), dt))
        t = TT(h[:], name)
        if dma:
            self.give_sem(t)
        return t

    def ps(self, es, name, shape, dt=F32):
        self.uid = getattr(self, "uid", 0) + 1
        name = f"{name}_u{self.uid}"
        full = 512 if dt == F32 else 1024
        h = es.enter_context(self.nc.psum_tensor(name, [128, full], dt))
        n = 1
        for d in shape[1:]:
            n *= d
        ap = h[:, 0:n]
        if len(shape) == 3:
            ap = ap.rearrange("p (a b) -> p a b", a=shape[1])
        t = TT(ap, name)
        t.excl = True
        return t

    def give_sem(self, t):
        t.sem = "lazy"
        return t

    def _bind_sem(self, t, qn):
        fl = self.free_recs.setdefault(qn, [])
        if fl:
            rec = fl.pop()
        else:
            rec = [self.dsem(), 0, qn]
            self.dsems.append(rec)
        self.phase_recs.append(rec)
        t.sem = rec

    def recycle_sems(self):
        for rec in self.phase_recs:
            self.free_recs.setdefault(rec[2], []).append(rec)
        self.phase_recs = []

    def sub(self, t, ap, name=None):
        s = TT(ap, name or t.name)
        s.sem = t.sem
        return s

    def _wait(self, eng, ev):
        key, val, h = ev
        if eng.waited.get(key, 0) >= val:
            return
        if key == "pe" and eng.name == "pe":
            return
        eng.o.wait_ge(h, val)
        eng.waited[key] = val

    def _deps(self, eng, R, W):
        for t in R:
            if t.w is not None:
                self._wait(eng, t.w)
            if t.excl:
                for ev in t.r:
                    if ev[0] != eng.name:
                        self._wait(eng, ev)
        for t in W:
            if t.w is not None:
                self._wait(eng, t.w)
            for ev in t.r:
                self._wait(eng, ev)

    def _commit(self, ev, R, W):
        for t in R:
            if len(t.r) > 24:
                last = {}
                for e in t.r:
                    if e[0] not in last or last[e[0]][1] < e[1]:
                        last[e[0]] = e
                t.r = list(last.values())
            t.r.append(ev)
        for t in W:
            t.w = ev
            t.r = []

    def op(self, en, fn, R=(), W=(), inc=True):
        eng = self.E[en]
        self._deps(eng, R, W)
        ins = fn(eng.o)
        ev = (en, eng.cnt + 1, eng.sem)
        if inc:
            eng.cnt += 1
            ins.then_inc(eng.sem, 1)
        self._commit(ev, R, W)
        return ins

    def dma(self, qn, out_ap, in_ap, R=(), W=(), semt=None, split=None):
        eng = self.E[qn]
        self._deps(eng, R, W)
        if semt.sem == "lazy":
            self._bind_sem(semt, qn)
        rec = semt.sem
        assert rec[2] == qn, (semt.name, rec[2], qn)
        pieces = [(out_ap, in_ap)]
        if split is not None:
            n_tot, step = split
            if n_tot > step:
                pieces = [(out_ap[:, a:min(a + step, n_tot)], in_ap[:, a:min(a + step, n_tot)])
                          for a in range(0, n_tot, step)]
        for o_, i_ in pieces:
            rec[1] += 1
            ins = eng.o.dma_start(out=o_, in_=i_)
            ins.then_inc(rec[0], 16)
        ev = (id(rec), 16 * rec[1], rec[0])
        self._commit(ev, R, W)

    def barrier(self):
        for eng in self.E.values():
            for e2 in self.E.values():
                if e2.name == "sp" or e2.cnt == 0:
                    continue
                self._wait(eng, (e2.name, e2.cnt, e2.sem))
            for rec in self.dsems:
                if rec[1]:
                    self._wait(eng, (id(rec), 16 * rec[1], rec[0]))

    def alt_eng(self, choices=("dve", "act")):
        self.alt += 1
        return choices[self.alt % len(choices)]

    def copy(self, en, out_t, out_ap, in_t, in_ap):
        if en == "act":
            self.op("act", lambda e: e.activation(out=out_ap, in_=in_ap, func=AF.Copy), R=[in_t], W=[out_t])
        else:
            self.op(en, lambda e: e.tensor_copy(out=out_ap, in_=in_ap), R=[in_t], W=[out_t])

    def rstd_from_ssq(self, out_t, out_ap, ps_t, ps_ap, n):
        self.op("act", lambda e: e.activation(out=out_ap, in_=ps_ap, func=AF.Sqrt, scale=1.0 / n,
                                              bias=self.epsb.ap), R=[ps_t, self.epsb], W=[out_t])
        self.op("dve", lambda e: e.reciprocal(out=out_ap, in_=out_ap), R=[out_t], W=[out_t])

    def run(self, io):
        nc, cfg = self.nc, self.cfg
        self.io = io
        with contextlib.ExitStack() as es:
            self.cf = self.sb(es, "cf", [128, NCONST], F32, dma=True)
            self.cb = self.sb(es, "cb", [128, NBF], BF16)
            po = ptab_offsets(cfg)
            self.po = po
            self.pt = self.sb(es, "pt", [128, po["n"]], F32, dma=True)
            self.epsb = self.sb(es, "epsb", [128, 1], F32)
            self.op("pool", lambda e: e.memset(self.epsb.ap, EPS), W=[self.epsb])
            self.dma("sp", self.cf.ap, io["consts"], W=[self.cf], semt=self.cf)
            self.dma("sp", self.pt.ap, io["ptab"], W=[self.pt], semt=self.pt)
            self.op("dve", lambda e: e.tensor_copy(out=self.cb.ap, in_=self.cf.ap[:, 0:NBF]),
                    R=[self.cf], W=[self.cb])
            import os
            stop = os.environ.get("KSTOP", "")
            steps = [("w", self.phase_weights), ("xin", self.phase_xin)]
            for l in range(cfg.DEPTH):
                steps += [(f"p1_{l}", lambda l=l: self.phase1(l)), (f"a_{l}", lambda l=l: self.attn_a(l)),
                          (f"b_{l}", lambda l=l: self.attn_b(l)), (f"c_{l}", lambda l=l: self.attn_c(l)),
                          (f"p34_{l}", lambda l=l: self.phase34(l))]
            steps += [("xout", self.phase_xout)]
            skip = os.environ.get("KSKIP", "").split(",")
            self.barrier()
            self.phase_recs = []
            for name, fn in steps:
                if name not in skip:
                    fn()
                    self.barrier()
                    self.recycle_sems()
                if name == stop:
                    break

    def cbf(self, n):
        o, w = CL[n]
        return self.cb.ap[:, o:o + w]

    def cff(self, n):
        o, w = CL[n]
        return self.cf.ap[:, o:o + w]

    def phase_weights(self):
        cfg, io = self.cfg, self.io
        PW = 2048
        with contextlib.ExitStack() as es:
            sf = [self.sb(es, f"wsf{i}", [128, PW], F32, dma=True) for i in range(3)]
            sbf = [self.sb(es, f"wsb{i}", [128, PW], BF16, dma=True) for i in range(3)]
            n = 0
            for l in range(cfg.DEPTH):
                for name in ("w_in", "w_branch", "w_gate_up", "w_out", "w_ff1", "w_ff2"):
                    W = io[name]
                    Wb = io["wb_" + name][l]
                    K, E = W.shape[1], W.shape[2]
                    for kc in range(K // 128):
                        for c0 in range(0, E, PW):
                            pw = min(PW, E - c0)
                            a, b = sf[n % 3], sbf[n % 3]
                            self.dma("sp", a.ap[:, 0:pw], W[l, kc * 128:(kc + 1) * 128, c0:c0 + pw], W=[a], semt=a)
                            en = ("dve", "act", "pool")[n % 3]
                            self.copy(en, b, b.ap[:, 0:pw], a, a.ap[:, 0:pw])
                            g0, g1 = c0 // 256, (c0 + pw) // 256
                            self.dma("pool", Wb[g0:g1, :, kc, :].rearrange("g p w -> p g w"),
                                     b.ap[:, 0:pw].rearrange("p (g w) -> p g w", w=256), R=[b], semt=b)
                            n += 1

    def phase_xin(self):
        cfg, io = self.cfg, self.io
        KC = cfg.KC
        xTv = io["xT"].rearrange("(c p) s -> p c s", p=128)
        with contextlib.ExitStack() as es:
            xin = [self.sb(es, f"xin{i}", [128, cfg.D], F32, dma=True) for i in range(4)]
            xt = self.sb(es, "xt", [128, KC, 512], F32, dma=True)
            pss = [self.ps(es, f"psx{i}", [128, 512]) for i in range(4)]
            idf = self.cff("ident")
            for tt in range(cfg.NTT):
                for i in range(4):
                    r0 = tt * 512 + i * 128
                    self.dma("pool", xin[i].ap, io["x"][r0:r0 + 128, :], W=[xin[i]], semt=xin[i])
                for c in range(KC):
                    bank = pss[c % 4]
                    for i in range(4):
                        self.op("pe", lambda e, i=i, c=c, bank=bank: e.transpose(
                            bank.ap[:, i * 128:(i + 1) * 128], xin[i].ap[:, c * 128:(c + 1) * 128], idf),
                            R=[xin[i], self.cf], W=[bank], inc=(i == 3))
                    self.copy(self.alt_eng(), xt, xt.ap[:, c, :], bank, bank.ap)
                self.dma("sp", xTv[:, :, tt * 512:(tt + 1) * 512], xt.ap, R=[xt], semt=xt, split=(KC, 8))

    def phase_xout(self):
        cfg, io = self.cfg, self.io
        KC = cfg.KC
        xTv = io["xT"].rearrange("(c p) s -> p c s", p=128)
        with contextlib.ExitStack() as es:
            xo = [self.sb(es, f"xo{i}", [128, cfg.D], F32, dma=True) for i in range(4)]
            xt = self.sb(es, "xt", [128, KC, 512], F32, dma=True)
            pss = [self.ps(es, f"psx{i}", [128, 512]) for i in range(4)]
            idf = self.cff("ident")
            n = 0
            for tt in range(cfg.NTT):
                self.dma("pool", xt.ap, xTv[:, :, tt * 512:(tt + 1) * 512], W=[xt], semt=xt, split=(KC, 8))
                for i in range(4):
                    for c4 in range(0, KC, 4):
                        nn = min(4, KC - c4)
                        bank = pss[n % 4]
                        n += 1
                        for cc in range(nn):
                            c = c4 + cc
                            self.op("pe", lambda e, i=i, c=c, cc=cc, bank=bank: e.transpose(
                                bank.ap[:, cc * 128:(cc + 1) * 128], xt.ap[:, c, i * 128:(i + 1) * 128], idf),
                                R=[xt, self.cf], W=[bank], inc=(cc == nn - 1))
                        self.copy(self.alt_eng(), xo[i], xo[i].ap[:, c4 * 128:(c4 + nn) * 128], bank,
                                  bank.ap[:, 0:nn * 128])
                    r0 = tt * 512 + i * 128
                    self.dma("sp", io["out"][r0:r0 + 128, :], xo[i].ap, R=[xo[i]], semt=xo[i])

    def gemm_setup(self, es, nbuf=3):
        self.wbufs = [self.sb(es, f"wbuf{i}", [128, 32, 256], BF16, dma=True) for i in range(nbuf)]
        self.wn = 0
        self.pgroups = [[self.ps(es, f"pg{g}_{j}", [128, 512]) for j in range(2)] for g in range(3)]
        self.pgn = 0
        self.pending = None

    def loadw(self, Wb, g, kc0, kc1):
        wt = self.wbufs[self.wn % len(self.wbufs)]
        self.wn += 1
        nk = kc1 - kc0
        self.dma("sp", wt.ap[:, 0:nk, :], Wb[g, :, kc0:kc1, :], W=[wt], semt=wt)
        return wt

    def gemm_group(self, Wb, g, kc0, kc1, rhs, epi):
        wt = self.loadw(Wb, g, kc0, kc1)
        pg = self.pgroups[self.pgn % 3]
        self.pgn += 1
        nk = kc1 - kc0
        for ki in range(nk):
            rt, rap = rhs[ki]
            for j in range(2):
                self.op("pe", lambda e, j=j, ki=ki, rap=rap: e.matmul(
                    pg[j].ap, wt.ap[:, ki, j * 128:(j + 1) * 128], rap,
                    start=(ki == 0), stop=(ki == nk - 1)),
                    R=[wt, rt], W=[pg[j]], inc=(ki == nk - 1 and j == 1))
        self.flush_epi()
        gen = epi(pg)
        next(gen, None)
        self.pending = gen

    def flush_epi(self):
        if self.pending is not None:
            for _ in self.pending:
                pass
            self.pending = None

    def rmsnorm_tile(self, xa_chunks, xa_ap, out_tiles, gcol0, sq, pss, rstd):
        cfg = self.cfg
        KC = cfg.KC
        ones = self.cbf("ones")
        for c in range(KC):
            s = sq[c % 2]
            self.op("act", lambda e, c=c, s=s: e.activation(out=s.ap, in_=xa_ap[:, c, :], func=AF.Square),
                    R=[xa_chunks[c]], W=[s])
            self.op("pe", lambda e, c=c, s=s: e.matmul(pss.ap, ones, s.ap, start=(c == 0), stop=(c == KC - 1)),
                    R=[s, self.cb], W=[pss], inc=True)
        self.rstd_from_ssq(rstd, rstd.ap, pss, pss.ap, float(cfg.D))
        for c in range(KC):
            o = out_tiles[c]
            self.op("dve", lambda e, c=c, o=o: e.scalar_tensor_tensor(
                out=o.ap, in0=xa_ap[:, c, :], scalar=self.pt.ap[:, gcol0 + c:gcol0 + c + 1], in1=rstd.ap,
                op0=ALU.mult, op1=ALU.mult), R=[xa_chunks[c], rstd, self.pt], W=[o])

    def phase1(self, l):
        cfg, io = self.cfg, self.io
        KC, T = cfg.KC, cfg.T
        po = self.po
        xTv = io["xT"].rearrange("(c p) s -> p c s", p=128)
        projv = io["projT"].rearrange("(c p) s -> c p s", p=128)
        kbxv = io["kbx"].rearrange("(c p) s -> c p s", p=128)
        sgcv = io["sgc"].rearrange("(c p) s -> c p s", p=128)
        Wb = io["wb_w_in"][l]
        with contextlib.ExitStack() as es:
            self.gemm_setup(es)
            xacc = self.sb(es, "xacc", [128, KC, T], F32, dma=True)
            xch = [self.sub(xacc, xacc.ap[:, c, :], f"xacc{c}") for c in range(KC)]
            xnm = self.sb(es, "xn", [128, KC, T], BF16)
            xn = [self.sub(xnm, xnm.ap[:, c, :], f"xn{c}") for c in range(KC)]
            sq = [self.sb(es, f"sq{i}", [128, T], BF16) for i in range(2)]
            rstd = self.sb(es, "rstd", [128, T], F32)
            ps_x = self.ps(es, "ps_x", [128, 512])
            ps_y = self.ps(es, "ps_y", [128, 512])
            rot = self.sb(es, "rot", [128, 4, T], F32, dma=True)
            ob = [self.sb(es, f"ob{i}", [128, T], BF16, dma=True) for i in range(6)]
            of = [self.sb(es, f"of{i}", [128, T], F32, dma=True) for i in range(2)]
            tf = [self.sb(es, f"tf{i}", [128, T], F32) for i in range(2)]
            r2 = [self.sb(es, f"r2{i}", [128, T], F32) for i in range(2)]
            st = {"ob": 0, "of": 0, "tf": 0, "r2": 0, "sq": 0}

            def nxt(lst, k):
                st[k] += 1
                return lst[st[k] % len(lst)]

            for tt in range(cfg.NTT):
                ts = slice(tt * T, (tt + 1) * T)
                self.dma("pool", xacc.ap, xTv[:, :, ts], W=xch, semt=xacc, split=(KC, 8))
                self.dma("pool", rot.ap, io["rot"][:, :, ts].rearrange("f p s -> p f s"), W=[rot], semt=rot)
                self.rmsnorm_tile(xch, xacc.ap, xn, po["g1"] + l * KC, sq, ps_x, rstd)
                rhs = [(xn[c], xn[c].ap) for c in range(KC)]

                def epi(pg, g, ts=ts):
                    todo = []
                    for j in range(2):
                        ch = 2 * g + j
                        p = pg[j]
                        if ch < C_VA or C_QB <= ch < C_VB:
                            s = nxt(sq, "sq")
                            self.op("act", lambda e, s=s, p=p: e.activation(out=s.ap, in_=p.ap, func=AF.Square),
                                    R=[p], W=[s])
                            todo.append(("norm", ch, p, s))
                        elif C_QC <= ch < C_VC:
                            o = nxt(ob, "ob")
                            self.copy("act", o, o.ap, p, p.ap)
                            todo.append(("rot", ch, p, o))
                        elif C_GC <= ch < C_GL:
                            o = nxt(of, "of")
                            self.op("act", lambda e, o=o, p=p: e.activation(out=o.ap, in_=p.ap, func=AF.Silu),
                                    R=[p], W=[o])
                            self.dma("pool", sgcv[ch - C_GC][:, ts], o.ap, R=[o], semt=o)
                        else:
                            o = nxt(ob, "ob")
                            self.copy(self.alt_eng(), o, o.ap, p, p.ap)
                            self.dma("pool", projv[ch][:, ts], o.ap, R=[o], semt=o)
                    yield
                    for kind, ch, p, s in todo:
                        if kind == "norm":
                            isb = ch >= C_QB
                            red = self.cbf("blk64") if isb else self.cbf("ones")
                            self.op("pe", lambda e, s=s, red=red: e.matmul(ps_y.ap, red, s.ap, start=True, stop=True),
                                    R=[s, self.cb], W=[ps_y])
                            r = nxt(r2, "r2")
                            self.rstd_from_ssq(r, r.ap, ps_y, ps_y.ap, 64.0 if isb else 128.0)
                            if isb:
                                gc = po["qkb"] + 2 * l + (1 if ch == C_KB else 0)
                            else:
                                gc = po["qka"] + 2 * l + (1 if ch >= C_KA else 0)
                            o = nxt(ob, "ob")
                            self.op("dve", lambda e, o=o, p=p, r=r, gc=gc: e.scalar_tensor_tensor(
                                out=o.ap, in0=p.ap, scalar=self.pt.ap[:, gc:gc + 1], in1=r.ap,
                                op0=ALU.mult, op1=ALU.mult), R=[p, r, self.pt], W=[o])
                            if ch == C_KB:
                                for q4 in range(4):
                                    self.op("pe", lambda e, q4=q4, o=o: e.matmul(
                                        ps_y.ap, self.cbf(f"e{q4}"), o.ap, start=True, stop=True),
                                        R=[o, self.cb], W=[ps_y])
                                    o2 = nxt(ob, "ob")
                                    self.copy(self.alt_eng(), o2, o2.ap, ps_y, ps_y.ap)
                                    self.dma("pool", kbxv[q4][:, ts], o2.ap, R=[o2], semt=o2)
                            else:
                                self.dma("pool", projv[ch][:, ts], o.ap, R=[o], semt=o)
                        else:
                            isk = ch >= C_KC
                            self.op("pe", lambda e, s=s: e.matmul(ps_y.ap, self.cbf("rmatT"), s.ap, start=True, stop=True),
                                    R=[s, self.cb], W=[ps_y])
                            t1, t2 = nxt(tf, "tf"), nxt(tf, "tf")
                            ci, si = (2, 3) if isk else (0, 1)
                            self.op("dve", lambda e, t1=t1, p=p, ci=ci: e.tensor_tensor(
                                out=t1.ap, in0=p.ap, in1=rot.ap[:, ci, :], op=ALU.mult), R=[p, rot], W=[t1])
                            self.op("dve", lambda e, t2=t2, si=si: e.tensor_tensor(
                                out=t2.ap, in0=ps_y.ap, in1=rot.ap[:, si, :], op=ALU.mult), R=[ps_y, rot], W=[t2])
                            o = nxt(ob, "ob")
                            self.op("pool", lambda e, o=o, t1=t1, t2=t2: e.tensor_tensor(
                                out=o.ap, in0=t1.ap, in1=t2.ap, op=ALU.add), R=[t1, t2], W=[o])
                            self.dma("pool", projv[ch][:, ts], o.ap, R=[o], semt=o)

                for g in range(N_CH // 2):
                    self.gemm_group(Wb, g, 0, KC, rhs, lambda pg, g=g: epi(pg, g))
                self.flush_epi()

    def phase2(self, l):
        self.attn_a(l)
        self.barrier()
        self.attn_b(l)
        self.barrier()
        self.attn_c(l)

    def attn_a(self, l):
        cfg, io = self.cfg, self.io
        S, SB = cfg.S, cfg.SB
        projv = io["projT"].rearrange("(c p) s -> c p s", p=128)
        oTv = io["oT"].rearrange("(c p) s -> c p s", p=128)
        ident, ones = self.cbf("ident"), self.cbf("ones")
        scale = 128 ** -0.5
        with contextlib.ExitStack() as es:
            qt = [self.sb(es, f"qt{i}", [128, SB], BF16, dma=True) for i in range(2)]
            kt = [self.sb(es, f"kt{i}", [128, 2 * SB], BF16, dma=True) for i in range(2)]
            vt = [self.sb(es, f"vt{i}", [128, 2 * SB], BF16, dma=True) for i in range(2)]
            acc = [self.sb(es, f"acc{i}", [128, 2, SB], F32) for i in range(2)]
            ost = [self.sb(es, f"ost{i}", [128, SB], BF16, dma=True) for i in range(2)]
            vps = [self.ps(es, f"vps{i}", [128, 256], BF16) for i in range(2)]
            sps = [self.ps(es, f"sps{i}", [128, 256]) for i in range(2)]
            ops = [self.ps(es, f"ops{i}", [128, 2, 128]) for i in range(2)]
            vtok = [self.sb(es, f"vtok{i}", [128, 256], BF16) for i in range(3)]
            pT = [self.sb(es, f"pT{i}", [128, 256], BF16) for i in range(3)]
            units = [(sbi, h, g) for sbi in range(S // SB) for h in range(6) for g in range(3)]

            def load(i):
                sbi, h, g = units[i]
                w, r = A_GROUPS[g]
                halo = 128 * r
                t0 = sbi * SB
                lo = max(0, t0 - halo)
                off = t0 - lo
                q_, k_, v_ = qt[i % 2], kt[i % 2], vt[i % 2]
                ch = g * 6 + h
                self.dma("pool", q_.ap, projv[C_QA + ch][:, t0:t0 + SB], W=[q_], semt=q_)
                self.dma("pool", k_.ap[:, 0:off + SB], projv[C_KA + ch][:, lo:t0 + SB], W=[k_], semt=k_)
                self.dma("pool", v_.ap[:, 0:off + SB], projv[C_VA + ch][:, lo:t0 + SB], W=[v_], semt=v_)

            nb = 0
            load(0)
            for i, (sbi, h, g) in enumerate(units):
                if i + 1 < len(units):
                    load(i + 1)
                w, r = A_GROUPS[g]
                halo = 128 * r
                t0 = sbi * SB
                off = t0 - max(0, t0 - halo)
                q_, k_, v_ = qt[i % 2], kt[i % 2], vt[i % 2]
                nh = sbi * 6 + h
                ac = acc[nh % 2]
                for u in range(SB // halo):
                    for rho in range(r):
                        bq = u * halo + rho
                        has_prev = (t0 + u * halo) > 0
                        ext = 127 * r + 1
                        qs = slice(bq, bq + ext, r)
                        cs = slice(off + bq, off + bq + ext, r)
                        prs = slice(off + bq - halo, off + bq - halo + ext, r)
                        vp, sp_, op_ = vps[nb % 2], sps[nb % 2], ops[nb % 2]
                        vk, p_ = vtok[nb % 3], pT[nb % 3]
                        nb += 1
                        if has_prev:
                            self.op("pe", lambda e, vp=vp, v_=v_, prs=prs: e.transpose(
                                vp.ap[:, 0:128], v_.ap[:, prs], ident), R=[v_, self.cb], W=[vp], inc=False)
                        self.op("pe", lambda e, vp=vp, v_=v_, cs=cs: e.transpose(
                            vp.ap[:, 128:256], v_.ap[:, cs], ident), R=[v_, self.cb], W=[vp])
                        c0 = 0 if has_prev else 128
                        self.copy("dve", vk, vk.ap[:, c0:256], vp, vp.ap[:, c0:256])
                        mk = self.cbf("maskA") if has_prev else self.cbf("maskAf")
                        self.op("pe", lambda e, sp_=sp_, mk=mk: e.matmul(sp_.ap, ident, mk, start=True, stop=False),
                                R=[self.cb], W=[sp_], inc=False)
                        if has_prev:
                            self.op("pe", lambda e, sp_=sp_, k_=k_, q_=q_, prs=prs, qs=qs: e.matmul(
                                sp_.ap[:, 0:128], k_.ap[:, prs], q_.ap[:, qs], start=False, stop=False),
                                R=[k_, q_], W=[sp_], inc=False)
                        self.op("pe", lambda e, sp_=sp_, k_=k_, q_=q_, cs=cs, qs=qs: e.matmul(
                            sp_.ap[:, 128:256], k_.ap[:, cs], q_.ap[:, qs], start=False, stop=True),
                            R=[k_, q_], W=[sp_])
                        self.op("act", lambda e, p_=p_, sp_=sp_: e.activation(
                            out=p_.ap, in_=sp_.ap, func=AF.Exp, scale=scale), R=[sp_], W=[p_])
                        if has_prev:
                            self.op("pe", lambda e, op_=op_, vk=vk, p_=p_: e.matmul(
                                op_.ap[:, 0, :], vk.ap[:, 0:128], p_.ap[:, 0:128], start=True, stop=False),
                                R=[vk, p_], W=[op_], inc=False)
                        self.op("pe", lambda e, op_=op_, vk=vk, p_=p_, hp=has_prev: e.matmul(
                            op_.ap[:, 0, :], vk.ap[:, 128:256], p_.ap[:, 128:256], start=(not hp), stop=True),
                            R=[vk, p_], W=[op_], inc=False)
                        if has_prev:
                            self.op("pe", lambda e, op_=op_, p_=p_: e.matmul(
                                op_.ap[:, 1, :], ones, p_.ap[:, 0:128], start=True, stop=False),
                                R=[p_, self.cb], W=[op_], inc=False)
                        self.op("pe", lambda e, op_=op_, p_=p_, hp=has_prev: e.matmul(
                            op_.ap[:, 1, :], ones, p_.ap[:, 128:256], start=(not hp), stop=True),
                            R=[p_, self.cb], W=[op_])
                        if g == 0:
                            self.op("dve", lambda e, ac=ac, op_=op_, qs=qs: e.tensor_copy(
                                out=ac.ap[:, :, qs], in_=op_.ap), R=[op_], W=[ac])
                        else:
                            self.op("dve", lambda e, ac=ac, op_=op_, qs=qs: e.tensor_tensor(
                                out=ac.ap[:, :, qs], in0=ac.ap[:, :, qs], in1=op_.ap, op=ALU.add),
                                R=[op_, ac], W=[ac])
                if g == 2:
                    o_ = ost[nh % 2]
                    self.op("dve", lambda e, ac=ac: e.reciprocal(out=ac.ap[:, 1, :], in_=ac.ap[:, 1, :]), R=[ac], W=[ac])
                    self.op("pool", lambda e, o_=o_, ac=ac: e.tensor_tensor(
                        out=o_.ap, in0=ac.ap[:, 0, :], in1=ac.ap[:, 1, :], op=ALU.mult), R=[ac], W=[o_])
                    self.dma("pool", oTv[h][:, t0:t0 + SB], o_.ap, R=[o_], semt=o_)

    def attn_b(self, l):
        cfg, io = self.cfg, self.io
        S, SB = cfg.S, cfg.SB
        po = self.po
        projv = io["projT"].rearrange("(c p) s -> c p s", p=128)
        kbxv = io["kbx"].rearrange("(c p) s -> c p s", p=128)
        oTv = io["oT"].rearrange("(c p) s -> c p s", p=128)
        ident = self.cbf("ident")
        scale = 64 ** -0.5
        NBK = SB // 128
        with contextlib.ExitStack() as es:
            esink = self.sb(es, "esink", [128, 8], F32)
            self.op("act", lambda e: e.activation(out=esink.ap, in_=self.pt.ap[:, po["sink"] + 8 * l:po["sink"] + 8 * l + 8],
                                                  func=AF.Exp), R=[self.pt], W=[esink])
            qt = [self.sb(es, f"qt{i}", [128, SB], BF16, dma=True) for i in range(2)]
            klo = [self.sb(es, f"klo{i}", [128, 128 + SB], BF16, dma=True) for i in range(2)]
            khi = [self.sb(es, f"khi{i}", [128, 128 + SB], BF16, dma=True) for i in range(2)]
            vt = [self.sb(es, f"vt{i}", [128, 128 + SB], BF16, dma=True) for i in range(2)]
            vlo = [self.sb(es, f"vlo{i}", [128, NBK + 1, 128], BF16) for i in range(2)]
            vhi = [self.sb(es, f"vhi{i}", [128, NBK + 1, 128], BF16) for i in range(2)]
            for t in vlo + vhi:
                self.op("pool", lambda e, t=t: e.memset(t.ap, 0.0), W=[t])
            ost = [self.sb(es, f"ost{i}", [128, SB], BF16, dma=True) for i in range(2)]
            vps = [self.ps(es, f"vps{i}", [128, 128], BF16) for i in range(2)]
            sps = [self.ps(es, f"sps{i}", [128, 512]) for i in range(2)]
            ops = [self.ps(es, f"ops{i}", [128, 2, 128]) for i in range(2)]
            pT = [self.sb(es, f"pT{i}", [128, 512], BF16) for i in range(3)]
            tden = [self.sb(es, f"tden{i}", [128, 128], F32) for i in range(2)]
            nkv = 0
            nq = 0
            nb = 0
            for sbi in range(S // SB):
                t0 = sbi * SB
                lo = max(0, t0 - 128)
                off = t0 - lo
                for kv in range(2):
                    kl, kh, v_ = klo[nkv % 2], khi[nkv % 2], vt[nkv % 2]
                    vl, vh = vlo[nkv % 2], vhi[nkv % 2]
                    nkv += 1
                    self.dma("pool", kl.ap[:, 0:off + SB], kbxv[2 * kv][:, lo:t0 + SB], W=[kl], semt=kl)
                    self.dma("pool", kh.ap[:, 0:off + SB], kbxv[2 * kv + 1][:, lo:t0 + SB], W=[kh], semt=kh)
                    self.dma("pool", v_.ap[:, 0:off + SB], projv[C_VB][:, lo:t0 + SB], W=[v_], semt=v_)
                    nblk = (off + SB) // 128
                    for b in range(nblk):
                        vp = vps[b % 2]
                        self.op("pe", lambda e, vp=vp, v_=v_, b=b: e.transpose(
                            vp.ap, v_.ap[:, b * 128:(b + 1) * 128], ident), R=[v_, self.cb], W=[vp])
                        self.copy("dve", vl, vl.ap[:, b, 0:64], vp, vp.ap[:, kv * 64:(kv + 1) * 64])
                        self.copy("act", vh, vh.ap[:, b, 64:128], vp, vp.ap[:, kv * 64:(kv + 1) * 64])
                    for j in range(4):
                        jj = kv * 4 + j
                        q_ = qt[nq % 2]
                        o_ = ost[nq % 2]
                        nq += 1
                        self.dma("pool", q_.ap, projv[C_QB + jj][:, t0:t0 + SB], W=[q_], semt=q_)
                        for b in range(NBK):
                            has_prev = (t0 + b * 128) > 0
                            cb_ = off // 128 + b
                            qs = slice(b * 128, (b + 1) * 128)
                            cs = slice(cb_ * 128, (cb_ + 1) * 128)
                            prs = slice((cb_ - 1) * 128, cb_ * 128)
                            sp_, op_, p_ = sps[nb % 2], ops[nb % 2], pT[nb % 3]
                            td = tden[nb % 2]
                            nb += 1
                            mk = self.cbf("maskB") if has_prev else self.cbf("maskBf")
                            self.op("pe", lambda e, sp_=sp_, mk=mk: e.matmul(sp_.ap, ident, mk, start=True, stop=False),
                                    R=[self.cb], W=[sp_], inc=False)
                            for hh, kk in enumerate((kl, kh)):
                                if has_prev:
                                    self.op("pe", lambda e, sp_=sp_, kk=kk, hh=hh, q_=q_, prs=prs, qs=qs: e.matmul(
                                        sp_.ap[:, hh * 256:hh * 256 + 128], kk.ap[:, prs], q_.ap[:, qs],
                                        start=False, stop=False), R=[kk, q_], W=[sp_], inc=False)
                                self.op("pe", lambda e, sp_=sp_, kk=kk, hh=hh, q_=q_, cs=cs, qs=qs: e.matmul(
                                    sp_.ap[:, hh * 256 + 128:hh * 256 + 256], kk.ap[:, cs], q_.ap[:, qs],
                                    start=False, stop=(hh == 1)), R=[kk, q_], W=[sp_], inc=(hh == 1))
                            self.op("act", lambda e, p_=p_, sp_=sp_: e.activation(
                                out=p_.ap, in_=sp_.ap, func=AF.Exp, scale=scale), R=[sp_], W=[p_])
                            terms = []
                            for hh, (vv, on) in enumerate(((vl, "ones_lo"), (vh, "ones_hi"))):
                                if has_prev:
                                    terms.append((vv, vv.ap[:, cb_ - 1, :], on, hh * 256))
                                terms.append((vv, vv.ap[:, cb_, :], on, hh * 256 + 128))
                            nt = len(terms)
                            for ti, (vv, vap, on, pc) in enumerate(terms):
                                self.op("pe", lambda e, op_=op_, vap=vap, p_=p_, pc=pc, ti=ti: e.matmul(
                                    op_.ap[:, 0, :], vap, p_.ap[:, pc:pc + 128], start=(ti == 0), stop=(ti == nt - 1)),
                                    R=[vv, p_], W=[op_], inc=False)
                            for ti, (vv, vap, on, pc) in enumerate(terms):
                                self.op("pe", lambda e, op_=op_, on=on, p_=p_, pc=pc, ti=ti: e.matmul(
                                    op_.ap[:, 1, :], self.cbf(on), p_.ap[:, pc:pc + 128], start=(ti == 0),
                                    stop=(ti == nt - 1)), R=[p_, self.cb], W=[op_], inc=(ti == nt - 1))
                            self.op("dve", lambda e, td=td, op_=op_, jj=jj: e.tensor_scalar(
                                out=td.ap, in0=op_.ap[:, 1, :], scalar1=esink.ap[:, jj:jj + 1], scalar2=None,
                                op0=ALU.add), R=[op_, esink], W=[td])
                            self.op("dve", lambda e, td=td: e.reciprocal(out=td.ap, in_=td.ap), R=[td], W=[td])
                            self.op("dve", lambda e, o_=o_, op_=op_, td=td, qs=qs: e.tensor_tensor(
                                out=o_.ap[:, qs], in0=op_.ap[:, 0, :], in1=td.ap, op=ALU.mult),
                                R=[op_, td], W=[o_])
                        self.dma("pool", oTv[6 + jj][:, t0:t0 + SB], o_.ap, R=[o_], semt=o_)

    def attn_c(self, l):
        cfg, io = self.cfg, self.io
        S = cfg.S
        CS = 512
        projv = io["projT"].rearrange("(c p) s -> c p s", p=128)
        sgcv = io["sgc"].rearrange("(c p) s -> c p s", p=128)
        oTv = io["oT"].rearrange("(c p) s -> c p s", p=128)
        ident, ones = self.cbf("ident"), self.cbf("ones")
        kdo = CL["kd"][0]
        import os
        lvl = int(os.environ.get("KC_LEVEL", "9"))
        with contextlib.ExitStack() as es:
            qkv = [[self.sb(es, f"qkv{i}_{h}", [128, 4, CS], BF16, dma=True) for h in range(4)] for i in range(2)]
            qsub = [[(self.give_sem(self.sub(qkv[i][h], qkv[i][h].ap[:, 0, :], "cq")),
                      self.give_sem(self.sub(qkv[i][h], qkv[i][h].ap[:, 1, :], "ck")),
                      self.give_sem(self.sub(qkv[i][h], qkv[i][h].ap[:, 2:4, :], "cv"))) for h in range(4)]
                    for i in range(2)]
            gt = [[self.sb(es, f"gt{i}_{h}", [128, 2, CS], F32, dma=True) for h in range(4)] for i in range(2)]
            state = [self.sb(es, f"state{h}", [128, 256], F32) for h in range(4)]
            sbf = [self.sb(es, f"sbf{h}", [128, 256], BF16) for h in range(4)]
            for h in range(4):
                self.op("pool", lambda e, h=h: e.memset(state[h].ap, 0.0), W=[state[h]])
                self.op("pool", lambda e, h=h: e.memset(sbf[h].ap, 0.0), W=[sbf[h]])
            ost = [self.sb(es, f"ost{i}", [128, 2, CS], BF16, dma=True) for i in range(8)]
            pa = [self.ps(es, f"pa{i}", [128, 384], BF16) for i in range(2)]
            pb = [self.ps(es, f"pb{i}", [128, 384]) for i in range(2)]
            pc = [self.ps(es, f"pc{i}", [128, 512]) for i in range(2)]
            vtok = [self.sb(es, f"vtok{i}", [128, 256], BF16) for i in range(2)]
            kdec = [self.sb(es, f"kdec{i}", [128, 128], BF16) for i in range(2)]
            inb = [self.sb(es, f"inb{i}", [128, 128], BF16) for i in range(2)]
            qdec = [self.sb(es, f"qdec{i}", [128, 128], BF16) for i in range(2)]
            sqc = [self.sb(es, f"sqc{i}", [128, 256], BF16) for i in range(2)]
            rs = [self.sb(es, f"rs{i}", [128, 256], F32) for i in range(2)]
            tm = [self.sb(es, f"tm{i}", [128, 256], F32) for i in range(2)]
            n = 0
            for ci in range(S // CS):
                t0 = ci * CS
                bufs = qkv[ci % 2]
                gts = gt[ci % 2]
                for h in range(4):
                    b = bufs[h]
                    tq, tk, tv = qsub[ci % 2][h]
                    self.dma("pool", tq.ap, projv[C_QC + h][:, t0:t0 + CS], W=[tq], semt=tq)
                    self.dma("pool", tk.ap, projv[C_KC + h][:, t0:t0 + CS], W=[tk], semt=tk)
                    self.dma("pool", tv.ap, io["projT"][(C_VC + 2 * h) * 128:(C_VC + 2 * h + 2) * 128, t0:t0 + CS]
                             .rearrange("(e p) s -> p e s", p=128), W=[tv], semt=tv)
                    self.dma("pool", gts[h].ap, io["sgc"][2 * h * 128:(2 * h + 2) * 128, t0:t0 + CS]
                             .rearrange("(e p) s -> p e s", p=128), W=[gts[h]], semt=gts[h])
                outs = [ost[(ci % 2) * 4 + h] for h in range(4)]
                for c in range(CS // 128):
                    cs = slice(c * 128, (c + 1) * 128)
                    for h in range(4):
                        b = bufs[h]
                        tq, tk, tv = qsub[ci % 2][h]
                        A, B, C = pa[n % 2], pb[n % 2], pc[n % 2]
                        vk, kd_, ib, qd_, sq_, r_, t_ = (vtok[n % 2], kdec[n % 2], inb[n % 2], qdec[n % 2],
                                                        sqc[n % 2], rs[n % 2], tm[n % 2])
                        n += 1
                        gam = chunk_decay(h)
                        if lvl < 1:
                            continue
                        self.op("pool", lambda e, qd_=qd_, b=b, cs=cs, h=h: e.tensor_tensor(
                            out=qd_.ap, in0=b.ap[:, 0, cs], in1=self.cff(f"qd{h}"), op=ALU.mult),
                            R=[tq, self.cf], W=[qd_])
                        if lvl < 2:
                            continue
                        for e_ in range(2):
                            self.op("pe", lambda e, A=A, b=b, e_=e_, cs=cs: e.transpose(
                                A.ap[:, e_ * 128:(e_ + 1) * 128], b.ap[:, 2 + e_, cs], ident),
                                R=[tv, self.cb], W=[A], inc=False)
                        self.op("pe", lambda e, A=A, b=b, cs=cs: e.transpose(A.ap[:, 256:384], b.ap[:, 1, cs], ident),
                                R=[tk, self.cb], W=[A])
                        self.copy("act", vk, vk.ap, A, A.ap[:, 0:256])
                        self.op("dve", lambda e, kd_=kd_, A=A, h=h: e.tensor_scalar(
                            out=kd_.ap, in0=A.ap[:, 256:384], scalar1=self.cf.ap[:, kdo + h:kdo + h + 1], scalar2=None,
                            op0=ALU.mult), R=[A, self.cf], W=[kd_])
                        if lvl < 3:
                            continue
                        self.op("pe", lambda e, B=B, b=b, cs=cs: e.matmul(B.ap[:, 0:128], b.ap[:, 1, cs], b.ap[:, 0, cs],
                                                                       start=True, stop=True), R=[tq, tk], W=[B])
                        self.op("dve", lambda e, ib=ib, B=B, h=h: e.tensor_tensor(
                            out=ib.ap, in0=B.ap[:, 0:128], in1=self.cff(f"decay{h}"), op=ALU.mult),
                            R=[B, self.cf], W=[ib])
                        if lvl < 4:
                            continue
                        for e_ in range(2):
                            self.op("pe", lambda e, C=C, vk=vk, ib=ib, e_=e_: e.matmul(
                                C.ap[:, e_ * 128:(e_ + 1) * 128], vk.ap[:, e_ * 128:(e_ + 1) * 128], ib.ap,
                                start=True, stop=False), R=[vk, ib], W=[C], inc=False)
                            self.op("pe", lambda e, C=C, qd_=qd_, e_=e_, h=h: e.matmul(
                                C.ap[:, e_ * 128:(e_ + 1) * 128], sbf[h].ap[:, e_ * 128:(e_ + 1) * 128], qd_.ap,
                                start=False, stop=True), R=[sbf[h], qd_], W=[C], inc=False)
                        if lvl < 5:
                            continue
                        self.op("pe", lambda e, C=C, kd_=kd_, vk=vk: e.matmul(C.ap[:, 256:512], kd_.ap, vk.ap,
                                                                           start=True, stop=True), R=[kd_, vk], W=[C])
                        self.op("dve", lambda e, h=h, C=C, gam=gam: e.scalar_tensor_tensor(
                            out=state[h].ap, in0=state[h].ap, scalar=gam, in1=C.ap[:, 256:512],
                            op0=ALU.mult, op1=ALU.add), R=[state[h], C], W=[state[h]])
                        self.copy("act", sbf[h], sbf[h].ap, state[h], state[h].ap)
                        if lvl < 6:
                            continue
                        self.op("act", lambda e, sq_=sq_, C=C: e.activation(out=sq_.ap, in_=C.ap[:, 0:256], func=AF.Square),
                                R=[C], W=[sq_])
                        for half in range(2):
                            for e_ in range(2):
                                self.op("pe", lambda e, B=B, sq_=sq_, half=half, e_=e_: e.matmul(
                                    B.ap[:, 128 + half * 128:256 + half * 128], ones, sq_.ap[:, e_ * 128:(e_ + 1) * 128],
                                    start=(e_ == 0), stop=(e_ == 1)), R=[sq_, self.cb], W=[B],
                                    inc=(half == 1 and e_ == 1))
                        self.rstd_from_ssq(r_, r_.ap, B, B.ap[:, 128:384], 256.0)
                        self.op("dve", lambda e, t_=t_, C=C, r_=r_: e.tensor_tensor(
                            out=t_.ap, in0=C.ap[:, 0:256], in1=r_.ap, op=ALU.mult), R=[C, r_], W=[t_])
                        o_ = outs[h]
                        self.op("pool", lambda e, o_=o_, t_=t_, cs=cs, h=h: e.tensor_tensor(
                            out=o_.ap[:, :, cs], in0=t_.ap.rearrange("p (e s) -> p e s", e=2), in1=gts[h].ap[:, :, cs],
                            op=ALU.mult), R=[t_, gts[h]], W=[o_])
                for h in range(4):
                    self.dma("pool", io["oT"][(14 + 2 * h) * 128:(16 + 2 * h) * 128, t0:t0 + CS]
                             .rearrange("(e p) s -> p e s", p=128), outs[h].ap, R=[outs[h]], semt=outs[h])

    def phase34(self, l):
        cfg, io = self.cfg, self.io
        KC, T, D, DFF, FB = cfg.KC, cfg.T, cfg.D, cfg.DFF, cfg.FB
        po = self.po
        xTv = io["xT"].rearrange("(c p) s -> p c s", p=128)
        oTv = io["oT"].rearrange("(c p) s -> p c s", p=128)
        glv = io["projT"][C_GL * 128:(C_GL + 2) * 128, :].rearrange("(c p) s -> p c s", p=128)
        NG = D // 256
        FC = FB // 128
        NSLOT = max(22 + 2 + KC, KC + FC)
        br_kc = ((0, 6), (6, 14), (14, 22))
        with contextlib.ExitStack() as es:
            self.gemm_setup(es, 2)
            xacc = self.sb(es, "xacc", [128, KC, T], F32, dma=True)
            xch = [self.sub(xacc, xacc.ap[:, c, :], f"xacc{c}") for c in range(KC)]
            slotm = self.sb(es, "slots", [128, NSLOT, T], BF16, dma=True)
            slots = [self.sub(slotm, slotm.ap[:, i, :], f"slot{i}") for i in range(NSLOT)]
            o_sl, gl_sl, mix_sl = slots[0:22], slots[22:24], slots[24:24 + KC]
            glsem = self.give_sem(TT(None, "glsem"))
            xn_sl, hid_sl = slots[0:KC], slots[KC:KC + FC]
            sq = [self.sb(es, f"sq{i}", [128, T], BF16) for i in range(2)]
            rstd = self.sb(es, "rstd", [128, T], F32)
            ps_x = self.ps(es, "ps_x", [128, 512])
            gsb = [self.sb(es, f"gsb{i}", [128, T], F32) for i in range(2)]
            tmp = [self.sb(es, f"tmp{i}", [128, T], F32) for i in range(3)]
            mixf = [self.sb(es, f"mixf{i}", [128, T], F32) for i in range(2)]
            st = {"g": 0, "t": 0}

            for tt in range(cfg.NTT):
                ts = slice(tt * T, (tt + 1) * T)
                self.dma("pool", slotm.ap[:, 0:22, :], oTv[:, :, ts], W=o_sl, semt=slotm, split=(22, 8))
                self.dma("pool", slotm.ap[:, 22:24, :], glv[:, :, ts], W=gl_sl, semt=glsem)
                self.dma("pool", xacc.ap, xTv[:, :, ts], W=xch, semt=xacc, split=(KC, 8))
                rhs_gl = [(gl_sl[c], gl_sl[c].ap) for c in range(2)]
                rhs_o = [(o_sl[c], o_sl[c].ap) for c in range(22)]
                for dg in range(NG):
                    for i in range(3):
                        gts = []

                        def epi_gate(pg, i=i, dg=dg, gts=gts):
                            for j in range(2):
                                st["g"] += 1
                                g_ = gsb[st["g"] % 2]
                                bc = po["bg"] + (l * 3 + i) * KC + 2 * dg + j
                                self.op("act", lambda e, g_=g_, p=pg[j], bc=bc: e.activation(
                                    out=g_.ap, in_=p.ap, func=AF.Sigmoid, bias=self.pt.ap[:, bc:bc + 1]),
                                    R=[pg[j], self.pt], W=[g_])
                                gts.append(g_)
                            yield

                        self.gemm_group(io["wb_w_gate_up"][l], i * NG + dg, 0, 2, rhs_gl, epi_gate)

                        def epi_br(pg, i=i, dg=dg, gts=gts):
                            for j in range(2):
                                g_ = gts[j]
                                mf = mixf[j]
                                p = pg[j]
                                if i == 0:
                                    self.op("dve", lambda e, mf=mf, p=p, g_=g_: e.tensor_tensor(
                                        out=mf.ap, in0=p.ap, in1=g_.ap, op=ALU.mult), R=[p, g_], W=[mf])
                                else:
                                    st["t"] += 1
                                    t_ = tmp[st["t"] % 3]
                                    self.op("dve", lambda e, t_=t_, p=p, g_=g_: e.tensor_tensor(
                                        out=t_.ap, in0=p.ap, in1=g_.ap, op=ALU.mult), R=[p, g_], W=[t_])
                                    if i == 1:
                                        self.op("pool", lambda e, mf=mf, t_=t_: e.tensor_tensor(
                                            out=mf.ap, in0=mf.ap, in1=t_.ap, op=ALU.add), R=[mf, t_], W=[mf])
                                    else:
                                        ms = mix_sl[2 * dg + j]
                                        self.op("pool", lambda e, ms=ms, mf=mf, t_=t_: e.tensor_tensor(
                                            out=ms.ap, in0=mf.ap, in1=t_.ap, op=ALU.add), R=[mf, t_], W=[ms])
                            yield

                        k0, k1 = br_kc[i]
                        self.gemm_group(io["wb_w_branch"][l], dg, k0, k1, rhs_o[k0:k1], epi_br)
                self.flush_epi()
                rhs_m = [(mix_sl[c], mix_sl[c].ap) for c in range(KC)]

                def epi_acc(pg, dg):
                    for j in range(2):
                        xc = xch[2 * dg + j]
                        self.op("dve", lambda e, xc=xc, p=pg[j]: e.tensor_tensor(
                            out=xc.ap, in0=p.ap, in1=xc.ap, op=ALU.add), R=[pg[j], xc], W=[xc])
                    yield

                for dg in range(NG):
                    self.gemm_group(io["wb_w_out"][l], dg, 0, KC, rhs_m, lambda pg, dg=dg: epi_acc(pg, dg))
                self.flush_epi()
                self.rmsnorm_tile(xch, xacc.ap, xn_sl, po["g2"] + l * KC, sq, ps_x, rstd)
                rhs_x = [(xn_sl[c], xn_sl[c].ap) for c in range(KC)]
                rhs_h = [(hid_sl[c], hid_sl[c].ap) for c in range(FC)]
                for fb in range(DFF // FB):

                    def epi_h(pg, fg):
                        for j in range(2):
                            st["t"] += 1
                            t_ = tmp[st["t"] % 3]
                            hs = hid_sl[2 * fg + j]
                            self.op("act", lambda e, t_=t_, p=pg[j]: e.activation(out=t_.ap, in_=p.ap, func=AF.Relu),
                                    R=[pg[j]], W=[t_])
                            en = self.alt_eng(("dve", "pool"))
                            self.op(en, lambda e, hs=hs, t_=t_: e.tensor_tensor(
                                out=hs.ap, in0=t_.ap, in1=t_.ap, op=ALU.mult), R=[t_], W=[hs])
                        yield

                    for fg in range(FB // 256):
                        self.gemm_group(io["wb_w_ff1"][l], fb * (FB // 256) + fg, 0, KC, rhs_x,
                                        lambda pg, fg=fg: epi_h(pg, fg))
                    self.flush_epi()
                    for dg in range(NG):
                        self.gemm_group(io["wb_w_ff2"][l], dg, fb * FC, (fb + 1) * FC, rhs_h,
                                        lambda pg, dg=dg: epi_acc(pg, dg))
                    self.flush_epi()
                self.dma("pool", xTv[:, :, ts], xacc.ap, R=xch, semt=xacc, split=(KC, 8))


def build_program(cfg):
    nc = bass.Bass("TRN2", target_bir_lowering=False)
    D, DFF, S, L = cfg.D, cfg.DFF, cfg.S, cfg.DEPTH
    io = {}

    def ext(name, shape, kind="ExternalInput", dt=F32):
        io[name] = nc.dram_tensor(name, list(shape), dt, kind=kind).ap()

    ext("x", [S, D])
    ext("w_in", [L, D, IN_W])
    ext("w_branch", [L, 2816, D])
    ext("w_gate_up", [L, 256, 3 * D])
    ext("w_out", [L, D, D])
    ext("w_ff1", [L, D, DFF])
    ext("w_ff2", [L, DFF, D])
    ext("consts", [128, NCONST])
    ext("ptab", [128, ptab_offsets(cfg)["n"]])
    ext("rot", [4, 128, S])
    ext("out", [S, D], kind="ExternalOutput")

    def scratch(name, shape, dt):
        io[name] = nc.dram_tensor(name, list(shape), dt).ap()

    scratch("xT", [D, S], F32)
    scratch("projT", [IN_W, S], BF16)
    scratch("kbx", [512, S], BF16)
    scratch("sgc", [1024, S], F32)
    scratch("oT", [2816, S], BF16)
    for name, K, E in (("w_in", D, IN_W), ("w_branch", 2816, D), ("w_gate_up", 256, 3 * D),
                       ("w_out", D, D), ("w_ff1", D, DFF), ("w_ff2", DFF, D)):
        io["wb_" + name] = [nc.dram_tensor(f"wb_{name}_{l}", [E // 256, 128, K // 128, 256], BF16).ap()
                            for l in range(L)]
    k = Kern(nc, cfg)
    k.run(io)
    return nc


def make_in_maps(cfg, inputs):
    consts = make_consts()
    rot = make_rot(cfg.S)
    ptab = layout_params(cfg, {k: np.asarray(v, np.float32) for k, v in inputs.items()
                               if k in ("norm1_g", "norm2_g", "qn_a", "kn_a", "qn_b", "kn_b", "sinks", "b_gate")})
    maps = []
    for c in range(cfg.NCORE):
        m = {"x": np.ascontiguousarray(np.asarray(inputs["x"][c], np.float32)),
             "consts": consts, "ptab": ptab, "rot": rot}
        for n in ("w_in", "w_branch", "w_gate_up", "w_out", "w_ff1", "w_ff2"):
            m[n] = np.asarray(inputs[n], np.float32)
        maps.append(m)
    return maps


def run_cfg(cfg, inputs, trace=False):
    nc = build_program(cfg)
    maps = make_in_maps(cfg, inputs)
    res = run_bass_kernel_spmd(nc, maps, core_ids=list(range(cfg.NCORE)), trace=trace)
    out = np.stack([np.asarray(r["out"]) for r in res.results], 0)
    return out.astype(np.float32), res


def kernel(**inputs):
    cfg = Cfg()
    out, _ = run_cfg(cfg, inputs)
    return out
```

```python
import contextlib
import numpy as np
import concourse.bass as bass
import concourse.mybir as mybir
from concourse.bass_utils import run_bass_kernel_spmd

F32 = mybir.dt.float32
BF16 = mybir.dt.bfloat16
AF = mybir.ActivationFunctionType
ALU = mybir.AluOpType

EPS = 1e-6
NEG = -30000.0
IN_W = 11520
A_GROUPS = ((128, 1), (512, 4), (2048, 16))
C_QA, C_KA, C_VA = 0, 18, 36
C_QB, C_KB, C_VB = 54, 62, 63
C_QC, C_KC, C_VC, C_GC, C_GL = 64, 68, 72, 80, 88
N_CH = 90


class Cfg:
    def __init__(self, D=4096, DFF=16384, SEQ=8192, DEPTH=4, NSEQ=2, P=2):
        self.D, self.DFF, self.DEPTH = D, DFF, DEPTH
        self.SEQ, self.NSEQ, self.P = SEQ, NSEQ, P
        self.S = SEQ // P
        self.NCORE = NSEQ * P
        self.groups = [[q * P + i for i in range(P)] for q in range(NSEQ)]
        self.KC = D // 128
        self.T = 512
        self.NTT = self.S // 512
        self.FB = min(2048, DFF)
        self.SB = 2048


def halo_item_a(g, kind, h):
    idx = 2 * h + kind
    if g == 2:
        return idx // 2, (idx % 2) * 2048, 2048
    if g == 1:
        return 6 + idx // 8, (idx % 8) * 512, 512
    return 8, idx * 128, 128


def halo_item_b(i):
    return 8, 1536 + i * 128, 128


N_PAGES = 9


def _const_layout():
    names = [("ident", 128), ("ones", 128), ("blk64", 128), ("rmatT", 128),
             ("e0", 128), ("e1", 128), ("e2", 128), ("e3", 128),
             ("ones_lo", 128), ("ones_hi", 128),
             ("maskA", 256), ("maskAf", 256), ("maskB", 512), ("maskBf", 512),
             ("decay0", 128), ("decay1", 128), ("decay2", 128), ("decay3", 128),
             ("qd0", 128), ("qd1", 128), ("qd2", 128), ("qd3", 128), ("kd", 4)]
    off = {}
    o = 0
    for n, w in names:
        off[n] = (o, w)
        o += w
    return off, o


CL, NCONST = _const_layout()
NBF = CL["decay0"][0]


def make_consts():
    c = np.zeros((128, NCONST), np.float32)

    def put(n, a):
        o, w = CL[n]
        c[:, o:o + w] = a

    i = np.arange(128)
    put("ident", np.eye(128, dtype=np.float32))
    put("ones", np.ones((128, 128), np.float32))
    blk = np.zeros((128, 128), np.float32)
    blk[:64, :64] = 1
    blk[64:, 64:] = 1
    put("blk64", blk)
    R = np.zeros((128, 128), np.float32)
    for a in range(64):
        R[2 * a, 2 * a + 1] = -1.0
        R[2 * a + 1, 2 * a] = 1.0
    put("rmatT", R.T.copy())
    e0 = np.zeros((128, 128), np.float32)
    e1 = np.zeros((128, 128), np.float32)
    e2 = np.zeros((128, 128), np.float32)
    e3 = np.zeros((128, 128), np.float32)
    for m in range(64):
        e0[m, m] = 1
        e1[m, m + 64] = 1
        e2[m + 64, m] = 1
        e3[m + 64, m + 64] = 1
    put("e0", e0), put("e1", e1), put("e2", e2), put("e3", e3)
    lo = np.zeros((128, 128), np.float32)
    lo[:, :64] = 1
    hi = np.zeros((128, 128), np.float32)
    hi[:, 64:] = 1
    put("ones_lo", lo), put("ones_hi", hi)
    j = i[:, None]
    q = i[None, :]
    cur = np.where(j <= q, 0.0, NEG).astype(np.float32)
    prevA = np.where(j >= q, 0.0, NEG).astype(np.float32)
    prevB = np.where(j >= q + 1, 0.0, NEG).astype(np.float32)
    negs = np.full((128, 128), NEG, np.float32)
    put("maskA", np.concatenate([prevA, cur], 1))
    put("maskAf", np.concatenate([negs, cur], 1))
    put("maskB", np.concatenate([prevB, cur, prevB, cur], 1))
    put("maskBf", np.concatenate([negs, cur, negs, cur], 1))
    for h in range(4):
        lg = np.log1p(-(2.0 ** (-5.0 - h)))
        rel = (q - j).astype(np.float64)
        dec = np.where(rel >= 0, np.exp(lg * np.maximum(rel, 0)), 0.0)
        put(f"decay{h}", dec.astype(np.float32))
        put(f"qd{h}", np.broadcast_to(np.exp(lg * (i + 1.0))[None, :], (128, 128)).astype(np.float32))
        c[:, CL["kd"][0] + h] = np.exp(lg * (127.0 - i)).astype(np.float32)
    return c


def chunk_decay(h):
    return float(np.exp(np.log1p(-(2.0 ** (-5.0 - h))) * 128.0))


def make_rot(S):
    inv = (1.0 / (np.float32(10000.0) ** np.linspace(0.0, 1.0, 64, dtype=np.float32))).astype(np.float32)
    pos = np.arange(S, dtype=np.float32)
    ang = (pos[:, None] * np.repeat(inv, 2)[None, :]).astype(np.float32)
    cos = np.cos(ang.astype(np.float64)).astype(np.float32).T
    sin = np.sin(ang.astype(np.float64)).astype(np.float32).T
    ksc = np.float32(128 ** -0.5)
    return np.ascontiguousarray(np.stack([cos, sin, cos * ksc, sin * ksc], 0))


def layout_params(cfg, p):
    L, KC, D = cfg.DEPTH, cfg.KC, cfg.D
    g1 = p["norm1_g"].reshape(L, KC, 128).transpose(2, 0, 1).reshape(128, L * KC)
    g2 = p["norm2_g"].reshape(L, KC, 128).transpose(2, 0, 1).reshape(128, L * KC)
    qka = np.stack([p["qn_a"], p["kn_a"]], 1).transpose(2, 0, 1).reshape(128, L * 2)
    qb2 = np.concatenate([p["qn_b"], p["qn_b"]], 1)
    kb2 = np.concatenate([p["kn_b"], p["kn_b"]], 1)
    qkb = np.stack([qb2, kb2], 1).transpose(2, 0, 1).reshape(128, L * 2)
    sk = p["sinks"].reshape(L, 8, 2)
    sinkt = np.repeat(sk.transpose(2, 0, 1), 64, axis=0).reshape(128, L * 8)
    bg = p["b_gate"].reshape(L, 3, KC, 128).transpose(3, 0, 1, 2).reshape(128, L * 3 * KC)
    tab = np.concatenate([g1, g2, qka, qkb, sinkt, bg], 1).astype(np.float32)
    return np.ascontiguousarray(tab)


def ptab_offsets(cfg):
    L, KC = cfg.DEPTH, cfg.KC
    o = {}
    o["g1"] = 0
    o["g2"] = L * KC
    o["qka"] = 2 * L * KC
    o["qkb"] = o["qka"] + 2 * L
    o["sink"] = o["qkb"] + 2 * L
    o["bg"] = o["sink"] + 8 * L
    o["n"] = o["bg"] + 3 * KC * L
    return o


class Eng:
    def __init__(self, name, obj, sem):
        self.name, self.o, self.sem = name, obj, sem
        self.cnt = 0
        self.waited = {}


class TT:
    __slots__ = ("ap", "w", "r", "sem", "dcnt", "name", "excl")

    def __init__(self, ap, name, sem=None):
        self.ap, self.name, self.sem = ap, name, sem
        self.w = None
        self.r = []
        self.dcnt = 0
        self.excl = False


class Kern:
    def __init__(self, nc, cfg):
        self.nc, self.cfg = nc, cfg
        self.E = {}
        for n, o in (("pe", nc.tensor), ("act", nc.scalar), ("dve", nc.vector),
                     ("pool", nc.gpsimd), ("sp", nc.sync)):
            self.E[n] = Eng(n, o, nc.alloc_semaphore(name="sem_" + n))
        self.rec = {n: [] for n in self.E}
        self.pend = []
        self.csem = nc.alloc_semaphore(name="csem")
        self.ccnt = 0
        self.dsems = []
        self.free_recs = {}
        self.phase_recs = []
        self.nsem = 0
        self.alt = 0

    def dsem(self):
        s = self.nc.alloc_semaphore(name=f"dsem{self.nsem}")
        self.nsem += 1
        return s

    def sb(self, es, name, shape, dt, dma=False):
        self.uid = getattr(self, "uid", 0) + 1
        name = f"{name}_u{self.uid}"
        h = es.enter_context(self.nc.sbuf_tensor(name, list(shape), dt))
        t = TT(h[:], name)
        if dma:
            self.give_sem(t)
        return t

    def ps(self, es, name, shape, dt=F32):
        self.uid = getattr(self, "uid", 0) + 1
        name = f"{name}_u{self.uid}"
        full = 512 if dt == F32 else 1024
        h = es.enter_context(self.nc.psum_tensor(name, [128, full], dt))
        n = 1
        for d in shape[1:]:
            n *= d
        ap = h[:, 0:n]
        if len(shape) == 3:
            ap = ap.rearrange("p (a b) -> p a b", a=shape[1])
        t = TT(ap, name)
        t.excl = True
        return t

    def give_sem(self, t):
        t.sem = "lazy"
        return t

    def _bind_sem(self, t, qn):
        fl = self.free_recs.setdefault(qn, [])
        if fl:
            rec = fl.pop()
        else:
            rec = [self.dsem(), 0, qn]
            self.dsems.append(rec)
        self.phase_recs.append(rec)
        t.sem = rec

    def recycle_sems(self):
        for rec in self.phase_recs:
            self.free_recs.setdefault(rec[2], []).append(rec)
        self.phase_recs = []

    def sub(self, t, ap, name=None):
        s = TT(ap, name or t.name)
        s.sem = t.sem
        return s

    def _wait(self, eng, ev):
        key, val, h = ev
        if eng.waited.get(key, 0) >= val:
            return
        if key == "pe" and eng.name == "pe":
            return
        self.pend.append((h, val))
        eng.waited[key] = val

    def _deps(self, eng, R, W):
        for t in R:
            if t.w is not None:
                self._wait(eng, t.w)
            if t.excl:
                for ev in t.r:
                    if ev[0] != eng.name:
                        self._wait(eng, ev)
        for t in W:
            if t.w is not None:
                self._wait(eng, t.w)
            for ev in t.r:
                self._wait(eng, ev)

    def _commit(self, ev, R, W):
        for t in R:
            if len(t.r) > 24:
                last = {}
                for e in t.r:
                    if e[0] not in last or last[e[0]][1] < e[1]:
                        last[e[0]] = e
                t.r = list(last.values())
            t.r.append(ev)
        for t in W:
            t.w = ev
            t.r = []

    def op(self, en, fn, R=(), W=(), inc=True):
        eng = self.E[en]
        self.pend = []
        self._deps(eng, R, W)
        waits = self.pend
        ev = (en, eng.cnt + 1, eng.sem)
        if inc:
            eng.cnt += 1

        def emit(waits=waits, fn=fn, inc=inc, o=eng.o, sem=eng.sem):
            for h, v in waits:
                o.wait_ge(h, v)
            ins = fn(o)
            if inc:
                ins.then_inc(sem, 1)

        self.rec[en].append(emit)
        self._commit(ev, R, W)

    def flush(self):
        rec = self.rec
        self.rec = {n: [] for n in rec}
        if not any(rec.values()):
            return
        with self.nc.Block() as block:
            @block.tensor
            def _(e):
                for f in rec["pe"]:
                    f()

            @block.scalar
            def _(e):
                for f in rec["act"]:
                    f()

            @block.vector
            def _(e):
                for f in rec["dve"]:
                    f()

            @block.gpsimd
            def _(e):
                for f in rec["pool"]:
                    f()

            @block.sync
            def _(e):
                for f in rec["sp"]:
                    f()

    def allgather(self, in_ap, out_ap):
        self.ccnt += 1
        nc, csem, groups = self.nc, self.csem, self.cfg.groups

        def emit():
            nc.gpsimd.collective_compute("AllGather", ALU.bypass, replica_groups=groups,
                                         ins=[in_ap], outs=[out_ap]).then_inc(csem)

        self.rec["pool"].append(emit)

    def dma(self, qn, out_ap, in_ap, R=(), W=(), semt=None, split=None):
        eng = self.E[qn]
        self.pend = []
        self._deps(eng, R, W)
        if semt.sem == "lazy":
            self._bind_sem(semt, qn)
        rec = semt.sem
        assert rec[2] == qn, (semt.name, rec[2], qn)
        pieces = [(out_ap, in_ap)]
        if split is not None:
            n_tot, step = split
            if n_tot > step:
                pieces = [(out_ap[:, a:min(a + step, n_tot)], in_ap[:, a:min(a + step, n_tot)])
                          for a in range(0, n_tot, step)]
        rec[1] += len(pieces)
        waits = self.pend

        def emit(waits=waits, pieces=pieces, o=eng.o, sem=rec[0]):
            for h, v in waits:
                o.wait_ge(h, v)
            for o_, i_ in pieces:
                o.dma_start(out=o_, in_=i_).then_inc(sem, 16)

        self.rec[qn].append(emit)
        ev = (id(rec), 16 * rec[1], rec[0])
        self._commit(ev, R, W)

    def barrier(self):
        for eng in self.E.values():
            self.pend = []
            for e2 in self.E.values():
                if e2.name == "sp" or e2.cnt == 0:
                    continue
                self._wait(eng, (e2.name, e2.cnt, e2.sem))
            for rec in self.dsems:
                if rec[1]:
                    self._wait(eng, (id(rec), 16 * rec[1], rec[0]))
            if self.ccnt:
                self._wait(eng, ("csem", self.ccnt, self.csem))
            waits = self.pend

            def emit(waits=waits, o=eng.o):
                for h, v in waits:
                    o.wait_ge(h, v)

            if waits:
                self.rec[eng.name].append(emit)

    def alt_eng(self, choices=("dve", "act")):
        self.alt += 1
        return choices[self.alt % len(choices)]

    def copy(self, en, out_t, out_ap, in_t, in_ap):
        if en == "act":
            self.op("act", lambda e: e.activation(out=out_ap, in_=in_ap, func=AF.Copy), R=[in_t], W=[out_t])
        else:
            self.op(en, lambda e: e.tensor_copy(out=out_ap, in_=in_ap), R=[in_t], W=[out_t])

    def rstd_from_ssq(self, out_t, out_ap, ps_t, ps_ap, n):
        self.op("act", lambda e: e.activation(out=out_ap, in_=ps_ap, func=AF.Sqrt, scale=1.0 / n,
                                              bias=self.epsb.ap), R=[ps_t, self.epsb], W=[out_t])
        self.op("dve", lambda e: e.reciprocal(out=out_ap, in_=out_ap), R=[out_t], W=[out_t])

    def run(self, io):
        nc, cfg = self.nc, self.cfg
        self.io = io
        with contextlib.ExitStack() as es:
            self.cf = self.sb(es, "cf", [128, NCONST], F32, dma=True)
            self.cb = self.sb(es, "cb", [128, NBF], BF16)
            po = ptab_offsets(cfg)
            self.po = po
            self.pt = self.sb(es, "pt", [128, po["n"]], F32, dma=True)
            self.epsb = self.sb(es, "epsb", [128, 1], F32)
            self.op("pool", lambda e: e.memset(self.epsb.ap, EPS), W=[self.epsb])
            self.dma("sp", self.cf.ap, io["consts"], W=[self.cf], semt=self.cf)
            self.dma("sp", self.pt.ap, io["ptab"], W=[self.pt], semt=self.pt)
            self.op("dve", lambda e: e.tensor_copy(out=self.cb.ap, in_=self.cf.ap[:, 0:NBF]),
                    R=[self.cf], W=[self.cb])
            self.cmf = self.sb(es, "cmf", [128, 768], F32, dma=True)
            self.cmb = self.sb(es, "cmb", [128, 768], BF16)
            self.cflag = self.sb(es, "cflag", [128, 1], F32, dma=True)
            self.dma("sp", self.cmf.ap, io["cmask"], W=[self.cmf], semt=self.cmf)
            self.dma("sp", self.cflag.ap, io["cflag"], W=[self.cflag], semt=self.cflag)
            self.op("dve", lambda e: e.tensor_copy(out=self.cmb.ap, in_=self.cmf.ap), R=[self.cmf], W=[self.cmb])
            import os
            stop = os.environ.get("KSTOP", "")
            steps = [("w", self.phase_weights), ("xin", self.phase_xin)]
            for l in range(cfg.DEPTH):
                steps += [(f"p1_{l}", lambda l=l: self.phase1(l)), (f"h_{l}", lambda l=l: self.halo_exchange(l)),
                          (f"a_{l}", lambda l=l: self.attn_a(l)),
                          (f"b_{l}", lambda l=l: self.attn_b(l)), (f"c_{l}", lambda l=l: self.attn_c(l)),
                          (f"p34_{l}", lambda l=l: self.phase34(l))]
            steps += [("xout", self.phase_xout)]
            skip = os.environ.get("KSKIP", "").split(",")
            self.barrier()
            self.phase_recs = []
            for name, fn in steps:
                if name not in skip:
                    fn()
                    self.barrier()
                    self.recycle_sems()
                if name == stop:
                    break
            self.flush()

    def cbf(self, n):
        o, w = CL[n]
        return self.cb.ap[:, o:o + w]

    def cff(self, n):
        o, w = CL[n]
        return self.cf.ap[:, o:o + w]

    def phase_weights(self):
        cfg, io = self.cfg, self.io
        PW = 2048
        with contextlib.ExitStack() as es:
            sf = [self.sb(es, f"wsf{i}", [128, PW], F32, dma=True) for i in range(3)]
            sbf = [self.sb(es, f"wsb{i}", [128, PW], BF16, dma=True) for i in range(3)]
            n = 0
            for l in range(cfg.DEPTH):
                for name in ("w_in", "w_branch", "w_gate_up", "w_out", "w_ff1", "w_ff2"):
                    W = io[name]
                    Wb = io["wb_" + name][l]
                    K, E = W.shape[1], W.shape[2]
                    for kc in range(K // 128):
                        for c0 in range(0, E, PW):
                            pw = min(PW, E - c0)
                            a, b = sf[n % 3], sbf[n % 3]
                            self.dma("sp", a.ap[:, 0:pw], W[l, kc * 128:(kc + 1) * 128, c0:c0 + pw], W=[a], semt=a)
                            en = ("dve", "act", "pool")[n % 3]
                            self.copy(en, b, b.ap[:, 0:pw], a, a.ap[:, 0:pw])
                            g0, g1 = c0 // 256, (c0 + pw) // 256
                            self.dma("pool", Wb[g0:g1, :, kc, :].rearrange("g p w -> p g w"),
                                     b.ap[:, 0:pw].rearrange("p (g w) -> p g w", w=256), R=[b], semt=b)
                            n += 1
            self.flush()

    def phase_xin(self):
        cfg, io = self.cfg, self.io
        KC = cfg.KC
        xTv = io["xT"].rearrange("(c p) s -> p c s", p=128)
        with contextlib.ExitStack() as es:
            xin = [self.sb(es, f"xin{i}", [128, cfg.D], F32, dma=True) for i in range(4)]
            xt = self.sb(es, "xt", [128, KC, 512], F32, dma=True)
            pss = [self.ps(es, f"psx{i}", [128, 512]) for i in range(4)]
            idf = self.cff("ident")
            for tt in range(cfg.NTT):
                for i in range(4):
                    r0 = tt * 512 + i * 128
                    self.dma("pool", xin[i].ap, io["x"][r0:r0 + 128, :], W=[xin[i]], semt=xin[i])
                for c in range(KC):
                    bank = pss[c % 4]
                    for i in range(4):
                        self.op("pe", lambda e, i=i, c=c, bank=bank: e.transpose(
                            bank.ap[:, i * 128:(i + 1) * 128], xin[i].ap[:, c * 128:(c + 1) * 128], idf),
                            R=[xin[i], self.cf], W=[bank], inc=(i == 3))
                    self.copy(self.alt_eng(), xt, xt.ap[:, c, :], bank, bank.ap)
                self.dma("sp", xTv[:, :, tt * 512:(tt + 1) * 512], xt.ap, R=[xt], semt=xt, split=(KC, 8))
            self.flush()

    def phase_xout(self):
        cfg, io = self.cfg, self.io
        KC = cfg.KC
        xTv = io["xT"].rearrange("(c p) s -> p c s", p=128)
        with contextlib.ExitStack() as es:
            xo = [self.sb(es, f"xo{i}", [128, cfg.D], F32, dma=True) for i in range(4)]
            xt = self.sb(es, "xt", [128, KC, 512], F32, dma=True)
            pss = [self.ps(es, f"psx{i}", [128, 512]) for i in range(4)]
            idf = self.cff("ident")
            n = 0
            for tt in range(cfg.NTT):
                self.dma("pool", xt.ap, xTv[:, :, tt * 512:(tt + 1) * 512], W=[xt], semt=xt, split=(KC, 8))
                for i in range(4):
                    for c4 in range(0, KC, 4):
                        nn = min(4, KC - c4)
                        bank = pss[n % 4]
                        n += 1
                        for cc in range(nn):
                            c = c4 + cc
                            self.op("pe", lambda e, i=i, c=c, cc=cc, bank=bank: e.transpose(
                                bank.ap[:, cc * 128:(cc + 1) * 128], xt.ap[:, c, i * 128:(i + 1) * 128], idf),
                                R=[xt, self.cf], W=[bank], inc=(cc == nn - 1))
                        self.copy(self.alt_eng(), xo[i], xo[i].ap[:, c4 * 128:(c4 + nn) * 128], bank,
                                  bank.ap[:, 0:nn * 128])
                    r0 = tt * 512 + i * 128
                    self.dma("sp", io["out"][r0:r0 + 128, :], xo[i].ap, R=[xo[i]], semt=xo[i])
            self.flush()

    def gemm_setup(self, es, nbuf=3):
        self.wbufs = [self.sb(es, f"wbuf{i}", [128, 32, 256], BF16, dma=True) for i in range(nbuf)]
        self.wn = 0
        self.pgroups = [[self.ps(es, f"pg{g}_{j}", [128, 512]) for j in range(2)] for g in range(3)]
        self.pgn = 0
        self.pending = None

    def loadw(self, Wb, g, kc0, kc1):
        wt = self.wbufs[self.wn % len(self.wbufs)]
        self.wn += 1
        nk = kc1 - kc0
        self.dma("sp", wt.ap[:, 0:nk, :], Wb[g, :, kc0:kc1, :], W=[wt], semt=wt)
        return wt

    def gemm_group(self, Wb, g, kc0, kc1, rhs, epi):
        wt = self.loadw(Wb, g, kc0, kc1)
        pg = self.pgroups[self.pgn % 3]
        self.pgn += 1
        nk = kc1 - kc0
        for ki in range(nk):
            rt, rap = rhs[ki]
            for j in range(2):
                self.op("pe", lambda e, j=j, ki=ki, rap=rap: e.matmul(
                    pg[j].ap, wt.ap[:, ki, j * 128:(j + 1) * 128], rap,
                    start=(ki == 0), stop=(ki == nk - 1)),
                    R=[wt, rt], W=[pg[j]], inc=(ki == nk - 1 and j == 1))
        self.flush_epi()
        gen = epi(pg)
        next(gen, None)
        self.pending = gen

    def flush_epi(self):
        if self.pending is not None:
            for _ in self.pending:
                pass
            self.pending = None

    def rmsnorm_tile(self, xa_chunks, xa_ap, out_tiles, gcol0, sq, pss, rstd):
        cfg = self.cfg
        KC = cfg.KC
        ones = self.cbf("ones")
        for c in range(KC):
            s = sq[c % 2]
            self.op("act", lambda e, c=c, s=s: e.activation(out=s.ap, in_=xa_ap[:, c, :], func=AF.Square),
                    R=[xa_chunks[c]], W=[s])
            self.op("pe", lambda e, c=c, s=s: e.matmul(pss.ap, ones, s.ap, start=(c == 0), stop=(c == KC - 1)),
                    R=[s, self.cb], W=[pss], inc=True)
        self.rstd_from_ssq(rstd, rstd.ap, pss, pss.ap, float(cfg.D))
        for c in range(KC):
            o = out_tiles[c]
            self.op("dve", lambda e, c=c, o=o: e.scalar_tensor_tensor(
                out=o.ap, in0=xa_ap[:, c, :], scalar=self.pt.ap[:, gcol0 + c:gcol0 + c + 1], in1=rstd.ap,
                op0=ALU.mult, op1=ALU.mult), R=[xa_chunks[c], rstd, self.pt], W=[o])

    def phase1(self, l):
        cfg, io = self.cfg, self.io
        KC, T = cfg.KC, cfg.T
        po = self.po
        xTv = io["xT"].rearrange("(c p) s -> p c s", p=128)
        projv = io["projT"].rearrange("(c p) s -> c p s", p=128)
        kbxv = io["kbx"].rearrange("(c p) s -> c p s", p=128)
        sgcv = io["sgc"].rearrange("(c p) s -> c p s", p=128)
        Wb = io["wb_w_in"][l]
        with contextlib.ExitStack() as es:
            self.gemm_setup(es)
            xacc = self.sb(es, "xacc", [128, KC, T], F32, dma=True)
            xch = [self.sub(xacc, xacc.ap[:, c, :], f"xacc{c}") for c in range(KC)]
            xnm = self.sb(es, "xn", [128, KC, T], BF16)
            xn = [self.sub(xnm, xnm.ap[:, c, :], f"xn{c}") for c in range(KC)]
            sq = [self.sb(es, f"sq{i}", [128, T], BF16) for i in range(2)]
            rstd = self.sb(es, "rstd", [128, T], F32)
            ps_x = self.ps(es, "ps_x", [128, 512])
            ps_y = self.ps(es, "ps_y", [128, 512])
            rot = self.sb(es, "rot", [128, 4, T], F32, dma=True)
            ob = [self.sb(es, f"ob{i}", [128, T], BF16, dma=True) for i in range(6)]
            of = [self.sb(es, f"of{i}", [128, T], F32, dma=True) for i in range(2)]
            tf = [self.sb(es, f"tf{i}", [128, T], F32) for i in range(2)]
            r2 = [self.sb(es, f"r2{i}", [128, T], F32) for i in range(2)]
            st = {"ob": 0, "of": 0, "tf": 0, "r2": 0, "sq": 0}

            def nxt(lst, k):
                st[k] += 1
                return lst[st[k] % len(lst)]

            for tt in range(cfg.NTT):
                ts = slice(tt * T, (tt + 1) * T)
                self.dma("pool", xacc.ap, xTv[:, :, ts], W=xch, semt=xacc, split=(KC, 8))
                self.dma("pool", rot.ap, io["rot"][:, :, ts].rearrange("f p s -> p f s"), W=[rot], semt=rot)
                self.rmsnorm_tile(xch, xacc.ap, xn, po["g1"] + l * KC, sq, ps_x, rstd)
                rhs = [(xn[c], xn[c].ap) for c in range(KC)]

                def epi(pg, g, ts=ts):
                    todo = []
                    for j in range(2):
                        ch = 2 * g + j
                        p = pg[j]
                        if ch < C_VA or C_QB <= ch < C_VB:
                            s = nxt(sq, "sq")
                            self.op("act", lambda e, s=s, p=p: e.activation(out=s.ap, in_=p.ap, func=AF.Square),
                                    R=[p], W=[s])
                            todo.append(("norm", ch, p, s))
                        elif C_QC <= ch < C_VC:
                            o = nxt(ob, "ob")
                            self.copy("act", o, o.ap, p, p.ap)
                            todo.append(("rot", ch, p, o))
                        elif C_GC <= ch < C_GL:
                            o = nxt(of, "of")
                            self.op("act", lambda e, o=o, p=p: e.activation(out=o.ap, in_=p.ap, func=AF.Silu),
                                    R=[p], W=[o])
                            self.dma("pool", sgcv[ch - C_GC][:, ts], o.ap, R=[o], semt=o)
                        else:
                            o = nxt(ob, "ob")
                            self.copy(self.alt_eng(), o, o.ap, p, p.ap)
                            self.dma("pool", projv[ch][:, ts], o.ap, R=[o], semt=o)
                    yield
                    for kind, ch, p, s in todo:
                        if kind == "norm":
                            isb = ch >= C_QB
                            red = self.cbf("blk64") if isb else self.cbf("ones")
                            self.op("pe", lambda e, s=s, red=red: e.matmul(ps_y.ap, red, s.ap, start=True, stop=True),
                                    R=[s, self.cb], W=[ps_y])
                            r = nxt(r2, "r2")
                            self.rstd_from_ssq(r, r.ap, ps_y, ps_y.ap, 64.0 if isb else 128.0)
                            if isb:
                                gc = po["qkb"] + 2 * l + (1 if ch == C_KB else 0)
                            else:
                                gc = po["qka"] + 2 * l + (1 if ch >= C_KA else 0)
                            o = nxt(ob, "ob")
                            self.op("dve", lambda e, o=o, p=p, r=r, gc=gc: e.scalar_tensor_tensor(
                                out=o.ap, in0=p.ap, scalar=self.pt.ap[:, gc:gc + 1], in1=r.ap,
                                op0=ALU.mult, op1=ALU.mult), R=[p, r, self.pt], W=[o])
                            if ch == C_KB:
                                for q4 in range(4):
                                    self.op("pe", lambda e, q4=q4, o=o: e.matmul(
                                        ps_y.ap, self.cbf(f"e{q4}"), o.ap, start=True, stop=True),
                                        R=[o, self.cb], W=[ps_y])
                                    o2 = nxt(ob, "ob")
                                    self.copy(self.alt_eng(), o2, o2.ap, ps_y, ps_y.ap)
                                    self.dma("pool", kbxv[q4][:, ts], o2.ap, R=[o2], semt=o2)
                            else:
                                self.dma("pool", projv[ch][:, ts], o.ap, R=[o], semt=o)
                        else:
                            isk = ch >= C_KC
                            self.op("pe", lambda e, s=s: e.matmul(ps_y.ap, self.cbf("rmatT"), s.ap, start=True, stop=True),
                                    R=[s, self.cb], W=[ps_y])
                            t1, t2 = nxt(tf, "tf"), nxt(tf, "tf")
                            ci, si = (2, 3) if isk else (0, 1)
                            self.op("dve", lambda e, t1=t1, p=p, ci=ci: e.tensor_tensor(
                                out=t1.ap, in0=p.ap, in1=rot.ap[:, ci, :], op=ALU.mult), R=[p, rot], W=[t1])
                            self.op("dve", lambda e, t2=t2, si=si: e.tensor_tensor(
                                out=t2.ap, in0=ps_y.ap, in1=rot.ap[:, si, :], op=ALU.mult), R=[ps_y, rot], W=[t2])
                            o = nxt(ob, "ob")
                            self.op("pool", lambda e, o=o, t1=t1, t2=t2: e.tensor_tensor(
                                out=o.ap, in0=t1.ap, in1=t2.ap, op=ALU.add), R=[t1, t2], W=[o])
                            self.dma("pool", projv[ch][:, ts], o.ap, R=[o], semt=o)

                for g in range(N_CH // 2):
                    self.gemm_group(Wb, g, 0, KC, rhs, lambda pg, g=g: epi(pg, g))
                self.flush_epi()
            self.flush()

    def phase2(self, l):
        self.attn_a(l)
        self.barrier()
        self.attn_b(l)
        self.barrier()
        self.attn_c(l)

    def halo_exchange(self, l):
        cfg, io = self.cfg, self.io
        if cfg.P == 1:
            return
        S = cfg.S
        projv = io["projT"].rearrange("(c p) s -> c p s", p=128)
        kbxv = io["kbx"].rearrange("(c p) s -> c p s", p=128)
        hs = TT(None, "hsem")
        self.give_sem(hs)
        for g in range(3):
            for h in range(6):
                for kind, base in ((0, C_KA), (1, C_VA)):
                    pg, c0, halo = halo_item_a(g, kind, h)
                    self.dma("pool", io["HB"][pg][:, c0:c0 + halo], projv[base + g * 6 + h][:, S - halo:S], semt=hs)
        for i in range(5):
            pg, c0, halo = halo_item_b(i)
            src = kbxv[i] if i < 4 else projv[C_VB]
            self.dma("pool", io["HB"][pg][:, c0:c0 + halo], src[:, S - halo:S], semt=hs)
        self.barrier()
        for pg in range(N_PAGES):
            self.allgather(io["HB"][pg], io["GB"][pg])
        self.barrier()
        self.flush()

    def attn_a(self, l):
        cfg, io = self.cfg, self.io
        S, SB = cfg.S, cfg.SB
        projv = io["projT"].rearrange("(c p) s -> c p s", p=128)
        oTv = io["oT"].rearrange("(c p) s -> c p s", p=128)
        ident, ones = self.cbf("ident"), self.cbf("ones")
        scale = 128 ** -0.5
        HM = cfg.P > 1
        with contextlib.ExitStack() as es:
            qt = [self.sb(es, f"qt{i}", [128, SB], BF16, dma=True) for i in range(2)]
            kt = [self.sb(es, f"kt{i}", [128, 2 * SB], BF16, dma=True) for i in range(2)]
            vt = [self.sb(es, f"vt{i}", [128, 2 * SB], BF16, dma=True) for i in range(2)]
            acc = [self.sb(es, f"acc{i}", [128, 2, SB], F32) for i in range(2)]
            ost = [self.sb(es, f"ost{i}", [128, SB], BF16, dma=True) for i in range(2)]
            vps = [self.ps(es, f"vps{i}", [128, 256], BF16) for i in range(2)]
            sps = [self.ps(es, f"sps{i}", [128, 256]) for i in range(2)]
            ops = [self.ps(es, f"ops{i}", [128, 2, 128]) for i in range(2)]
            vtok = [self.sb(es, f"vtok{i}", [128, 256], BF16) for i in range(3)]
            pT = [self.sb(es, f"pT{i}", [128, 256], BF16) for i in range(3)]
            units = [(sbi, h, g) for sbi in range(S // SB) for h in range(6) for g in range(3)]

            def load(i):
                sbi, h, g = units[i]
                w, r = A_GROUPS[g]
                halo = 128 * r
                t0 = sbi * SB
                lo = max(0, t0 - halo)
                off = t0 - lo
                q_, k_, v_ = qt[i % 2], kt[i % 2], vt[i % 2]
                ch = g * 6 + h
                self.dma("pool", q_.ap, projv[C_QA + ch][:, t0:t0 + SB], W=[q_], semt=q_)
                if HM and sbi == 0:
                    for kind, base, tl in ((0, C_KA, k_), (1, C_VA, v_)):
                        pg, c0, _ = halo_item_a(g, kind, h)
                        self.dma("pool", tl.ap[:, 0:halo], io["GB"][pg][0:128, c0:c0 + halo], W=[tl], semt=tl)
                        self.dma("pool", tl.ap[:, halo:halo + SB], projv[base + ch][:, 0:SB], W=[tl], semt=tl)
                else:
                    self.dma("pool", k_.ap[:, 0:off + SB], projv[C_KA + ch][:, lo:t0 + SB], W=[k_], semt=k_)
                    self.dma("pool", v_.ap[:, 0:off + SB], projv[C_VA + ch][:, lo:t0 + SB], W=[v_], semt=v_)

            nb = 0
            load(0)
            for i, (sbi, h, g) in enumerate(units):
                if i + 1 < len(units):
                    load(i + 1)
                w, r = A_GROUPS[g]
                halo = 128 * r
                t0 = sbi * SB
                off = halo if HM else t0 - max(0, t0 - halo)
                q_, k_, v_ = qt[i % 2], kt[i % 2], vt[i % 2]
                nh = sbi * 6 + h
                ac = acc[nh % 2]
                for u in range(SB // halo):
                    for rho in range(r):
                        bq = u * halo + rho
                        has_prev = HM or (t0 + u * halo) > 0
                        bnd = HM and (t0 + u * halo) == 0
                        ext = 127 * r + 1
                        qs = slice(bq, bq + ext, r)
                        cs = slice(off + bq, off + bq + ext, r)
                        prs = slice(off + bq - halo, off + bq - halo + ext, r)
                        vp, sp_, op_ = vps[nb % 2], sps[nb % 2], ops[nb % 2]
                        vk, p_ = vtok[nb % 3], pT[nb % 3]
                        nb += 1
                        if has_prev:
                            self.op("pe", lambda e, vp=vp, v_=v_, prs=prs: e.transpose(
                                vp.ap[:, 0:128], v_.ap[:, prs], ident), R=[v_, self.cb], W=[vp], inc=False)
                        self.op("pe", lambda e, vp=vp, v_=v_, cs=cs: e.transpose(
                            vp.ap[:, 128:256], v_.ap[:, cs], ident), R=[v_, self.cb], W=[vp])
                        c0 = 0 if has_prev else 128
                        self.copy("dve", vk, vk.ap[:, c0:256], vp, vp.ap[:, c0:256])
                        mk = self.cbf("maskA") if has_prev else self.cbf("maskAf")
                        if bnd:
                            mk = self.cmb.ap[:, 0:256]
                        self.op("pe", lambda e, sp_=sp_, mk=mk: e.matmul(sp_.ap, ident, mk, start=True, stop=False),
                                R=[self.cb, self.cmb], W=[sp_], inc=False)
                        if has_prev:
                            self.op("pe", lambda e, sp_=sp_, k_=k_, q_=q_, prs=prs, qs=qs: e.matmul(
                                sp_.ap[:, 0:128], k_.ap[:, prs], q_.ap[:, qs], start=False, stop=False),
                                R=[k_, q_], W=[sp_], inc=False)
                        self.op("pe", lambda e, sp_=sp_, k_=k_, q_=q_, cs=cs, qs=qs: e.matmul(
                            sp_.ap[:, 128:256], k_.ap[:, cs], q_.ap[:, qs], start=False, stop=True),
                            R=[k_, q_], W=[sp_])
                        self.op("act", lambda e, p_=p_, sp_=sp_: e.activation(
                            out=p_.ap, in_=sp_.ap, func=AF.Exp, scale=scale), R=[sp_], W=[p_])
                        if has_prev:
                            self.op("pe", lambda e, op_=op_, vk=vk, p_=p_: e.matmul(
                                op_.ap[:, 0, :], vk.ap[:, 0:128], p_.ap[:, 0:128], start=True, stop=False),
                                R=[vk, p_], W=[op_], inc=False)
                        self.op("pe", lambda e, op_=op_, vk=vk, p_=p_, hp=has_prev: e.matmul(
                            op_.ap[:, 0, :], vk.ap[:, 128:256], p_.ap[:, 128:256], start=(not hp), stop=True),
                            R=[vk, p_], W=[op_], inc=False)
                        if has_prev:
                            self.op("pe", lambda e, op_=op_, p_=p_: e.matmul(
                                op_.ap[:, 1, :], ones, p_.ap[:, 0:128], start=True, stop=False),
                                R=[p_, self.cb], W=[op_], inc=False)
                        self.op("pe", lambda e, op_=op_, p_=p_, hp=has_prev: e.matmul(
                            op_.ap[:, 1, :], ones, p_.ap[:, 128:256], start=(not hp), stop=True),
                            R=[p_, self.cb], W=[op_])
                        if g == 0:
                            self.op("dve", lambda e, ac=ac, op_=op_, qs=qs: e.tensor_copy(
                                out=ac.ap[:, :, qs], in_=op_.ap), R=[op_], W=[ac])
                        else:
                            self.op("dve", lambda e, ac=ac, op_=op_, qs=qs: e.tensor_tensor(
                                out=ac.ap[:, :, qs], in0=ac.ap[:, :, qs], in1=op_.ap, op=ALU.add),
                                R=[op_, ac], W=[ac])
                if g == 2:
                    o_ = ost[nh % 2]
                    self.op("dve", lambda e, ac=ac: e.reciprocal(out=ac.ap[:, 1, :], in_=ac.ap[:, 1, :]), R=[ac], W=[ac])
                    self.op("pool", lambda e, o_=o_, ac=ac: e.tensor_tensor(
                        out=o_.ap, in0=ac.ap[:, 0, :], in1=ac.ap[:, 1, :], op=ALU.mult), R=[ac], W=[o_])
                    self.dma("pool", oTv[h][:, t0:t0 + SB], o_.ap, R=[o_], semt=o_)
            self.flush()

    def attn_b(self, l):
        cfg, io = self.cfg, self.io
        S, SB = cfg.S, cfg.SB
        po = self.po
        projv = io["projT"].rearrange("(c p) s -> c p s", p=128)
        kbxv = io["kbx"].rearrange("(c p) s -> c p s", p=128)
        oTv = io["oT"].rearrange("(c p) s -> c p s", p=128)
        ident = self.cbf("ident")
        scale = 64 ** -0.5
        HM = cfg.P > 1
        NBK = SB // 128
        with contextlib.ExitStack() as es:
            esink = self.sb(es, "esink", [128, 8], F32)
            self.op("act", lambda e: e.activation(out=esink.ap, in_=self.pt.ap[:, po["sink"] + 8 * l:po["sink"] + 8 * l + 8],
                                                  func=AF.Exp), R=[self.pt], W=[esink])
            qt = [self.sb(es, f"qt{i}", [128, SB], BF16, dma=True) for i in range(2)]
            klo = [self.sb(es, f"klo{i}", [128, 128 + SB], BF16, dma=True) for i in range(2)]
            khi = [self.sb(es, f"khi{i}", [128, 128 + SB], BF16, dma=True) for i in range(2)]
            vt = [self.sb(es, f"vt{i}", [128, 128 + SB], BF16, dma=True) for i in range(2)]
            vlo = [self.sb(es, f"vlo{i}", [128, NBK + 1, 128], BF16) for i in range(2)]
            vhi = [self.sb(es, f"vhi{i}", [128, NBK + 1, 128], BF16) for i in range(2)]
            for t in vlo + vhi:
                self.op("pool", lambda e, t=t: e.memset(t.ap, 0.0), W=[t])
            ost = [self.sb(es, f"ost{i}", [128, SB], BF16, dma=True) for i in range(2)]
            vps = [self.ps(es, f"vps{i}", [128, 128], BF16) for i in range(2)]
            sps = [self.ps(es, f"sps{i}", [128, 512]) for i in range(2)]
            ops = [self.ps(es, f"ops{i}", [128, 2, 128]) for i in range(2)]
            pT = [self.sb(es, f"pT{i}", [128, 512], BF16) for i in range(3)]
            tden = [self.sb(es, f"tden{i}", [128, 128], F32) for i in range(2)]
            nkv = 0
            nq = 0
            nb = 0
            for sbi in range(S // SB):
                t0 = sbi * SB
                lo = max(0, t0 - 128)
                off = 128 if HM else t0 - lo
                for kv in range(2):
                    kl, kh, v_ = klo[nkv % 2], khi[nkv % 2], vt[nkv % 2]
                    vl, vh = vlo[nkv % 2], vhi[nkv % 2]
                    nkv += 1
                    if HM and sbi == 0:
                        for tl, it, src in ((kl, 2 * kv, kbxv[2 * kv]), (kh, 2 * kv + 1, kbxv[2 * kv + 1]),
                                            (v_, 4, projv[C_VB])):
                            pg, c0, _ = halo_item_b(it)
                            self.dma("pool", tl.ap[:, 0:128], io["GB"][pg][0:128, c0:c0 + 128], W=[tl], semt=tl)
                            self.dma("pool", tl.ap[:, 128:128 + SB], src[:, 0:SB], W=[tl], semt=tl)
                    else:
                        self.dma("pool", kl.ap[:, 0:off + SB], kbxv[2 * kv][:, lo:t0 + SB], W=[kl], semt=kl)
                        self.dma("pool", kh.ap[:, 0:off + SB], kbxv[2 * kv + 1][:, lo:t0 + SB], W=[kh], semt=kh)
                        self.dma("pool", v_.ap[:, 0:off + SB], projv[C_VB][:, lo:t0 + SB], W=[v_], semt=v_)
                    nblk = (off + SB) // 128
                    for b in range(nblk):
                        vp = vps[b % 2]
                        self.op("pe", lambda e, vp=vp, v_=v_, b=b: e.transpose(
                            vp.ap, v_.ap[:, b * 128:(b + 1) * 128], ident), R=[v_, self.cb], W=[vp])
                        self.copy("dve", vl, vl.ap[:, b, 0:64], vp, vp.ap[:, kv * 64:(kv + 1) * 64])
                        self.copy("act", vh, vh.ap[:, b, 64:128], vp, vp.ap[:, kv * 64:(kv + 1) * 64])
                    for j in range(4):
                        jj = kv * 4 + j
                        q_ = qt[nq % 2]
                        o_ = ost[nq % 2]
                        nq += 1
                        self.dma("pool", q_.ap, projv[C_QB + jj][:, t0:t0 + SB], W=[q_], semt=q_)
                        for b in range(NBK):
                            has_prev = HM or (t0 + b * 128) > 0
                            bnd = HM and (t0 + b * 128) == 0
                            cb_ = off // 128 + b
                            qs = slice(b * 128, (b + 1) * 128)
                            cs = slice(cb_ * 128, (cb_ + 1) * 128)
                            prs = slice((cb_ - 1) * 128, cb_ * 128)
                            sp_, op_, p_ = sps[nb % 2], ops[nb % 2], pT[nb % 3]
                            td = tden[nb % 2]
                            nb += 1
                            mk = self.cbf("maskB") if has_prev else self.cbf("maskBf")
                            if bnd:
                                mk = self.cmb.ap[:, 256:768]
                            self.op("pe", lambda e, sp_=sp_, mk=mk: e.matmul(sp_.ap, ident, mk, start=True, stop=False),
                                    R=[self.cb, self.cmb], W=[sp_], inc=False)
                            for hh, kk in enumerate((kl, kh)):
                                if has_prev:
                                    self.op("pe", lambda e, sp_=sp_, kk=kk, hh=hh, q_=q_, prs=prs, qs=qs: e.matmul(
                                        sp_.ap[:, hh * 256:hh * 256 + 128], kk.ap[:, prs], q_.ap[:, qs],
                                        start=False, stop=False), R=[kk, q_], W=[sp_], inc=False)
                                self.op("pe", lambda e, sp_=sp_, kk=kk, hh=hh, q_=q_, cs=cs, qs=qs: e.matmul(
                                    sp_.ap[:, hh * 256 + 128:hh * 256 + 256], kk.ap[:, cs], q_.ap[:, qs],
                                    start=False, stop=(hh == 1)), R=[kk, q_], W=[sp_], inc=(hh == 1))
                            self.op("act", lambda e, p_=p_, sp_=sp_: e.activation(
                                out=p_.ap, in_=sp_.ap, func=AF.Exp, scale=scale), R=[sp_], W=[p_])
                            terms = []
                            for hh, (vv, on) in enumerate(((vl, "ones_lo"), (vh, "ones_hi"))):
                                if has_prev:
                                    terms.append((vv, vv.ap[:, cb_ - 1, :], on, hh * 256))
                                terms.append((vv, vv.ap[:, cb_, :], on, hh * 256 + 128))
                            nt = len(terms)
                            for ti, (vv, vap, on, pc) in enumerate(terms):
                                self.op("pe", lambda e, op_=op_, vap=vap, p_=p_, pc=pc, ti=ti, nt=nt: e.matmul(
                                    op_.ap[:, 0, :], vap, p_.ap[:, pc:pc + 128], start=(ti == 0), stop=(ti == nt - 1)),
                                    R=[vv, p_], W=[op_], inc=False)
                            for ti, (vv, vap, on, pc) in enumerate(terms):
                                self.op("pe", lambda e, op_=op_, on=on, p_=p_, pc=pc, ti=ti, nt=nt: e.matmul(
                                    op_.ap[:, 1, :], self.cbf(on), p_.ap[:, pc:pc + 128], start=(ti == 0),
                                    stop=(ti == nt - 1)), R=[p_, self.cb], W=[op_], inc=(ti == nt - 1))
                            self.op("dve", lambda e, td=td, op_=op_, jj=jj: e.tensor_scalar(
                                out=td.ap, in0=op_.ap[:, 1, :], scalar1=esink.ap[:, jj:jj + 1], scalar2=None,
                                op0=ALU.add), R=[op_, esink], W=[td])
                            self.op("dve", lambda e, td=td: e.reciprocal(out=td.ap, in_=td.ap), R=[td], W=[td])
                            self.op("dve", lambda e, o_=o_, op_=op_, td=td, qs=qs: e.tensor_tensor(
                                out=o_.ap[:, qs], in0=op_.ap[:, 0, :], in1=td.ap, op=ALU.mult),
                                R=[op_, td], W=[o_])
                        self.dma("pool", oTv[6 + jj][:, t0:t0 + SB], o_.ap, R=[o_], semt=o_)
            self.flush()

    def attn_c(self, l):
        cfg, io = self.cfg, self.io
        S = cfg.S
        CS = 512
        projv = io["projT"].rearrange("(c p) s -> c p s", p=128)
        sgcv = io["sgc"].rearrange("(c p) s -> c p s", p=128)
        oTv = io["oT"].rearrange("(c p) s -> c p s", p=128)
        ident, ones = self.cbf("ident"), self.cbf("ones")
        kdo = CL["kd"][0]
        import os
        lvl = int(os.environ.get("KC_LEVEL", "9"))
        with contextlib.ExitStack() as es:
            qkv = [[self.sb(es, f"qkv{i}_{h}", [128, 4, CS], BF16, dma=True) for h in range(4)] for i in range(2)]
            qsub = [[(self.give_sem(self.sub(qkv[i][h], qkv[i][h].ap[:, 0, :], "cq")),
                      self.give_sem(self.sub(qkv[i][h], qkv[i][h].ap[:, 1, :], "ck")),
                      self.give_sem(self.sub(qkv[i][h], qkv[i][h].ap[:, 2:4, :], "cv"))) for h in range(4)]
                    for i in range(2)]
            gt = [[self.sb(es, f"gt{i}_{h}", [128, 2, CS], F32, dma=True) for h in range(4)] for i in range(2)]
            state = [self.sb(es, f"state{h}", [128, 256], F32) for h in range(4)]
            sbf = [self.sb(es, f"sbf{h}", [128, 256], BF16) for h in range(4)]
            for h in range(4):
                self.op("pool", lambda e, h=h: e.memset(state[h].ap, 0.0), W=[state[h]])
                self.op("pool", lambda e, h=h: e.memset(sbf[h].ap, 0.0), W=[sbf[h]])
            ost = [self.sb(es, f"ost{i}", [128, 2, CS], BF16, dma=True) for i in range(8)]
            pa = [self.ps(es, f"pa{i}", [128, 384], BF16) for i in range(2)]
            pb = [self.ps(es, f"pb{i}", [128, 384]) for i in range(2)]
            pc = [self.ps(es, f"pc{i}", [128, 512]) for i in range(2)]
            vtok = [self.sb(es, f"vtok{i}", [128, 256], BF16) for i in range(2)]
            kdec = [self.sb(es, f"kdec{i}", [128, 128], BF16) for i in range(2)]
            inb = [self.sb(es, f"inb{i}", [128, 128], BF16) for i in range(2)]
            qdec = [self.sb(es, f"qdec{i}", [128, 128], BF16) for i in range(2)]
            sqc = [self.sb(es, f"sqc{i}", [128, 256], BF16) for i in range(2)]
            rs = [self.sb(es, f"rs{i}", [128, 256], F32) for i in range(2)]
            tm = [self.sb(es, f"tm{i}", [128, 256], F32) for i in range(2)]
            if cfg.P > 1:
                n1 = 0
                for ci in range(S // CS):
                    t0 = ci * CS
                    for h in range(4):
                        _, tk, tv = qsub[ci % 2][h]
                        self.dma("pool", tk.ap, projv[C_KC + h][:, t0:t0 + CS], W=[tk], semt=tk)
                        self.dma("pool", tv.ap, io["projT"][(C_VC + 2 * h) * 128:(C_VC + 2 * h + 2) * 128, t0:t0 + CS]
                                 .rearrange("(e p) s -> p e s", p=128), W=[tv], semt=tv)
                    for c in range(CS // 128):
                        cs = slice(c * 128, (c + 1) * 128)
                        for h in range(4):
                            b = qkv[ci % 2][h]
                            _, tk, tv = qsub[ci % 2][h]
                            A, C = pa[n1 % 2], pc[n1 % 2]
                            vk, kd_ = vtok[n1 % 2], kdec[n1 % 2]
                            n1 += 1
                            gam = chunk_decay(h)
                            for e_ in range(2):
                                self.op("pe", lambda e, A=A, b=b, e_=e_, cs=cs: e.transpose(
                                    A.ap[:, e_ * 128:(e_ + 1) * 128], b.ap[:, 2 + e_, cs], ident),
                                    R=[tv, self.cb], W=[A], inc=False)
                            self.op("pe", lambda e, A=A, b=b, cs=cs: e.transpose(A.ap[:, 256:384], b.ap[:, 1, cs], ident),
                                    R=[tk, self.cb], W=[A])
                            self.copy("act", vk, vk.ap, A, A.ap[:, 0:256])
                            self.op("dve", lambda e, kd_=kd_, A=A, h=h: e.tensor_scalar(
                                out=kd_.ap, in0=A.ap[:, 256:384], scalar1=self.cf.ap[:, kdo + h:kdo + h + 1],
                                scalar2=None, op0=ALU.mult), R=[A, self.cf], W=[kd_])
                            self.op("pe", lambda e, C=C, kd_=kd_, vk=vk: e.matmul(C.ap[:, 256:512], kd_.ap, vk.ap,
                                                                               start=True, stop=True), R=[kd_, vk], W=[C])
                            self.op("dve", lambda e, h=h, C=C, gam=gam: e.scalar_tensor_tensor(
                                out=state[h].ap, in0=state[h].ap, scalar=gam, in1=C.ap[:, 256:512],
                                op0=ALU.mult, op1=ALU.add), R=[state[h], C], W=[state[h]])
                stx = self.sb(es, "stx", [128, 1024], F32, dma=True)
                for h in range(4):
                    self.copy("dve", stx, stx.ap[:, h * 256:(h + 1) * 256], state[h], state[h].ap)
                self.dma("pool", io["SBf"], stx.ap, R=[stx], semt=stx)
                self.barrier()
                self.allgather(io["SBf"], io["SG"])
                self.barrier()
                self.dma("pool", stx.ap, io["SG"][0:128, :], W=[stx], semt=stx)
                for h in range(4):
                    self.op("dve", lambda e, h=h: e.tensor_scalar(
                        out=state[h].ap, in0=stx.ap[:, h * 256:(h + 1) * 256], scalar1=self.cflag.ap[:, 0:1],
                        scalar2=None, op0=ALU.mult), R=[stx, self.cflag], W=[state[h]])
                    self.copy("act", sbf[h], sbf[h].ap, state[h], state[h].ap)
            n = 0
            for ci in range(S // CS):
                t0 = ci * CS
                bufs = qkv[ci % 2]
                gts = gt[ci % 2]
                for h in range(4):
                    b = bufs[h]
                    tq, tk, tv = qsub[ci % 2][h]
                    self.dma("pool", tq.ap, projv[C_QC + h][:, t0:t0 + CS], W=[tq], semt=tq)
                    self.dma("pool", tk.ap, projv[C_KC + h][:, t0:t0 + CS], W=[tk], semt=tk)
                    self.dma("pool", tv.ap, io["projT"][(C_VC + 2 * h) * 128:(C_VC + 2 * h + 2) * 128, t0:t0 + CS]
                             .rearrange("(e p) s -> p e s", p=128), W=[tv], semt=tv)
                    self.dma("pool", gts[h].ap, io["sgc"][2 * h * 128:(2 * h + 2) * 128, t0:t0 + CS]
                             .rearrange("(e p) s -> p e s", p=128), W=[gts[h]], semt=gts[h])
                outs = [ost[(ci % 2) * 4 + h] for h in range(4)]
                for c in range(CS // 128):
                    cs = slice(c * 128, (c + 1) * 128)
                    for h in range(4):
                        b = bufs[h]
                        tq, tk, tv = qsub[ci % 2][h]
                        A, B, C = pa[n % 2], pb[n % 2], pc[n % 2]
                        vk, kd_, ib, qd_, sq_, r_, t_ = (vtok[n % 2], kdec[n % 2], inb[n % 2], qdec[n % 2],
                                                        sqc[n % 2], rs[n % 2], tm[n % 2])
                        n += 1
                        gam = chunk_decay(h)
                        if lvl < 1:
                            continue
                        self.op("pool", lambda e, qd_=qd_, b=b, cs=cs, h=h: e.tensor_tensor(
                            out=qd_.ap, in0=b.ap[:, 0, cs], in1=self.cff(f"qd{h}"), op=ALU.mult),
                            R=[tq, self.cf], W=[qd_])
                        if lvl < 2:
                            continue
                        for e_ in range(2):
                            self.op("pe", lambda e, A=A, b=b, e_=e_, cs=cs: e.transpose(
                                A.ap[:, e_ * 128:(e_ + 1) * 128], b.ap[:, 2 + e_, cs], ident),
                                R=[tv, self.cb], W=[A], inc=False)
                        self.op("pe", lambda e, A=A, b=b, cs=cs: e.transpose(A.ap[:, 256:384], b.ap[:, 1, cs], ident),
                                R=[tk, self.cb], W=[A])
                        self.copy("act", vk, vk.ap, A, A.ap[:, 0:256])
                        self.op("dve", lambda e, kd_=kd_, A=A, h=h: e.tensor_scalar(
                            out=kd_.ap, in0=A.ap[:, 256:384], scalar1=self.cf.ap[:, kdo + h:kdo + h + 1], scalar2=None,
                            op0=ALU.mult), R=[A, self.cf], W=[kd_])
                        if lvl < 3:
                            continue
                        self.op("pe", lambda e, B=B, b=b, cs=cs: e.matmul(B.ap[:, 0:128], b.ap[:, 1, cs], b.ap[:, 0, cs],
                                                                       start=True, stop=True), R=[tq, tk], W=[B])
                        self.op("dve", lambda e, ib=ib, B=B, h=h: e.tensor_tensor(
                            out=ib.ap, in0=B.ap[:, 0:128], in1=self.cff(f"decay{h}"), op=ALU.mult),
                            R=[B, self.cf], W=[ib])
                        if lvl < 4:
                            continue
                        for e_ in range(2):
                            self.op("pe", lambda e, C=C, vk=vk, ib=ib, e_=e_: e.matmul(
                                C.ap[:, e_ * 128:(e_ + 1) * 128], vk.ap[:, e_ * 128:(e_ + 1) * 128], ib.ap,
                                start=True, stop=False), R=[vk, ib], W=[C], inc=False)
                            self.op("pe", lambda e, C=C, qd_=qd_, e_=e_, h=h: e.matmul(
                                C.ap[:, e_ * 128:(e_ + 1) * 128], sbf[h].ap[:, e_ * 128:(e_ + 1) * 128], qd_.ap,
                                start=False, stop=True), R=[sbf[h], qd_], W=[C], inc=False)
                        if lvl < 5:
                            continue
                        self.op("pe", lambda e, C=C, kd_=kd_, vk=vk: e.matmul(C.ap[:, 256:512], kd_.ap, vk.ap,
                                                                           start=True, stop=True), R=[kd_, vk], W=[C])
                        self.op("dve", lambda e, h=h, C=C, gam=gam: e.scalar_tensor_tensor(
                            out=state[h].ap, in0=state[h].ap, scalar=gam, in1=C.ap[:, 256:512],
                            op0=ALU.mult, op1=ALU.add), R=[state[h], C], W=[state[h]])
                        self.copy("act", sbf[h], sbf[h].ap, state[h], state[h].ap)
                        if lvl < 6:
                            continue
                        self.op("act", lambda e, sq_=sq_, C=C: e.activation(out=sq_.ap, in_=C.ap[:, 0:256], func=AF.Square),
                                R=[C], W=[sq_])
                        for half in range(2):
                            for e_ in range(2):
                                self.op("pe", lambda e, B=B, sq_=sq_, half=half, e_=e_: e.matmul(
                                    B.ap[:, 128 + half * 128:256 + half * 128], ones, sq_.ap[:, e_ * 128:(e_ + 1) * 128],
                                    start=(e_ == 0), stop=(e_ == 1)), R=[sq_, self.cb], W=[B],
                                    inc=(half == 1 and e_ == 1))
                        self.rstd_from_ssq(r_, r_.ap, B, B.ap[:, 128:384], 256.0)
                        self.op("dve", lambda e, t_=t_, C=C, r_=r_: e.tensor_tensor(
                            out=t_.ap, in0=C.ap[:, 0:256], in1=r_.ap, op=ALU.mult), R=[C, r_], W=[t_])
                        o_ = outs[h]
                        self.op("pool", lambda e, o_=o_, t_=t_, cs=cs, h=h, gts=gts: e.tensor_tensor(
                            out=o_.ap[:, :, cs], in0=t_.ap.rearrange("p (e s) -> p e s", e=2), in1=gts[h].ap[:, :, cs],
                            op=ALU.mult), R=[t_, gts[h]], W=[o_])
                for h in range(4):
                    self.dma("pool", io["oT"][(14 + 2 * h) * 128:(16 + 2 * h) * 128, t0:t0 + CS]
                             .rearrange("(e p) s -> p e s", p=128), outs[h].ap, R=[outs[h]], semt=outs[h])
            self.flush()

    def phase34(self, l):
        cfg, io = self.cfg, self.io
        KC, T, D, DFF, FB = cfg.KC, cfg.T, cfg.D, cfg.DFF, cfg.FB
        po = self.po
        xTv = io["xT"].rearrange("(c p) s -> p c s", p=128)
        oTv = io["oT"].rearrange("(c p) s -> p c s", p=128)
        glv = io["projT"][C_GL * 128:(C_GL + 2) * 128, :].rearrange("(c p) s -> p c s", p=128)
        NG = D // 256
        FC = FB // 128
        NSLOT = max(22 + 2 + KC, KC + FC)
        br_kc = ((0, 6), (6, 14), (14, 22))
        with contextlib.ExitStack() as es:
            self.gemm_setup(es, 2)
            xacc = self.sb(es, "xacc", [128, KC, T], F32, dma=True)
            xch = [self.sub(xacc, xacc.ap[:, c, :], f"xacc{c}") for c in range(KC)]
            slotm = self.sb(es, "slots", [128, NSLOT, T], BF16, dma=True)
            slots = [self.sub(slotm, slotm.ap[:, i, :], f"slot{i}") for i in range(NSLOT)]
            o_sl, gl_sl, mix_sl = slots[0:22], slots[22:24], slots[24:24 + KC]
            glsem = self.give_sem(TT(None, "glsem"))
            xn_sl, hid_sl = slots[0:KC], slots[KC:KC + FC]
            sq = [self.sb(es, f"sq{i}", [128, T], BF16) for i in range(2)]
            rstd = self.sb(es, "rstd", [128, T], F32)
            ps_x = self.ps(es, "ps_x", [128, 512])
            gsb = [self.sb(es, f"gsb{i}", [128, T], F32) for i in range(2)]
            tmp = [self.sb(es, f"tmp{i}", [128, T], F32) for i in range(3)]
            mixf = [self.sb(es, f"mixf{i}", [128, T], F32) for i in range(2)]
            st = {"g": 0, "t": 0}

            for tt in range(cfg.NTT):
                ts = slice(tt * T, (tt + 1) * T)
                self.dma("pool", slotm.ap[:, 0:22, :], oTv[:, :, ts], W=o_sl, semt=slotm, split=(22, 8))
                self.dma("pool", slotm.ap[:, 22:24, :], glv[:, :, ts], W=gl_sl, semt=glsem)
                self.dma("pool", xacc.ap, xTv[:, :, ts], W=xch, semt=xacc, split=(KC, 8))
                rhs_gl = [(gl_sl[c], gl_sl[c].ap) for c in range(2)]
                rhs_o = [(o_sl[c], o_sl[c].ap) for c in range(22)]
                for dg in range(NG):
                    for i in range(3):
                        gts = []

                        def epi_gate(pg, i=i, dg=dg, gts=gts):
                            for j in range(2):
                                st["g"] += 1
                                g_ = gsb[st["g"] % 2]
                                bc = po["bg"] + (l * 3 + i) * KC + 2 * dg + j
                                self.op("act", lambda e, g_=g_, p=pg[j], bc=bc: e.activation(
                                    out=g_.ap, in_=p.ap, func=AF.Sigmoid, bias=self.pt.ap[:, bc:bc + 1]),
                                    R=[pg[j], self.pt], W=[g_])
                                gts.append(g_)
                            yield

                        self.gemm_group(io["wb_w_gate_up"][l], i * NG + dg, 0, 2, rhs_gl, epi_gate)

                        def epi_br(pg, i=i, dg=dg, gts=gts):
                            for j in range(2):
                                g_ = gts[j]
                                mf = mixf[j]
                                p = pg[j]
                                if i == 0:
                                    self.op("dve", lambda e, mf=mf, p=p, g_=g_: e.tensor_tensor(
                                        out=mf.ap, in0=p.ap, in1=g_.ap, op=ALU.mult), R=[p, g_], W=[mf])
                                else:
                                    st["t"] += 1
                                    t_ = tmp[st["t"] % 3]
                                    self.op("dve", lambda e, t_=t_, p=p, g_=g_: e.tensor_tensor(
                                        out=t_.ap, in0=p.ap, in1=g_.ap, op=ALU.mult), R=[p, g_], W=[t_])
                                    if i == 1:
                                        self.op("pool", lambda e, mf=mf, t_=t_: e.tensor_tensor(
                                            out=mf.ap, in0=mf.ap, in1=t_.ap, op=ALU.add), R=[mf, t_], W=[mf])
                                    else:
                                        ms = mix_sl[2 * dg + j]
                                        self.op("pool", lambda e, ms=ms, mf=mf, t_=t_: e.tensor_tensor(
                                            out=ms.ap, in0=mf.ap, in1=t_.ap, op=ALU.add), R=[mf, t_], W=[ms])
                            yield

                        k0, k1 = br_kc[i]
                        self.gemm_group(io["wb_w_branch"][l], dg, k0, k1, rhs_o[k0:k1], epi_br)
                self.flush_epi()
                rhs_m = [(mix_sl[c], mix_sl[c].ap) for c in range(KC)]

                def epi_acc(pg, dg):
                    for j in range(2):
                        xc = xch[2 * dg + j]
                        self.op("dve", lambda e, xc=xc, p=pg[j]: e.tensor_tensor(
                            out=xc.ap, in0=p.ap, in1=xc.ap, op=ALU.add), R=[pg[j], xc], W=[xc])
                    yield

                for dg in range(NG):
                    self.gemm_group(io["wb_w_out"][l], dg, 0, KC, rhs_m, lambda pg, dg=dg: epi_acc(pg, dg))
                self.flush_epi()
                self.rmsnorm_tile(xch, xacc.ap, xn_sl, po["g2"] + l * KC, sq, ps_x, rstd)
                rhs_x = [(xn_sl[c], xn_sl[c].ap) for c in range(KC)]
                rhs_h = [(hid_sl[c], hid_sl[c].ap) for c in range(FC)]
                for fb in range(DFF // FB):

                    def epi_h(pg, fg):
                        for j in range(2):
                            st["t"] += 1
                            t_ = tmp[st["t"] % 3]
                            hs = hid_sl[2 * fg + j]
                            self.op("act", lambda e, t_=t_, p=pg[j]: e.activation(out=t_.ap, in_=p.ap, func=AF.Relu),
                                    R=[pg[j]], W=[t_])
                            en = self.alt_eng(("dve", "pool"))
                            self.op(en, lambda e, hs=hs, t_=t_: e.tensor_tensor(
                                out=hs.ap, in0=t_.ap, in1=t_.ap, op=ALU.mult), R=[t_], W=[hs])
                        yield

                    for fg in range(FB // 256):
                        self.gemm_group(io["wb_w_ff1"][l], fb * (FB // 256) + fg, 0, KC, rhs_x,
                                        lambda pg, fg=fg: epi_h(pg, fg))
                    self.flush_epi()
                    for dg in range(NG):
                        self.gemm_group(io["wb_w_ff2"][l], dg, fb * FC, (fb + 1) * FC, rhs_h,
                                        lambda pg, dg=dg: epi_acc(pg, dg))
                    self.flush_epi()
                self.dma("pool", xTv[:, :, ts], xacc.ap, R=xch, semt=xacc, split=(KC, 8))
            self.flush()


def build_program(cfg):
    nc = bass.Bass("TRN2", target_bir_lowering=False)
    D, DFF, S, L = cfg.D, cfg.DFF, cfg.S, cfg.DEPTH
    io = {}

    def ext(name, shape, kind="ExternalInput", dt=F32):
        io[name] = nc.dram_tensor(name, list(shape), dt, kind=kind).ap()

    ext("x", [S, D])
    ext("w_in", [L, D, IN_W])
    ext("w_branch", [L, 2816, D])
    ext("w_gate_up", [L, 256, 3 * D])
    ext("w_out", [L, D, D])
    ext("w_ff1", [L, D, DFF])
    ext("w_ff2", [L, DFF, D])
    ext("consts", [128, NCONST])
    ext("ptab", [128, ptab_offsets(cfg)["n"]])
    ext("rot", [4, 128, S])
    ext("cmask", [128, 768])
    ext("cflag", [128, 1])
    ext("out", [S, D], kind="ExternalOutput")

    def scratch(name, shape, dt):
        io[name] = nc.dram_tensor(name, list(shape), dt).ap()

    scratch("xT", [D, S], F32)
    scratch("projT", [IN_W, S], BF16)
    scratch("kbx", [512, S], BF16)
    scratch("sgc", [1024, S], F32)
    scratch("oT", [2816, S], BF16)
    io["HB"] = [nc.dram_tensor(f"HB{i}", [128, 4096], BF16).ap() for i in range(N_PAGES)]
    io["GB"] = [nc.dram_tensor(f"GB{i}", [cfg.P * 128, 4096], BF16).ap() for i in range(N_PAGES)]
    scratch("SBf", [128, 1024], F32)
    scratch("SG", [cfg.P * 128, 1024], F32)
    for name, K, E in (("w_in", D, IN_W), ("w_branch", 2816, D), ("w_gate_up", 256, 3 * D),
                       ("w_out", D, D), ("w_ff1", D, DFF), ("w_ff2", DFF, D)):
        io["wb_" + name] = [nc.dram_tensor(f"wb_{name}_{l}", [E // 256, 128, K // 128, 256], BF16).ap()
                            for l in range(L)]
    k = Kern(nc, cfg)
    k.run(io)
    return nc


def make_in_maps(cfg, inputs):
    consts = make_consts()
    rot = make_rot(cfg.SEQ)
    ptab = layout_params(cfg, {k: np.asarray(v, np.float32) for k, v in inputs.items()
                               if k in ("norm1_g", "norm2_g", "qn_a", "kn_a", "qn_b", "kn_b", "sinks", "b_gate")})

    def cc(n):
        o, w = CL[n]
        return consts[:, o:o + w]

    S = cfg.S
    maps = []
    for c in range(cfg.NCORE):
        q, pos = c // cfg.P, c % cfg.P
        first = pos == 0
        cmask = np.concatenate([cc("maskAf") if first else cc("maskA"),
                                cc("maskBf") if first else cc("maskB")], 1).astype(np.float32)
        m = {"x": np.ascontiguousarray(np.asarray(inputs["x"][q, pos * S:(pos + 1) * S], np.float32)),
             "consts": consts, "ptab": ptab,
             "rot": np.ascontiguousarray(rot[:, :, pos * S:(pos + 1) * S]),
             "cmask": np.ascontiguousarray(cmask),
             "cflag": np.full((128, 1), 0.0 if first else 1.0, np.float32)}
        for n in ("w_in", "w_branch", "w_gate_up", "w_out", "w_ff1", "w_ff2"):
            m[n] = np.asarray(inputs[n], np.float32)
        maps.append(m)
    return maps


def run_cfg(cfg, inputs, trace=False):
    nc = build_program(cfg)
    maps = make_in_maps(cfg, inputs)
    res = run_bass_kernel_spmd(nc, maps, core_ids=list(range(cfg.NCORE)), trace=trace)
    out = np.zeros((cfg.NSEQ, cfg.SEQ, cfg.D), np.float32)
    for c in range(cfg.NCORE):
        q, pos = c // cfg.P, c % cfg.P
        out[q, pos * cfg.S:(pos + 1) * cfg.S] = np.asarray(res.results[c]["out"])
    return out, res


def kernel(**inputs):
    cfg = Cfg()
    out, _ = run_cfg(cfg, inputs)
    return out
```

```python
import contextlib
import numpy as np
import concourse.bass as bass
import concourse.mybir as mybir
from concourse.bass_utils import run_bass_kernel_spmd

F32 = mybir.dt.float32
BF16 = mybir.dt.bfloat16
AF = mybir.ActivationFunctionType
ALU = mybir.AluOpType

EPS = 1e-6
NEG = -30000.0
IN_W = 11520
A_GROUPS = ((128, 1), (512, 4), (2048, 16))
C_QA, C_KA, C_VA = 0, 18, 36
C_QB, C_KB, C_VB = 54, 62, 63
C_QC, C_KC, C_VC, C_GC, C_GL = 64, 68, 72, 80, 88
N_CH = 90


class Cfg:
    def __init__(self, D=4096, DFF=16384, SEQ=8192, DEPTH=4, NSEQ=2, P=4):
        self.D, self.DFF, self.DEPTH = D, DFF, DEPTH
        self.SEQ, self.NSEQ, self.P = SEQ, NSEQ, P
        self.S = SEQ // P
        self.NCORE = NSEQ * P
        self.groups = [[q * P + i for i in range(P)] for q in range(NSEQ)]
        self.KC = D // 128
        self.T = 512
        self.NTT = self.S // 512
        self.FB = min(2048, DFF)
        self.SB = 2048


def halo_item_a(g, kind, h):
    idx = 2 * h + kind
    if g == 2:
        return idx // 2, (idx % 2) * 2048, 2048
    if g == 1:
        return 6 + idx // 8, (idx % 8) * 512, 512
    return 8, idx * 128, 128


def halo_item_b(i):
    return 8, 1536 + i * 128, 128


N_PAGES = 9


def _const_layout():
    names = [("ident", 128), ("ones", 128), ("blk64", 128), ("rmatT", 128),
             ("e0", 128), ("e1", 128), ("e2", 128), ("e3", 128),
             ("ones_lo", 128), ("ones_hi", 128),
             ("maskA", 256), ("maskAf", 256), ("maskB", 512), ("maskBf", 512),
             ("decay0", 128), ("decay1", 128), ("decay2", 128), ("decay3", 128),
             ("qd0", 128), ("qd1", 128), ("qd2", 128), ("qd3", 128), ("kd", 4)]
    off = {}
    o = 0
    for n, w in names:
        off[n] = (o, w)
        o += w
    return off, o


CL, NCONST = _const_layout()
NBF = CL["decay0"][0]


def make_consts():
    c = np.zeros((128, NCONST), np.float32)

    def put(n, a):
        o, w = CL[n]
        c[:, o:o + w] = a

    i = np.arange(128)
    put("ident", np.eye(128, dtype=np.float32))
    put("ones", np.ones((128, 128), np.float32))
    blk = np.zeros((128, 128), np.float32)
    blk[:64, :64] = 1
    blk[64:, 64:] = 1
    put("blk64", blk)
    R = np.zeros((128, 128), np.float32)
    for a in range(64):
        R[2 * a, 2 * a + 1] = -1.0
        R[2 * a + 1, 2 * a] = 1.0
    put("rmatT", R.T.copy())
    e0 = np.zeros((128, 128), np.float32)
    e1 = np.zeros((128, 128), np.float32)
    e2 = np.zeros((128, 128), np.float32)
    e3 = np.zeros((128, 128), np.float32)
    for m in range(64):
        e0[m, m] = 1
        e1[m, m + 64] = 1
        e2[m + 64, m] = 1
        e3[m + 64, m + 64] = 1
    put("e0", e0), put("e1", e1), put("e2", e2), put("e3", e3)
    lo = np.zeros((128, 128), np.float32)
    lo[:, :64] = 1
    hi = np.zeros((128, 128), np.float32)
    hi[:, 64:] = 1
    put("ones_lo", lo), put("ones_hi", hi)
    j = i[:, None]
    q = i[None, :]
    cur = np.where(j <= q, 0.0, NEG).astype(np.float32)
    prevA = np.where(j >= q, 0.0, NEG).astype(np.float32)
    prevB = np.where(j >= q + 1, 0.0, NEG).astype(np.float32)
    negs = np.full((128, 128), NEG, np.float32)
    put("maskA", np.concatenate([prevA, cur], 1))
    put("maskAf", np.concatenate([negs, cur], 1))
    put("maskB", np.concatenate([prevB, cur, prevB, cur], 1))
    put("maskBf", np.concatenate([negs, cur, negs, cur], 1))
    for h in range(4):
        lg = np.log1p(-(2.0 ** (-5.0 - h)))
        rel = (q - j).astype(np.float64)
        dec = np.where(rel >= 0, np.exp(lg * np.maximum(rel, 0)), 0.0)
        put(f"decay{h}", dec.astype(np.float32))
        put(f"qd{h}", np.broadcast_to(np.exp(lg * (i + 1.0))[None, :], (128, 128)).astype(np.float32))
        c[:, CL["kd"][0] + h] = np.exp(lg * (127.0 - i)).astype(np.float32)
    return c


def chunk_decay(h):
    return float(np.exp(np.log1p(-(2.0 ** (-5.0 - h))) * 128.0))


def make_rot(S):
    inv = (1.0 / (np.float32(10000.0) ** np.linspace(0.0, 1.0, 64, dtype=np.float32))).astype(np.float32)
    pos = np.arange(S, dtype=np.float32)
    ang = (pos[:, None] * np.repeat(inv, 2)[None, :]).astype(np.float32)
    cos = np.cos(ang.astype(np.float64)).astype(np.float32).T
    sin = np.sin(ang.astype(np.float64)).astype(np.float32).T
    ksc = np.float32(128 ** -0.5)
    return np.ascontiguousarray(np.stack([cos, sin, cos * ksc, sin * ksc], 0))


def layout_params(cfg, p):
    L, KC, D = cfg.DEPTH, cfg.KC, cfg.D
    g1 = p["norm1_g"].reshape(L, KC, 128).transpose(2, 0, 1).reshape(128, L * KC)
    g2 = p["norm2_g"].reshape(L, KC, 128).transpose(2, 0, 1).reshape(128, L * KC)
    qka = np.stack([p["qn_a"], p["kn_a"]], 1).transpose(2, 0, 1).reshape(128, L * 2)
    qb2 = np.concatenate([p["qn_b"], p["qn_b"]], 1)
    kb2 = np.concatenate([p["kn_b"], p["kn_b"]], 1)
    qkb = np.stack([qb2, kb2], 1).transpose(2, 0, 1).reshape(128, L * 2)
    sk = p["sinks"].reshape(L, 8, 2)
    sinkt = np.repeat(sk.transpose(2, 0, 1), 64, axis=0).reshape(128, L * 8)
    bg = p["b_gate"].reshape(L, 3, KC, 128).transpose(3, 0, 1, 2).reshape(128, L * 3 * KC)
    tab = np.concatenate([g1, g2, qka, qkb, sinkt, bg], 1).astype(np.float32)
    return np.ascontiguousarray(tab)


def ptab_offsets(cfg):
    L, KC = cfg.DEPTH, cfg.KC
    o = {}
    o["g1"] = 0
    o["g2"] = L * KC
    o["qka"] = 2 * L * KC
    o["qkb"] = o["qka"] + 2 * L
    o["sink"] = o["qkb"] + 2 * L
    o["bg"] = o["sink"] + 8 * L
    o["n"] = o["bg"] + 3 * KC * L
    return o


class Eng:
    def __init__(self, name, obj, sem):
        self.name, self.o, self.sem = name, obj, sem
        self.cnt = 0
        self.waited = {}


class TT:
    __slots__ = ("ap", "w", "r", "sem", "dcnt", "name", "excl")

    def __init__(self, ap, name, sem=None):
        self.ap, self.name, self.sem = ap, name, sem
        self.w = None
        self.r = []
        self.dcnt = 0
        self.excl = False


class Kern:
    def __init__(self, nc, cfg):
        self.nc, self.cfg = nc, cfg
        self.E = {}
        for n, o in (("pe", nc.tensor), ("act", nc.scalar), ("dve", nc.vector),
                     ("pool", nc.gpsimd), ("sp", nc.sync)):
            self.E[n] = Eng(n, o, nc.alloc_semaphore(name="sem_" + n))
        self.rec = {n: [] for n in self.E}
        self.pend = []
        self.csem = nc.alloc_semaphore(name="csem")
        self.ccnt = 0
        self.dsems = []
        self.free_recs = {}
        self.phase_recs = []
        self.nsem = 0
        self.alt = 0

    def dsem(self):
        s = self.nc.alloc_semaphore(name=f"dsem{self.nsem}")
        self.nsem += 1
        return s

    def sb(self, es, name, shape, dt, dma=False):
        self.uid = getattr(self, "uid", 0) + 1
        name = f"{name}_u{self.uid}"
        h = es.enter_context(self.nc.sbuf_tensor(name, list(shape), dt))
        t = TT(h[:], name)
        if dma:
            self.give_sem(t)
        return t

    def ps(self, es, name, shape, dt=F32):
        self.uid = getattr(self, "uid", 0) + 1
        name = f"{name}_u{self.uid}"
        full = 512 if dt == F32 else 1024
        h = es.enter_context(self.nc.psum_tensor(name, [128, full], dt))
        n = 1
        for d in shape[1:]:
            n *= d
        ap = h[:, 0:n]
        if len(shape) == 3:
            ap = ap.rearrange("p (a b) -> p a b", a=shape[1])
        t = TT(ap, name)
        t.excl = True
        return t

    def give_sem(self, t):
        t.sem = "lazy"
        return t

    def _bind_sem(self, t, qn):
        fl = self.free_recs.setdefault(qn, [])
        if fl:
            rec = fl.pop()
        else:
            rec = [self.dsem(), 0, qn]
            self.dsems.append(rec)
        self.phase_recs.append(rec)
        t.sem = rec

    def recycle_sems(self):
        for rec in self.phase_recs:
            self.free_recs.setdefault(rec[2], []).append(rec)
        self.phase_recs = []

    def sub(self, t, ap, name=None):
        s = TT(ap, name or t.name)
        s.sem = t.sem
        return s

    def _wait(self, eng, ev):
        key, val, h = ev
        if eng.waited.get(key, 0) >= val:
            return
        if key == "pe" and eng.name == "pe":
            return
        self.pend.append((h, val))
        eng.waited[key] = val

    def _deps(self, eng, R, W):
        for t in R:
            if t.w is not None:
                self._wait(eng, t.w)
            if t.excl:
                for ev in t.r:
                    if ev[0] != eng.name:
                        self._wait(eng, ev)
        for t in W:
            if t.w is not None:
                self._wait(eng, t.w)
            for ev in t.r:
                self._wait(eng, ev)

    def _commit(self, ev, R, W):
        for t in R:
            if len(t.r) > 24:
                last = {}
                for e in t.r:
                    if e[0] not in last or last[e[0]][1] < e[1]:
                        last[e[0]] = e
                t.r = list(last.values())
            t.r.append(ev)
        for t in W:
            t.w = ev
            t.r = []

    def op(self, en, fn, R=(), W=(), inc=True):
        eng = self.E[en]
        self.pend = []
        self._deps(eng, R, W)
        waits = self.pend
        ev = (en, eng.cnt + 1, eng.sem)
        if inc:
            eng.cnt += 1

        def emit(waits=waits, fn=fn, inc=inc, o=eng.o, sem=eng.sem):
            for h, v in waits:
                o.wait_ge(h, v)
            ins = fn(o)
            if inc:
                ins.then_inc(sem, 1)

        self.rec[en].append(emit)
        self._commit(ev, R, W)

    def flush(self):
        rec = self.rec
        self.rec = {n: [] for n in rec}
        if not any(rec.values()):
            return
        with self.nc.Block() as block:
            @block.tensor
            def _(e):
                for f in rec["pe"]:
                    f()

            @block.scalar
            def _(e):
                for f in rec["act"]:
                    f()

            @block.vector
            def _(e):
                for f in rec["dve"]:
                    f()

            @block.gpsimd
            def _(e):
                for f in rec["pool"]:
                    f()

            @block.sync
            def _(e):
                for f in rec["sp"]:
                    f()

    def allgather(self, in_ap, out_ap):
        self.ccnt += 1
        nc, csem, groups = self.nc, self.csem, self.cfg.groups

        def emit():
            nc.gpsimd.collective_compute("AllGather", ALU.bypass, replica_groups=groups,
                                         ins=[in_ap], outs=[out_ap]).then_inc(csem)

        self.rec["pool"].append(emit)

    def dma(self, qn, out_ap, in_ap, R=(), W=(), semt=None, split=None):
        eng = self.E[qn]
        self.pend = []
        self._deps(eng, R, W)
        if semt.sem == "lazy":
            self._bind_sem(semt, qn)
        rec = semt.sem
        assert rec[2] == qn, (semt.name, rec[2], qn)
        pieces = [(out_ap, in_ap)]
        if split is not None:
            n_tot, step = split
            if n_tot > step:
                pieces = [(out_ap[:, a:min(a + step, n_tot)], in_ap[:, a:min(a + step, n_tot)])
                          for a in range(0, n_tot, step)]
        rec[1] += len(pieces)
        waits = self.pend

        def emit(waits=waits, pieces=pieces, o=eng.o, sem=rec[0]):
            for h, v in waits:
                o.wait_ge(h, v)
            for o_, i_ in pieces:
                o.dma_start(out=o_, in_=i_).then_inc(sem, 16)

        self.rec[qn].append(emit)
        ev = (id(rec), 16 * rec[1], rec[0])
        self._commit(ev, R, W)

    def barrier(self):
        for eng in self.E.values():
            self.pend = []
            for e2 in self.E.values():
                if e2.name == "sp" or e2.cnt == 0:
                    continue
                self._wait(eng, (e2.name, e2.cnt, e2.sem))
            for rec in self.dsems:
                if rec[1]:
                    self._wait(eng, (id(rec), 16 * rec[1], rec[0]))
            if self.ccnt:
                self._wait(eng, ("csem", self.ccnt, self.csem))
            waits = self.pend

            def emit(waits=waits, o=eng.o):
                for h, v in waits:
                    o.wait_ge(h, v)

            if waits:
                self.rec[eng.name].append(emit)

    def alt_eng(self, choices=("dve", "act")):
        self.alt += 1
        return choices[self.alt % len(choices)]

    def copy(self, en, out_t, out_ap, in_t, in_ap):
        if en == "act":
            self.op("act", lambda e: e.activation(out=out_ap, in_=in_ap, func=AF.Copy), R=[in_t], W=[out_t])
        else:
            self.op(en, lambda e: e.tensor_copy(out=out_ap, in_=in_ap), R=[in_t], W=[out_t])

    def rstd_from_ssq(self, out_t, out_ap, ps_t, ps_ap, n):
        self.op("act", lambda e: e.activation(out=out_ap, in_=ps_ap, func=AF.Sqrt, scale=1.0 / n,
                                              bias=self.epsb.ap), R=[ps_t, self.epsb], W=[out_t])
        self.op("dve", lambda e: e.reciprocal(out=out_ap, in_=out_ap), R=[out_t], W=[out_t])

    def run(self, io):
        nc, cfg = self.nc, self.cfg
        self.io = io
        with contextlib.ExitStack() as es:
            self.cf = self.sb(es, "cf", [128, NCONST], F32, dma=True)
            self.cb = self.sb(es, "cb", [128, NBF], BF16)
            po = ptab_offsets(cfg)
            self.po = po
            self.pt = self.sb(es, "pt", [128, po["n"]], F32, dma=True)
            self.epsb = self.sb(es, "epsb", [128, 1], F32)
            self.op("pool", lambda e: e.memset(self.epsb.ap, EPS), W=[self.epsb])
            self.dma("sp", self.cf.ap, io["consts"], W=[self.cf], semt=self.cf)
            self.dma("sp", self.pt.ap, io["ptab"], W=[self.pt], semt=self.pt)
            self.op("dve", lambda e: e.tensor_copy(out=self.cb.ap, in_=self.cf.ap[:, 0:NBF]),
                    R=[self.cf], W=[self.cb])
            self.cmf = self.sb(es, "cmf", [128, 768], F32, dma=True)
            self.cmb = self.sb(es, "cmb", [128, 768], BF16)
            self.cflag = self.sb(es, "cflag", [128, 16], F32, dma=True)
            self.dma("sp", self.cmf.ap, io["cmask"], W=[self.cmf], semt=self.cmf)
            self.dma("sp", self.cflag.ap, io["cflag"], W=[self.cflag], semt=self.cflag)
            self.op("dve", lambda e: e.tensor_copy(out=self.cmb.ap, in_=self.cmf.ap), R=[self.cmf], W=[self.cmb])
            if cfg.P > 1:
                zt = self.sb(es, "zt", [128, 2048], BF16, dma=True)
                self.op("pool", lambda e: e.memset(zt.ap, 0.0), W=[zt])
                self.dma("pool", io["HB"][7][:, 2048:4096], zt.ap, R=[zt], semt=zt)
                self.dma("pool", io["HB"][8][:, 2176:4096], zt.ap[:, 0:1920], R=[zt], semt=zt)
            import os
            stop = os.environ.get("KSTOP", "")
            steps = [("w", self.phase_weights), ("xin", self.phase_xin)]
            for l in range(cfg.DEPTH):
                steps += [(f"p1_{l}", lambda l=l: self.phase1(l)), (f"h_{l}", lambda l=l: self.halo_exchange(l)),
                          (f"a_{l}", lambda l=l: self.attn_a(l)),
                          (f"b_{l}", lambda l=l: self.attn_b(l)), (f"c_{l}", lambda l=l: self.attn_c(l)),
                          (f"p34_{l}", lambda l=l: self.phase34(l))]
            steps += [("xout", self.phase_xout)]
            skip = os.environ.get("KSKIP", "").split(",")
            self.barrier()
            self.phase_recs = []
            for name, fn in steps:
                if name not in skip:
                    fn()
                    self.barrier()
                    self.recycle_sems()
                if name == stop:
                    break
            self.flush()

    def cbf(self, n):
        o, w = CL[n]
        return self.cb.ap[:, o:o + w]

    def cff(self, n):
        o, w = CL[n]
        return self.cf.ap[:, o:o + w]

    def phase_weights(self):
        cfg, io = self.cfg, self.io
        PW = 2048
        with contextlib.ExitStack() as es:
            sf = [self.sb(es, f"wsf{i}", [128, PW], F32, dma=True) for i in range(3)]
            sbf = [self.sb(es, f"wsb{i}", [128, PW], BF16, dma=True) for i in range(3)]
            n = 0
            for l in range(cfg.DEPTH):
                for name in ("w_in", "w_branch", "w_gate_up", "w_out", "w_ff1", "w_ff2"):
                    W = io[name]
                    Wb = io["wb_" + name][l]
                    K, E = W.shape[1], W.shape[2]
                    for kc in range(K // 128):
                        for c0 in range(0, E, PW):
                            pw = min(PW, E - c0)
                            a, b = sf[n % 3], sbf[n % 3]
                            self.dma("sp", a.ap[:, 0:pw], W[l, kc * 128:(kc + 1) * 128, c0:c0 + pw], W=[a], semt=a)
                            en = ("dve", "act", "pool")[n % 3]
                            self.copy(en, b, b.ap[:, 0:pw], a, a.ap[:, 0:pw])
                            g0, g1 = c0 // 256, (c0 + pw) // 256
                            self.dma("pool", Wb[g0:g1, :, kc, :].rearrange("g p w -> p g w"),
                                     b.ap[:, 0:pw].rearrange("p (g w) -> p g w", w=256), R=[b], semt=b)
                            n += 1
            self.flush()

    def phase_xin(self):
        cfg, io = self.cfg, self.io
        KC = cfg.KC
        xTv = io["xT"].rearrange("(c p) s -> p c s", p=128)
        with contextlib.ExitStack() as es:
            xin = [self.sb(es, f"xin{i}", [128, cfg.D], F32, dma=True) for i in range(4)]
            xt = self.sb(es, "xt", [128, KC, 512], F32, dma=True)
            pss = [self.ps(es, f"psx{i}", [128, 512]) for i in range(4)]
            idf = self.cff("ident")
            for tt in range(cfg.NTT):
                for i in range(4):
                    r0 = tt * 512 + i * 128
                    self.dma("pool", xin[i].ap, io["x"][r0:r0 + 128, :], W=[xin[i]], semt=xin[i])
                for c in range(KC):
                    bank = pss[c % 4]
                    for i in range(4):
                        self.op("pe", lambda e, i=i, c=c, bank=bank: e.transpose(
                            bank.ap[:, i * 128:(i + 1) * 128], xin[i].ap[:, c * 128:(c + 1) * 128], idf),
                            R=[xin[i], self.cf], W=[bank], inc=(i == 3))
                    self.copy(self.alt_eng(), xt, xt.ap[:, c, :], bank, bank.ap)
                self.dma("sp", xTv[:, :, tt * 512:(tt + 1) * 512], xt.ap, R=[xt], semt=xt, split=(KC, 8))
            self.flush()

    def phase_xout(self):
        cfg, io = self.cfg, self.io
        KC = cfg.KC
        xTv = io["xT"].rearrange("(c p) s -> p c s", p=128)
        with contextlib.ExitStack() as es:
            xo = [self.sb(es, f"xo{i}", [128, cfg.D], F32, dma=True) for i in range(4)]
            xt = self.sb(es, "xt", [128, KC, 512], F32, dma=True)
            pss = [self.ps(es, f"psx{i}", [128, 512]) for i in range(4)]
            idf = self.cff("ident")
            n = 0
            for tt in range(cfg.NTT):
                self.dma("pool", xt.ap, xTv[:, :, tt * 512:(tt + 1) * 512], W=[xt], semt=xt, split=(KC, 8))
                for i in range(4):
                    for c4 in range(0, KC, 4):
                        nn = min(4, KC - c4)
                        bank = pss[n % 4]
                        n += 1
                        for cc in range(nn):
                            c = c4 + cc
                            self.op("pe", lambda e, i=i, c=c, cc=cc, bank=bank: e.transpose(
                                bank.ap[:, cc * 128:(cc + 1) * 128], xt.ap[:, c, i * 128:(i + 1) * 128], idf),
                                R=[xt, self.cf], W=[bank], inc=(cc == nn - 1))
                        self.copy(self.alt_eng(), xo[i], xo[i].ap[:, c4 * 128:(c4 + nn) * 128], bank,
                                  bank.ap[:, 0:nn * 128])
                    r0 = tt * 512 + i * 128
                    self.dma("sp", io["out"][r0:r0 + 128, :], xo[i].ap, R=[xo[i]], semt=xo[i])
            self.flush()

    def gemm_setup(self, es, nbuf=3):
        self.wbufs = [self.sb(es, f"wbuf{i}", [128, 32, 256], BF16, dma=True) for i in range(nbuf)]
        self.wn = 0
        self.pgroups = [[self.ps(es, f"pg{g}_{j}", [128, 512]) for j in range(2)] for g in range(3)]
        self.pgn = 0
        self.pending = None

    def loadw(self, Wb, g, kc0, kc1):
        wt = self.wbufs[self.wn % len(self.wbufs)]
        self.wn += 1
        nk = kc1 - kc0
        self.dma("sp", wt.ap[:, 0:nk, :], Wb[g, :, kc0:kc1, :], W=[wt], semt=wt)
        return wt

    def gemm_group(self, Wb, g, kc0, kc1, rhs, epi):
        wt = self.loadw(Wb, g, kc0, kc1)
        pg = self.pgroups[self.pgn % 3]
        self.pgn += 1
        nk = kc1 - kc0
        for ki in range(nk):
            rt, rap = rhs[ki]
            for j in range(2):
                self.op("pe", lambda e, j=j, ki=ki, rap=rap: e.matmul(
                    pg[j].ap, wt.ap[:, ki, j * 128:(j + 1) * 128], rap,
                    start=(ki == 0), stop=(ki == nk - 1)),
                    R=[wt, rt], W=[pg[j]], inc=(ki == nk - 1 and j == 1))
        self.flush_epi()
        gen = epi(pg)
        next(gen, None)
        self.pending = gen

    def flush_epi(self):
        if self.pending is not None:
            for _ in self.pending:
                pass
            self.pending = None

    def rmsnorm_tile(self, xa_chunks, xa_ap, out_tiles, gcol0, sq, pss, rstd):
        cfg = self.cfg
        KC = cfg.KC
        ones = self.cbf("ones")
        for c in range(KC):
            s = sq[c % 2]
            self.op("act", lambda e, c=c, s=s: e.activation(out=s.ap, in_=xa_ap[:, c, :], func=AF.Square),
                    R=[xa_chunks[c]], W=[s])
            self.op("pe", lambda e, c=c, s=s: e.matmul(pss.ap, ones, s.ap, start=(c == 0), stop=(c == KC - 1)),
                    R=[s, self.cb], W=[pss], inc=True)
        self.rstd_from_ssq(rstd, rstd.ap, pss, pss.ap, float(cfg.D))
        for c in range(KC):
            o = out_tiles[c]
            self.op("dve", lambda e, c=c, o=o: e.scalar_tensor_tensor(
                out=o.ap, in0=xa_ap[:, c, :], scalar=self.pt.ap[:, gcol0 + c:gcol0 + c + 1], in1=rstd.ap,
                op0=ALU.mult, op1=ALU.mult), R=[xa_chunks[c], rstd, self.pt], W=[o])

    def phase1(self, l):
        cfg, io = self.cfg, self.io
        KC, T = cfg.KC, cfg.T
        po = self.po
        xTv = io["xT"].rearrange("(c p) s -> p c s", p=128)
        projv = io["projT"].rearrange("(c p) s -> c p s", p=128)
        kbxv = io["kbx"].rearrange("(c p) s -> c p s", p=128)
        sgcv = io["sgc"].rearrange("(c p) s -> c p s", p=128)
        Wb = io["wb_w_in"][l]
        with contextlib.ExitStack() as es:
            self.gemm_setup(es)
            xacc = self.sb(es, "xacc", [128, KC, T], F32, dma=True)
            xch = [self.sub(xacc, xacc.ap[:, c, :], f"xacc{c}") for c in range(KC)]
            xnm = self.sb(es, "xn", [128, KC, T], BF16)
            xn = [self.sub(xnm, xnm.ap[:, c, :], f"xn{c}") for c in range(KC)]
            sq = [self.sb(es, f"sq{i}", [128, T], BF16) for i in range(2)]
            rstd = self.sb(es, "rstd", [128, T], F32)
            ps_x = self.ps(es, "ps_x", [128, 512])
            ps_y = self.ps(es, "ps_y", [128, 512])
            rot = self.sb(es, "rot", [128, 4, T], F32, dma=True)
            ob = [self.sb(es, f"ob{i}", [128, T], BF16, dma=True) for i in range(6)]
            of = [self.sb(es, f"of{i}", [128, T], F32, dma=True) for i in range(2)]
            tf = [self.sb(es, f"tf{i}", [128, T], F32) for i in range(2)]
            r2 = [self.sb(es, f"r2{i}", [128, T], F32) for i in range(2)]
            st = {"ob": 0, "of": 0, "tf": 0, "r2": 0, "sq": 0}

            def nxt(lst, k):
                st[k] += 1
                return lst[st[k] % len(lst)]

            for tt in range(cfg.NTT):
                ts = slice(tt * T, (tt + 1) * T)
                self.dma("pool", xacc.ap, xTv[:, :, ts], W=xch, semt=xacc, split=(KC, 8))
                self.dma("pool", rot.ap, io["rot"][:, :, ts].rearrange("f p s -> p f s"), W=[rot], semt=rot)
                self.rmsnorm_tile(xch, xacc.ap, xn, po["g1"] + l * KC, sq, ps_x, rstd)
                rhs = [(xn[c], xn[c].ap) for c in range(KC)]

                def epi(pg, g, ts=ts):
                    todo = []
                    for j in range(2):
                        ch = 2 * g + j
                        p = pg[j]
                        if ch < C_VA or C_QB <= ch < C_VB:
                            s = nxt(sq, "sq")
                            self.op("act", lambda e, s=s, p=p: e.activation(out=s.ap, in_=p.ap, func=AF.Square),
                                    R=[p], W=[s])
                            todo.append(("norm", ch, p, s))
                        elif C_QC <= ch < C_VC:
                            o = nxt(ob, "ob")
                            self.copy("act", o, o.ap, p, p.ap)
                            todo.append(("rot", ch, p, o))
                        elif C_GC <= ch < C_GL:
                            o = nxt(of, "of")
                            self.op("act", lambda e, o=o, p=p: e.activation(out=o.ap, in_=p.ap, func=AF.Silu),
                                    R=[p], W=[o])
                            self.dma("pool", sgcv[ch - C_GC][:, ts], o.ap, R=[o], semt=o)
                        else:
                            o = nxt(ob, "ob")
                            self.copy(self.alt_eng(), o, o.ap, p, p.ap)
                            self.dma("pool", projv[ch][:, ts], o.ap, R=[o], semt=o)
                    yield
                    for kind, ch, p, s in todo:
                        if kind == "norm":
                            isb = ch >= C_QB
                            red = self.cbf("blk64") if isb else self.cbf("ones")
                            self.op("pe", lambda e, s=s, red=red: e.matmul(ps_y.ap, red, s.ap, start=True, stop=True),
                                    R=[s, self.cb], W=[ps_y])
                            r = nxt(r2, "r2")
                            self.rstd_from_ssq(r, r.ap, ps_y, ps_y.ap, 64.0 if isb else 128.0)
                            if isb:
                                gc = po["qkb"] + 2 * l + (1 if ch == C_KB else 0)
                            else:
                                gc = po["qka"] + 2 * l + (1 if ch >= C_KA else 0)
                            o = nxt(ob, "ob")
                            self.op("dve", lambda e, o=o, p=p, r=r, gc=gc: e.scalar_tensor_tensor(
                                out=o.ap, in0=p.ap, scalar=self.pt.ap[:, gc:gc + 1], in1=r.ap,
                                op0=ALU.mult, op1=ALU.mult), R=[p, r, self.pt], W=[o])
                            if ch == C_KB:
                                for q4 in range(4):
                                    self.op("pe", lambda e, q4=q4, o=o: e.matmul(
                                        ps_y.ap, self.cbf(f"e{q4}"), o.ap, start=True, stop=True),
                                        R=[o, self.cb], W=[ps_y])
                                    o2 = nxt(ob, "ob")
                                    self.copy(self.alt_eng(), o2, o2.ap, ps_y, ps_y.ap)
                                    self.dma("pool", kbxv[q4][:, ts], o2.ap, R=[o2], semt=o2)
                            else:
                                self.dma("pool", projv[ch][:, ts], o.ap, R=[o], semt=o)
                        else:
                            isk = ch >= C_KC
                            self.op("pe", lambda e, s=s: e.matmul(ps_y.ap, self.cbf("rmatT"), s.ap, start=True, stop=True),
                                    R=[s, self.cb], W=[ps_y])
                            t1, t2 = nxt(tf, "tf"), nxt(tf, "tf")
                            ci, si = (2, 3) if isk else (0, 1)
                            self.op("dve", lambda e, t1=t1, p=p, ci=ci: e.tensor_tensor(
                                out=t1.ap, in0=p.ap, in1=rot.ap[:, ci, :], op=ALU.mult), R=[p, rot], W=[t1])
                            self.op("dve", lambda e, t2=t2, si=si: e.tensor_tensor(
                                out=t2.ap, in0=ps_y.ap, in1=rot.ap[:, si, :], op=ALU.mult), R=[ps_y, rot], W=[t2])
                            o = nxt(ob, "ob")
                            self.op("pool", lambda e, o=o, t1=t1, t2=t2: e.tensor_tensor(
                                out=o.ap, in0=t1.ap, in1=t2.ap, op=ALU.add), R=[t1, t2], W=[o])
                            self.dma("pool", projv[ch][:, ts], o.ap, R=[o], semt=o)

                for g in range(N_CH // 2):
                    self.gemm_group(Wb, g, 0, KC, rhs, lambda pg, g=g: epi(pg, g))
                self.flush_epi()
            self.flush()

    def phase2(self, l):
        self.attn_a(l)
        self.barrier()
        self.attn_b(l)
        self.barrier()
        self.attn_c(l)

    def halo_select(self, tl, out_ap, gb, c0, halo, hst):
        ncand = self.cfg.P - 1
        for j in range(ncand):
            self.hsn = getattr(self, "hsn", 0) + 1
            st = hst[self.hsn % len(hst)]
            self.dma("pool", st.ap[:, 0:halo], gb[j * 128:(j + 1) * 128, c0:c0 + halo], W=[st], semt=st)
            en = "dve"
            sc = self.cflag.ap[:, j:j + 1]
            if j == 0:
                self.op(en, lambda e, st=st, sc=sc: e.tensor_scalar(
                    out=out_ap, in0=st.ap[:, 0:halo], scalar1=sc, scalar2=None, op0=ALU.mult),
                    R=[st, self.cflag], W=[tl])
            else:
                self.op(en, lambda e, st=st, sc=sc: e.scalar_tensor_tensor(
                    out=out_ap, in0=st.ap[:, 0:halo], scalar=sc, in1=out_ap, op0=ALU.mult, op1=ALU.add),
                    R=[st, self.cflag, tl], W=[tl])

    def halo_exchange(self, l):
        cfg, io = self.cfg, self.io
        if cfg.P == 1:
            return
        S = cfg.S
        projv = io["projT"].rearrange("(c p) s -> c p s", p=128)
        kbxv = io["kbx"].rearrange("(c p) s -> c p s", p=128)
        hs = TT(None, "hsem")
        self.give_sem(hs)
        for g in range(3):
            for h in range(6):
                for kind, base in ((0, C_KA), (1, C_VA)):
                    pg, c0, halo = halo_item_a(g, kind, h)
                    self.dma("pool", io["HB"][pg][:, c0:c0 + halo], projv[base + g * 6 + h][:, S - halo:S], semt=hs)
        for i in range(5):
            pg, c0, halo = halo_item_b(i)
            src = kbxv[i] if i < 4 else projv[C_VB]
            self.dma("pool", io["HB"][pg][:, c0:c0 + halo], src[:, S - halo:S], semt=hs)
        self.barrier()
        for pg in range(N_PAGES):
            self.allgather(io["HB"][pg], io["GB"][pg])
        self.barrier()
        self.flush()

    def attn_a(self, l):
        cfg, io = self.cfg, self.io
        S, SB = cfg.S, cfg.SB
        projv = io["projT"].rearrange("(c p) s -> c p s", p=128)
        oTv = io["oT"].rearrange("(c p) s -> c p s", p=128)
        ident, ones = self.cbf("ident"), self.cbf("ones")
        scale = 128 ** -0.5
        HM = cfg.P > 1
        with contextlib.ExitStack() as es:
            qt = [self.sb(es, f"qt{i}", [128, SB], BF16, dma=True) for i in range(2)]
            kt = [self.sb(es, f"kt{i}", [128, 2 * SB], BF16, dma=True) for i in range(2)]
            vt = [self.sb(es, f"vt{i}", [128, 2 * SB], BF16, dma=True) for i in range(2)]
            acc = [self.sb(es, f"acc{i}", [128, 2, SB], F32) for i in range(2)]
            ost = [self.sb(es, f"ost{i}", [128, SB], BF16, dma=True) for i in range(2)]
            vps = [self.ps(es, f"vps{i}", [128, 256], BF16) for i in range(2)]
            sps = [self.ps(es, f"sps{i}", [128, 256]) for i in range(2)]
            ops = [self.ps(es, f"ops{i}", [128, 2, 128]) for i in range(2)]
            vtok = [self.sb(es, f"vtok{i}", [128, 256], BF16) for i in range(3)]
            pT = [self.sb(es, f"pT{i}", [128, 256], BF16) for i in range(3)]
            hst = [self.sb(es, f"hst{i}", [128, SB], BF16, dma=True) for i in range(4)] if HM else None
            units = [(sbi, h, g) for sbi in range(S // SB) for h in range(6) for g in range(3)]

            def load(i):
                sbi, h, g = units[i]
                w, r = A_GROUPS[g]
                halo = 128 * r
                t0 = sbi * SB
                lo = max(0, t0 - halo)
                off = t0 - lo
                q_, k_, v_ = qt[i % 2], kt[i % 2], vt[i % 2]
                ch = g * 6 + h
                self.dma("pool", q_.ap, projv[C_QA + ch][:, t0:t0 + SB], W=[q_], semt=q_)
                if HM and sbi == 0:
                    for kind, base, tl in ((0, C_KA, k_), (1, C_VA, v_)):
                        pg, c0, _ = halo_item_a(g, kind, h)
                        self.halo_select(tl, tl.ap[:, 0:halo], io["GB"][pg], c0, halo, hst)
                        self.dma("pool", tl.ap[:, halo:halo + SB], projv[base + ch][:, 0:SB], W=[tl], semt=tl)
                else:
                    self.dma("pool", k_.ap[:, 0:off + SB], projv[C_KA + ch][:, lo:t0 + SB], W=[k_], semt=k_)
                    self.dma("pool", v_.ap[:, 0:off + SB], projv[C_VA + ch][:, lo:t0 + SB], W=[v_], semt=v_)

            nb = 0
            load(0)
            for i, (sbi, h, g) in enumerate(units):
                if i + 1 < len(units):
                    load(i + 1)
                w, r = A_GROUPS[g]
                halo = 128 * r
                t0 = sbi * SB
                off = halo if HM else t0 - max(0, t0 - halo)
                q_, k_, v_ = qt[i % 2], kt[i % 2], vt[i % 2]
                nh = sbi * 6 + h
                ac = acc[nh % 2]
                for u in range(SB // halo):
                    for rho in range(r):
                        bq = u * halo + rho
                        has_prev = HM or (t0 + u * halo) > 0
                        bnd = HM and (t0 + u * halo) == 0
                        ext = 127 * r + 1
                        qs = slice(bq, bq + ext, r)
                        cs = slice(off + bq, off + bq + ext, r)
                        prs = slice(off + bq - halo, off + bq - halo + ext, r)
                        vp, sp_, op_ = vps[nb % 2], sps[nb % 2], ops[nb % 2]
                        vk, p_ = vtok[nb % 3], pT[nb % 3]
                        nb += 1
                        if has_prev:
                            self.op("pe", lambda e, vp=vp, v_=v_, prs=prs: e.transpose(
                                vp.ap[:, 0:128], v_.ap[:, prs], ident), R=[v_, self.cb], W=[vp], inc=False)
                        self.op("pe", lambda e, vp=vp, v_=v_, cs=cs: e.transpose(
                            vp.ap[:, 128:256], v_.ap[:, cs], ident), R=[v_, self.cb], W=[vp])
                        c0 = 0 if has_prev else 128
                        self.copy("dve", vk, vk.ap[:, c0:256], vp, vp.ap[:, c0:256])
                        mk = self.cbf("maskA") if has_prev else self.cbf("maskAf")
                        if bnd:
                            mk = self.cmb.ap[:, 0:256]
                        self.op("pe", lambda e, sp_=sp_, mk=mk: e.matmul(sp_.ap, ident, mk, start=True, stop=False),
                                R=[self.cb, self.cmb], W=[sp_], inc=False)
                        if has_prev:
                            self.op("pe", lambda e, sp_=sp_, k_=k_, q_=q_, prs=prs, qs=qs: e.matmul(
                                sp_.ap[:, 0:128], k_.ap[:, prs], q_.ap[:, qs], start=False, stop=False),
                                R=[k_, q_], W=[sp_], inc=False)
                        self.op("pe", lambda e, sp_=sp_, k_=k_, q_=q_, cs=cs, qs=qs: e.matmul(
                            sp_.ap[:, 128:256], k_.ap[:, cs], q_.ap[:, qs], start=False, stop=True),
                            R=[k_, q_], W=[sp_])
                        self.op("act", lambda e, p_=p_, sp_=sp_: e.activation(
                            out=p_.ap, in_=sp_.ap, func=AF.Exp, scale=scale), R=[sp_], W=[p_])
                        if has_prev:
                            self.op("pe", lambda e, op_=op_, vk=vk, p_=p_: e.matmul(
                                op_.ap[:, 0, :], vk.ap[:, 0:128], p_.ap[:, 0:128], start=True, stop=False),
                                R=[vk, p_], W=[op_], inc=False)
                        self.op("pe", lambda e, op_=op_, vk=vk, p_=p_, hp=has_prev: e.matmul(
                            op_.ap[:, 0, :], vk.ap[:, 128:256], p_.ap[:, 128:256], start=(not hp), stop=True),
                            R=[vk, p_], W=[op_], inc=False)
                        if has_prev:
                            self.op("pe", lambda e, op_=op_, p_=p_: e.matmul(
                                op_.ap[:, 1, :], ones, p_.ap[:, 0:128], start=True, stop=False),
                                R=[p_, self.cb], W=[op_], inc=False)
                        self.op("pe", lambda e, op_=op_, p_=p_, hp=has_prev: e.matmul(
                            op_.ap[:, 1, :], ones, p_.ap[:, 128:256], start=(not hp), stop=True),
                            R=[p_, self.cb], W=[op_])
                        if g == 0:
                            self.op("dve", lambda e, ac=ac, op_=op_, qs=qs: e.tensor_copy(
                                out=ac.ap[:, :, qs], in_=op_.ap), R=[op_], W=[ac])
                        else:
                            self.op("dve", lambda e, ac=ac, op_=op_, qs=qs: e.tensor_tensor(
                                out=ac.ap[:, :, qs], in0=ac.ap[:, :, qs], in1=op_.ap, op=ALU.add),
                                R=[op_, ac], W=[ac])
                if g == 2:
                    o_ = ost[nh % 2]
                    self.op("dve", lambda e, ac=ac: e.reciprocal(out=ac.ap[:, 1, :], in_=ac.ap[:, 1, :]), R=[ac], W=[ac])
                    self.op("pool", lambda e, o_=o_, ac=ac: e.tensor_tensor(
                        out=o_.ap, in0=ac.ap[:, 0, :], in1=ac.ap[:, 1, :], op=ALU.mult), R=[ac], W=[o_])
                    self.dma("pool", oTv[h][:, t0:t0 + SB], o_.ap, R=[o_], semt=o_)
            self.flush()

    def attn_b(self, l):
        cfg, io = self.cfg, self.io
        S, SB = cfg.S, cfg.SB
        po = self.po
        projv = io["projT"].rearrange("(c p) s -> c p s", p=128)
        kbxv = io["kbx"].rearrange("(c p) s -> c p s", p=128)
        oTv = io["oT"].rearrange("(c p) s -> c p s", p=128)
        ident = self.cbf("ident")
        scale = 64 ** -0.5
        HM = cfg.P > 1
        NBK = SB // 128
        with contextlib.ExitStack() as es:
            esink = self.sb(es, "esink", [128, 8], F32)
            hst = [self.sb(es, f"hst{i}", [128, 128], BF16, dma=True) for i in range(4)] if HM else None
            self.op("act", lambda e: e.activation(out=esink.ap, in_=self.pt.ap[:, po["sink"] + 8 * l:po["sink"] + 8 * l + 8],
                                                  func=AF.Exp), R=[self.pt], W=[esink])
            qt = [self.sb(es, f"qt{i}", [128, SB], BF16, dma=True) for i in range(2)]
            klo = [self.sb(es, f"klo{i}", [128, 128 + SB], BF16, dma=True) for i in range(2)]
            khi = [self.sb(es, f"khi{i}", [128, 128 + SB], BF16, dma=True) for i in range(2)]
            vt = [self.sb(es, f"vt{i}", [128, 128 + SB], BF16, dma=True) for i in range(2)]
            vlo = [self.sb(es, f"vlo{i}", [128, NBK + 1, 128], BF16) for i in range(2)]
            vhi = [self.sb(es, f"vhi{i}", [128, NBK + 1, 128], BF16) for i in range(2)]
            for t in vlo + vhi:
                self.op("pool", lambda e, t=t: e.memset(t.ap, 0.0), W=[t])
            ost = [self.sb(es, f"ost{i}", [128, SB], BF16, dma=True) for i in range(2)]
            vps = [self.ps(es, f"vps{i}", [128, 128], BF16) for i in range(2)]
            sps = [self.ps(es, f"sps{i}", [128, 512]) for i in range(2)]
            ops = [self.ps(es, f"ops{i}", [128, 2, 128]) for i in range(2)]
            pT = [self.sb(es, f"pT{i}", [128, 512], BF16) for i in range(3)]
            tden = [self.sb(es, f"tden{i}", [128, 128], F32) for i in range(2)]
            nkv = 0
            nq = 0
            nb = 0
            for sbi in range(S // SB):
                t0 = sbi * SB
                lo = max(0, t0 - 128)
                off = 128 if HM else t0 - lo
                for kv in range(2):
                    kl, kh, v_ = klo[nkv % 2], khi[nkv % 2], vt[nkv % 2]
                    vl, vh = vlo[nkv % 2], vhi[nkv % 2]
                    nkv += 1
                    if HM and sbi == 0:
                        for tl, it, src in ((kl, 2 * kv, kbxv[2 * kv]), (kh, 2 * kv + 1, kbxv[2 * kv + 1]),
                                            (v_, 4, projv[C_VB])):
                            pg, c0, _ = halo_item_b(it)
                            self.halo_select(tl, tl.ap[:, 0:128], io["GB"][pg], c0, 128, hst)
                            self.dma("pool", tl.ap[:, 128:128 + SB], src[:, 0:SB], W=[tl], semt=tl)
                    else:
                        self.dma("pool", kl.ap[:, 0:off + SB], kbxv[2 * kv][:, lo:t0 + SB], W=[kl], semt=kl)
                        self.dma("pool", kh.ap[:, 0:off + SB], kbxv[2 * kv + 1][:, lo:t0 + SB], W=[kh], semt=kh)
                        self.dma("pool", v_.ap[:, 0:off + SB], projv[C_VB][:, lo:t0 + SB], W=[v_], semt=v_)
                    nblk = (off + SB) // 128
                    for b in range(nblk):
                        vp = vps[b % 2]
                        self.op("pe", lambda e, vp=vp, v_=v_, b=b: e.transpose(
                            vp.ap, v_.ap[:, b * 128:(b + 1) * 128], ident), R=[v_, self.cb], W=[vp])
                        self.copy("dve", vl, vl.ap[:, b, 0:64], vp, vp.ap[:, kv * 64:(kv + 1) * 64])
                        self.copy("act", vh, vh.ap[:, b, 64:128], vp, vp.ap[:, kv * 64:(kv + 1) * 64])
                    for j in range(4):
                        jj = kv * 4 + j
                        q_ = qt[nq % 2]
                        o_ = ost[nq % 2]
                        nq += 1
                        self.dma("pool", q_.ap, projv[C_QB + jj][:, t0:t0 + SB], W=[q_], semt=q_)
                        for b in range(NBK):
                            has_prev = HM or (t0 + b * 128) > 0
                            bnd = HM and (t0 + b * 128) == 0
                            cb_ = off // 128 + b
                            qs = slice(b * 128, (b + 1) * 128)
                            cs = slice(cb_ * 128, (cb_ + 1) * 128)
                            prs = slice((cb_ - 1) * 128, cb_ * 128)
                            sp_, op_, p_ = sps[nb % 2], ops[nb % 2], pT[nb % 3]
                            td = tden[nb % 2]
                            nb += 1
                            mk = self.cbf("maskB") if has_prev else self.cbf("maskBf")
                            if bnd:
                                mk = self.cmb.ap[:, 256:768]
                            self.op("pe", lambda e, sp_=sp_, mk=mk: e.matmul(sp_.ap, ident, mk, start=True, stop=False),
                                    R=[self.cb, self.cmb], W=[sp_], inc=False)
                            for hh, kk in enumerate((kl, kh)):
                                if has_prev:
                                    self.op("pe", lambda e, sp_=sp_, kk=kk, hh=hh, q_=q_, prs=prs, qs=qs: e.matmul(
                                        sp_.ap[:, hh * 256:hh * 256 + 128], kk.ap[:, prs], q_.ap[:, qs],
                                        start=False, stop=False), R=[kk, q_], W=[sp_], inc=False)
                                self.op("pe", lambda e, sp_=sp_, kk=kk, hh=hh, q_=q_, cs=cs, qs=qs: e.matmul(
                                    sp_.ap[:, hh * 256 + 128:hh * 256 + 256], kk.ap[:, cs], q_.ap[:, qs],
                                    start=False, stop=(hh == 1)), R=[kk, q_], W=[sp_], inc=(hh == 1))
                            self.op("act", lambda e, p_=p_, sp_=sp_: e.activation(
                                out=p_.ap, in_=sp_.ap, func=AF.Exp, scale=scale), R=[sp_], W=[p_])
                            terms = []
                            for hh, (vv, on) in enumerate(((vl, "ones_lo"), (vh, "ones_hi"))):
                                if has_prev:
                                    terms.append((vv, vv.ap[:, cb_ - 1, :], on, hh * 256))
                                terms.append((vv, vv.ap[:, cb_, :], on, hh * 256 + 128))
                            nt = len(terms)
                            for ti, (vv, vap, on, pc) in enumerate(terms):
                                self.op("pe", lambda e, op_=op_, vap=vap, p_=p_, pc=pc, ti=ti, nt=nt: e.matmul(
                                    op_.ap[:, 0, :], vap, p_.ap[:, pc:pc + 128], start=(ti == 0), stop=(ti == nt - 1)),
                                    R=[vv, p_], W=[op_], inc=False)
                            for ti, (vv, vap, on, pc) in enumerate(terms):
                                self.op("pe", lambda e, op_=op_, on=on, p_=p_, pc=pc, ti=ti, nt=nt: e.matmul(
                                    op_.ap[:, 1, :], self.cbf(on), p_.ap[:, pc:pc + 128], start=(ti == 0),
                                    stop=(ti == nt - 1)), R=[p_, self.cb], W=[op_], inc=(ti == nt - 1))
                            self.op("dve", lambda e, td=td, op_=op_, jj=jj: e.tensor_scalar(
                                out=td.ap, in0=op_.ap[:, 1, :], scalar1=esink.ap[:, jj:jj + 1], scalar2=None,
                                op0=ALU.add), R=[op_, esink], W=[td])
                            self.op("dve", lambda e, td=td: e.reciprocal(out=td.ap, in_=td.ap), R=[td], W=[td])
                            self.op("dve", lambda e, o_=o_, op_=op_, td=td, qs=qs: e.tensor_tensor(
                                out=o_.ap[:, qs], in0=op_.ap[:, 0, :], in1=td.ap, op=ALU.mult),
                                R=[op_, td], W=[o_])
                        self.dma("pool", oTv[6 + jj][:, t0:t0 + SB], o_.ap, R=[o_], semt=o_)
            self.flush()

    def attn_c(self, l):
        cfg, io = self.cfg, self.io
        S = cfg.S
        CS = 512
        projv = io["projT"].rearrange("(c p) s -> c p s", p=128)
        sgcv = io["sgc"].rearrange("(c p) s -> c p s", p=128)
        oTv = io["oT"].rearrange("(c p) s -> c p s", p=128)
        ident, ones = self.cbf("ident"), self.cbf("ones")
        kdo = CL["kd"][0]
        import os
        lvl = int(os.environ.get("KC_LEVEL", "9"))
        with contextlib.ExitStack() as es:
            qkv = [[self.sb(es, f"qkv{i}_{h}", [128, 4, CS], BF16, dma=True) for h in range(4)] for i in range(2)]
            qsub = [[(self.give_sem(self.sub(qkv[i][h], qkv[i][h].ap[:, 0, :], "cq")),
                      self.give_sem(self.sub(qkv[i][h], qkv[i][h].ap[:, 1, :], "ck")),
                      self.give_sem(self.sub(qkv[i][h], qkv[i][h].ap[:, 2:4, :], "cv"))) for h in range(4)]
                    for i in range(2)]
            gt = [[self.sb(es, f"gt{i}_{h}", [128, 2, CS], F32, dma=True) for h in range(4)] for i in range(2)]
            state = [self.sb(es, f"state{h}", [128, 256], F32) for h in range(4)]
            sbf = [self.sb(es, f"sbf{h}", [128, 256], BF16) for h in range(4)]
            for h in range(4):
                self.op("pool", lambda e, h=h: e.memset(state[h].ap, 0.0), W=[state[h]])
                self.op("pool", lambda e, h=h: e.memset(sbf[h].ap, 0.0), W=[sbf[h]])
            ost = [self.sb(es, f"ost{i}", [128, 2, CS], BF16, dma=True) for i in range(8)]
            pa = [self.ps(es, f"pa{i}", [128, 384], BF16) for i in range(2)]
            pb = [self.ps(es, f"pb{i}", [128, 384]) for i in range(2)]
            pc = [self.ps(es, f"pc{i}", [128, 512]) for i in range(2)]
            vtok = [self.sb(es, f"vtok{i}", [128, 256], BF16) for i in range(2)]
            kdec = [self.sb(es, f"kdec{i}", [128, 128], BF16) for i in range(2)]
            inb = [self.sb(es, f"inb{i}", [128, 128], BF16) for i in range(2)]
            qdec = [self.sb(es, f"qdec{i}", [128, 128], BF16) for i in range(2)]
            sqc = [self.sb(es, f"sqc{i}", [128, 256], BF16) for i in range(2)]
            rs = [self.sb(es, f"rs{i}", [128, 256], F32) for i in range(2)]
            tm = [self.sb(es, f"tm{i}", [128, 256], F32) for i in range(2)]
            if cfg.P > 1:
                n1 = 0
                for ci in range(S // CS):
                    t0 = ci * CS
                    for h in range(4):
                        _, tk, tv = qsub[ci % 2][h]
                        self.dma("pool", tk.ap, projv[C_KC + h][:, t0:t0 + CS], W=[tk], semt=tk)
                        self.dma("pool", tv.ap, io["projT"][(C_VC + 2 * h) * 128:(C_VC + 2 * h + 2) * 128, t0:t0 + CS]
                                 .rearrange("(e p) s -> p e s", p=128), W=[tv], semt=tv)
                    for c in range(CS // 128):
                        cs = slice(c * 128, (c + 1) * 128)
                        for h in range(4):
                            b = qkv[ci % 2][h]
                            _, tk, tv = qsub[ci % 2][h]
                            A, C = pa[n1 % 2], pc[n1 % 2]
                            vk, kd_ = vtok[n1 % 2], kdec[n1 % 2]
                            n1 += 1
                            gam = chunk_decay(h)
                            for e_ in range(2):
                                self.op("pe", lambda e, A=A, b=b, e_=e_, cs=cs: e.transpose(
                                    A.ap[:, e_ * 128:(e_ + 1) * 128], b.ap[:, 2 + e_, cs], ident),
                                    R=[tv, self.cb], W=[A], inc=False)
                            self.op("pe", lambda e, A=A, b=b, cs=cs: e.transpose(A.ap[:, 256:384], b.ap[:, 1, cs], ident),
                                    R=[tk, self.cb], W=[A])
                            self.copy("act", vk, vk.ap, A, A.ap[:, 0:256])
                            self.op("dve", lambda e, kd_=kd_, A=A, h=h: e.tensor_scalar(
                                out=kd_.ap, in0=A.ap[:, 256:384], scalar1=self.cf.ap[:, kdo + h:kdo + h + 1],
                                scalar2=None, op0=ALU.mult), R=[A, self.cf], W=[kd_])
                            self.op("pe", lambda e, C=C, kd_=kd_, vk=vk: e.matmul(C.ap[:, 256:512], kd_.ap, vk.ap,
                                                                               start=True, stop=True), R=[kd_, vk], W=[C])
                            self.op("dve", lambda e, h=h, C=C, gam=gam: e.scalar_tensor_tensor(
                                out=state[h].ap, in0=state[h].ap, scalar=gam, in1=C.ap[:, 256:512],
                                op0=ALU.mult, op1=ALU.add), R=[state[h], C], W=[state[h]])
                stx = self.sb(es, "stx", [128, 1024], F32, dma=True)
                for h in range(4):
                    self.copy("dve", stx, stx.ap[:, h * 256:(h + 1) * 256], state[h], state[h].ap)
                self.dma("pool", io["SBf"], stx.ap, R=[stx], semt=stx)
                self.barrier()
                self.allgather(io["SBf"], io["SG"])
                self.barrier()
                for j in range(cfg.P - 1):
                    self.dma("pool", stx.ap, io["SG"][j * 128:(j + 1) * 128, :], W=[stx], semt=stx)
                    for h in range(4):
                        sc = self.cflag.ap[:, 4 + h * 3 + j:5 + h * 3 + j]
                        if j == 0:
                            self.op("dve", lambda e, h=h, sc=sc: e.tensor_scalar(
                                out=state[h].ap, in0=stx.ap[:, h * 256:(h + 1) * 256], scalar1=sc,
                                scalar2=None, op0=ALU.mult), R=[stx, self.cflag], W=[state[h]])
                        else:
                            self.op("dve", lambda e, h=h, sc=sc: e.scalar_tensor_tensor(
                                out=state[h].ap, in0=stx.ap[:, h * 256:(h + 1) * 256], scalar=sc, in1=state[h].ap,
                                op0=ALU.mult, op1=ALU.add), R=[stx, self.cflag, state[h]], W=[state[h]])
                for h in range(4):
                    self.copy("act", sbf[h], sbf[h].ap, state[h], state[h].ap)
            n = 0
            for ci in range(S // CS):
                t0 = ci * CS
                bufs = qkv[ci % 2]
                gts = gt[ci % 2]
                for h in range(4):
                    b = bufs[h]
                    tq, tk, tv = qsub[ci % 2][h]
                    self.dma("pool", tq.ap, projv[C_QC + h][:, t0:t0 + CS], W=[tq], semt=tq)
                    self.dma("pool", tk.ap, projv[C_KC + h][:, t0:t0 + CS], W=[tk], semt=tk)
                    self.dma("pool", tv.ap, io["projT"][(C_VC + 2 * h) * 128:(C_VC + 2 * h + 2) * 128, t0:t0 + CS]
                             .rearrange("(e p) s -> p e s", p=128), W=[tv], semt=tv)
                    self.dma("pool", gts[h].ap, io["sgc"][2 * h * 128:(2 * h + 2) * 128, t0:t0 + CS]
                             .rearrange("(e p) s -> p e s", p=128), W=[gts[h]], semt=gts[h])
                outs = [ost[(ci % 2) * 4 + h] for h in range(4)]
                for c in range(CS // 128):
                    cs = slice(c * 128, (c + 1) * 128)
                    for h in range(4):
                        b = bufs[h]
                        tq, tk, tv = qsub[ci % 2][h]
                        A, B, C = pa[n % 2], pb[n % 2], pc[n % 2]
                        vk, kd_, ib, qd_, sq_, r_, t_ = (vtok[n % 2], kdec[n % 2], inb[n % 2], qdec[n % 2],
                                                        sqc[n % 2], rs[n % 2], tm[n % 2])
                        n += 1
                        gam = chunk_decay(h)
                        if lvl < 1:
                            continue
                        self.op("pool", lambda e, qd_=qd_, b=b, cs=cs, h=h: e.tensor_tensor(
                            out=qd_.ap, in0=b.ap[:, 0, cs], in1=self.cff(f"qd{h}"), op=ALU.mult),
                            R=[tq, self.cf], W=[qd_])
                        if lvl < 2:
                            continue
                        for e_ in range(2):
                            self.op("pe", lambda e, A=A, b=b, e_=e_, cs=cs: e.transpose(
                                A.ap[:, e_ * 128:(e_ + 1) * 128], b.ap[:, 2 + e_, cs], ident),
                                R=[tv, self.cb], W=[A], inc=False)
                        self.op("pe", lambda e, A=A, b=b, cs=cs: e.transpose(A.ap[:, 256:384], b.ap[:, 1, cs], ident),
                                R=[tk, self.cb], W=[A])
                        self.copy("act", vk, vk.ap, A, A.ap[:, 0:256])
                        self.op("dve", lambda e, kd_=kd_, A=A, h=h: e.tensor_scalar(
                            out=kd_.ap, in0=A.ap[:, 256:384], scalar1=self.cf.ap[:, kdo + h:kdo + h + 1], scalar2=None,
                            op0=ALU.mult), R=[A, self.cf], W=[kd_])
                        if lvl < 3:
                            continue
                        self.op("pe", lambda e, B=B, b=b, cs=cs: e.matmul(B.ap[:, 0:128], b.ap[:, 1, cs], b.ap[:, 0, cs],
                                                                       start=True, stop=True), R=[tq, tk], W=[B])
                        self.op("dve", lambda e, ib=ib, B=B, h=h: e.tensor_tensor(
                            out=ib.ap, in0=B.ap[:, 0:128], in1=self.cff(f"decay{h}"), op=ALU.mult),
                            R=[B, self.cf], W=[ib])
                        if lvl < 4:
                            continue
                        for e_ in range(2):
                            self.op("pe", lambda e, C=C, vk=vk, ib=ib, e_=e_: e.matmul(
                                C.ap[:, e_ * 128:(e_ + 1) * 128], vk.ap[:, e_ * 128:(e_ + 1) * 128], ib.ap,
                                start=True, stop=False), R=[vk, ib], W=[C], inc=False)
                            self.op("pe", lambda e, C=C, qd_=qd_, e_=e_, h=h: e.matmul(
                                C.ap[:, e_ * 128:(e_ + 1) * 128], sbf[h].ap[:, e_ * 128:(e_ + 1) * 128], qd_.ap,
                                start=False, stop=True), R=[sbf[h], qd_], W=[C], inc=False)
                        if lvl < 5:
                            continue
                        self.op("pe", lambda e, C=C, kd_=kd_, vk=vk: e.matmul(C.ap[:, 256:512], kd_.ap, vk.ap,
                                                                           start=True, stop=True), R=[kd_, vk], W=[C])
                        self.op("dve", lambda e, h=h, C=C, gam=gam: e.scalar_tensor_tensor(
                            out=state[h].ap, in0=state[h].ap, scalar=gam, in1=C.ap[:, 256:512],
                            op0=ALU.mult, op1=ALU.add), R=[state[h], C], W=[state[h]])
                        self.copy("act", sbf[h], sbf[h].ap, state[h], state[h].ap)
                        if lvl < 6:
                            continue
                        self.op("act", lambda e, sq_=sq_, C=C: e.activation(out=sq_.ap, in_=C.ap[:, 0:256], func=AF.Square),
                                R=[C], W=[sq_])
                        for half in range(2):
                            for e_ in range(2):
                                self.op("pe", lambda e, B=B, sq_=sq_, half=half, e_=e_: e.matmul(
                                    B.ap[:, 128 + half * 128:256 + half * 128], ones, sq_.ap[:, e_ * 128:(e_ + 1) * 128],
                                    start=(e_ == 0), stop=(e_ == 1)), R=[sq_, self.cb], W=[B],
                                    inc=(half == 1 and e_ == 1))
                        self.rstd_from_ssq(r_, r_.ap, B, B.ap[:, 128:384], 256.0)
                        self.op("dve", lambda e, t_=t_, C=C, r_=r_: e.tensor_tensor(
                            out=t_.ap, in0=C.ap[:, 0:256], in1=r_.ap, op=ALU.mult), R=[C, r_], W=[t_])
                        o_ = outs[h]
                        self.op("pool", lambda e, o_=o_, t_=t_, cs=cs, h=h, gts=gts: e.tensor_tensor(
                            out=o_.ap[:, :, cs], in0=t_.ap.rearrange("p (e s) -> p e s", e=2), in1=gts[h].ap[:, :, cs],
                            op=ALU.mult), R=[t_, gts[h]], W=[o_])
                for h in range(4):
                    self.dma("pool", io["oT"][(14 + 2 * h) * 128:(16 + 2 * h) * 128, t0:t0 + CS]
                             .rearrange("(e p) s -> p e s", p=128), outs[h].ap, R=[outs[h]], semt=outs[h])
            self.flush()

    def phase34(self, l):
        cfg, io = self.cfg, self.io
        KC, T, D, DFF, FB = cfg.KC, cfg.T, cfg.D, cfg.DFF, cfg.FB
        po = self.po
        xTv = io["xT"].rearrange("(c p) s -> p c s", p=128)
        oTv = io["oT"].rearrange("(c p) s -> p c s", p=128)
        glv = io["projT"][C_GL * 128:(C_GL + 2) * 128, :].rearrange("(c p) s -> p c s", p=128)
        NG = D // 256
        FC = FB // 128
        NSLOT = max(22 + 2 + KC, KC + FC)
        br_kc = ((0, 6), (6, 14), (14, 22))
        with contextlib.ExitStack() as es:
            self.gemm_setup(es, 2)
            xacc = self.sb(es, "xacc", [128, KC, T], F32, dma=True)
            xch = [self.sub(xacc, xacc.ap[:, c, :], f"xacc{c}") for c in range(KC)]
            slotm = self.sb(es, "slots", [128, NSLOT, T], BF16, dma=True)
            slots = [self.sub(slotm, slotm.ap[:, i, :], f"slot{i}") for i in range(NSLOT)]
            o_sl, gl_sl, mix_sl = slots[0:22], slots[22:24], slots[24:24 + KC]
            glsem = self.give_sem(TT(None, "glsem"))
            xn_sl, hid_sl = slots[0:KC], slots[KC:KC + FC]
            sq = [self.sb(es, f"sq{i}", [128, T], BF16) for i in range(2)]
            rstd = self.sb(es, "rstd", [128, T], F32)
            ps_x = self.ps(es, "ps_x", [128, 512])
            gsb = [self.sb(es, f"gsb{i}", [128, T], F32) for i in range(2)]
            tmp = [self.sb(es, f"tmp{i}", [128, T], F32) for i in range(3)]
            mixf = [self.sb(es, f"mixf{i}", [128, T], F32) for i in range(2)]
            st = {"g": 0, "t": 0}

            for tt in range(cfg.NTT):
                ts = slice(tt * T, (tt + 1) * T)
                self.dma("pool", slotm.ap[:, 0:22, :], oTv[:, :, ts], W=o_sl, semt=slotm, split=(22, 8))
                self.dma("pool", slotm.ap[:, 22:24, :], glv[:, :, ts], W=gl_sl, semt=glsem)
                self.dma("pool", xacc.ap, xTv[:, :, ts], W=xch, semt=xacc, split=(KC, 8))
                rhs_gl = [(gl_sl[c], gl_sl[c].ap) for c in range(2)]
                rhs_o = [(o_sl[c], o_sl[c].ap) for c in range(22)]
                for dg in range(NG):
                    for i in range(3):
                        gts = []

                        def epi_gate(pg, i=i, dg=dg, gts=gts):
                            for j in range(2):
                                st["g"] += 1
                                g_ = gsb[st["g"] % 2]
                                bc = po["bg"] + (l * 3 + i) * KC + 2 * dg + j
                                self.op("act", lambda e, g_=g_, p=pg[j], bc=bc: e.activation(
                                    out=g_.ap, in_=p.ap, func=AF.Sigmoid, bias=self.pt.ap[:, bc:bc + 1]),
                                    R=[pg[j], self.pt], W=[g_])
                                gts.append(g_)
                            yield

                        self.gemm_group(io["wb_w_gate_up"][l], i * NG + dg, 0, 2, rhs_gl, epi_gate)

                        def epi_br(pg, i=i, dg=dg, gts=gts):
                            for j in range(2):
                                g_ = gts[j]
                                mf = mixf[j]
                                p = pg[j]
                                if i == 0:
                                    self.op("dve", lambda e, mf=mf, p=p, g_=g_: e.tensor_tensor(
                                        out=mf.ap, in0=p.ap, in1=g_.ap, op=ALU.mult), R=[p, g_], W=[mf])
                                else:
                                    st["t"] += 1
                                    t_ = tmp[st["t"] % 3]
                                    self.op("dve", lambda e, t_=t_, p=p, g_=g_: e.tensor_tensor(
                                        out=t_.ap, in0=p.ap, in1=g_.ap, op=ALU.mult), R=[p, g_], W=[t_])
                                    if i == 1:
                                        self.op("pool", lambda e, mf=mf, t_=t_: e.tensor_tensor(
                                            out=mf.ap, in0=mf.ap, in1=t_.ap, op=ALU.add), R=[mf, t_], W=[mf])
                                    else:
                                        ms = mix_sl[2 * dg + j]
                                        self.op("pool", lambda e, ms=ms, mf=mf, t_=t_: e.tensor_tensor(
                                            out=ms.ap, in0=mf.ap, in1=t_.ap, op=ALU.add), R=[mf, t_], W=[ms])
                            yield

                        k0, k1 = br_kc[i]
                        self.gemm_group(io["wb_w_branch"][l], dg, k0, k1, rhs_o[k0:k1], epi_br)
                self.flush_epi()
                rhs_m = [(mix_sl[c], mix_sl[c].ap) for c in range(KC)]

                def epi_acc(pg, dg):
                    for j in range(2):
                        xc = xch[2 * dg + j]
                        self.op("dve", lambda e, xc=xc, p=pg[j]: e.tensor_tensor(
                            out=xc.ap, in0=p.ap, in1=xc.ap, op=ALU.add), R=[pg[j], xc], W=[xc])
                    yield

                for dg in range(NG):
                    self.gemm_group(io["wb_w_out"][l], dg, 0, KC, rhs_m, lambda pg, dg=dg: epi_acc(pg, dg))
                self.flush_epi()
                self.rmsnorm_tile(xch, xacc.ap, xn_sl, po["g2"] + l * KC, sq, ps_x, rstd)
                rhs_x = [(xn_sl[c], xn_sl[c].ap) for c in range(KC)]
                rhs_h = [(hid_sl[c], hid_sl[c].ap) for c in range(FC)]
                for fb in range(DFF // FB):

                    def epi_h(pg, fg):
                        for j in range(2):
                            st["t"] += 1
                            t_ = tmp[st["t"] % 3]
                            hs = hid_sl[2 * fg + j]
                            self.op("act", lambda e, t_=t_, p=pg[j]: e.activation(out=t_.ap, in_=p.ap, func=AF.Relu),
                                    R=[pg[j]], W=[t_])
                            en = self.alt_eng(("dve", "pool"))
                            self.op(en, lambda e, hs=hs, t_=t_: e.tensor_tensor(
                                out=hs.ap, in0=t_.ap, in1=t_.ap, op=ALU.mult), R=[t_], W=[hs])
                        yield

                    for fg in range(FB // 256):
                        self.gemm_group(io["wb_w_ff1"][l], fb * (FB // 256) + fg, 0, KC, rhs_x,
                                        lambda pg, fg=fg: epi_h(pg, fg))
                    self.flush_epi()
                    for dg in range(NG):
                        self.gemm_group(io["wb_w_ff2"][l], dg, fb * FC, (fb + 1) * FC, rhs_h,
                                        lambda pg, dg=dg: epi_acc(pg, dg))
                    self.flush_epi()
                self.dma("pool", xTv[:, :, ts], xacc.ap, R=xch, semt=xacc, split=(KC, 8))
            self.flush()


def build_program(cfg):
    nc = bass.Bass("TRN2", target_bir_lowering=False)
    D, DFF, S, L = cfg.D, cfg.DFF, cfg.S, cfg.DEPTH
    io = {}

    def ext(name, shape, kind="ExternalInput", dt=F32):
        io[name] = nc.dram_tensor(name, list(shape), dt, kind=kind).ap()

    ext("x", [S, D])
    ext("w_in", [L, D, IN_W])
    ext("w_branch", [L, 2816, D])
    ext("w_gate_up", [L, 256, 3 * D])
    ext("w_out", [L, D, D])
    ext("w_ff1", [L, D, DFF])
    ext("w_ff2", [L, DFF, D])
    ext("consts", [128, NCONST])
    ext("ptab", [128, ptab_offsets(cfg)["n"]])
    ext("rot", [4, 128, S])
    ext("cmask", [128, 768])
    ext("cflag", [128, 16])
    ext("out", [S, D], kind="ExternalOutput")

    def scratch(name, shape, dt):
        io[name] = nc.dram_tensor(name, list(shape), dt).ap()

    scratch("xT", [D, S], F32)
    scratch("projT", [IN_W, S], BF16)
    scratch("kbx", [512, S], BF16)
    scratch("sgc", [1024, S], F32)
    scratch("oT", [2816, S], BF16)
    io["HB"] = [nc.dram_tensor(f"HB{i}", [128, 4096], BF16).ap() for i in range(N_PAGES)]
    io["GB"] = [nc.dram_tensor(f"GB{i}", [cfg.P * 128, 4096], BF16).ap() for i in range(N_PAGES)]
    scratch("SBf", [128, 1024], F32)
    scratch("SG", [cfg.P * 128, 1024], F32)
    for name, K, E in (("w_in", D, IN_W), ("w_branch", 2816, D), ("w_gate_up", 256, 3 * D),
                       ("w_out", D, D), ("w_ff1", D, DFF), ("w_ff2", DFF, D)):
        io["wb_" + name] = [nc.dram_tensor(f"wb_{name}_{l}", [E // 256, 128, K // 128, 256], BF16).ap()
                            for l in range(L)]
    k = Kern(nc, cfg)
    k.run(io)
    return nc


def make_in_maps(cfg, inputs):
    consts = make_consts()
    rot = make_rot(cfg.SEQ)
    ptab = layout_params(cfg, {k: np.asarray(v, np.float32) for k, v in inputs.items()
                               if k in ("norm1_g", "norm2_g", "qn_a", "kn_a", "qn_b", "kn_b", "sinks", "b_gate")})

    def cc(n):
        o, w = CL[n]
        return consts[:, o:o + w]

    S = cfg.S
    maps = []
    for c in range(cfg.NCORE):
        q, pos = c // cfg.P, c % cfg.P
        first = pos == 0
        csel = np.zeros((128, 16), np.float32)
        for j in range(cfg.P - 1):
            if j == pos - 1:
                csel[:, j] = 1.0
            for h in range(4):
                if j < pos:
                    lg = np.log1p(-(2.0 ** (-5.0 - h)))
                    csel[:, 4 + h * 3 + j] = np.float32(np.exp(lg * float(S) * (pos - 1 - j)))
        cmask = np.concatenate([cc("maskAf") if first else cc("maskA"),
                                cc("maskBf") if first else cc("maskB")], 1).astype(np.float32)
        m = {"x": np.ascontiguousarray(np.asarray(inputs["x"][q, pos * S:(pos + 1) * S], np.float32)),
             "consts": consts, "ptab": ptab,
             "rot": np.ascontiguousarray(rot[:, :, pos * S:(pos + 1) * S]),
             "cmask": np.ascontiguousarray(cmask),
             "cflag": csel}
        for n in ("w_in", "w_branch", "w_gate_up", "w_out", "w_ff1", "w_ff2"):
            m[n] = np.asarray(inputs[n], np.float32)
        maps.append(m)
    return maps


def run_cfg(cfg, inputs, trace=False):
    nc = build_program(cfg)
    maps = make_in_maps(cfg, inputs)
    res = run_bass_kernel_spmd(nc, maps, core_ids=list(range(cfg.NCORE)), trace=trace)
    out = np.zeros((cfg.NSEQ, cfg.SEQ, cfg.D), np.float32)
    for c in range(cfg.NCORE):
        q, pos = c // cfg.P, c % cfg.P
        out[q, pos * cfg.S:(pos + 1) * cfg.S] = np.asarray(res.results[c]["out"])
    return out, res


def kernel(**inputs):
    cfg = Cfg()
    out, _ = run_cfg(cfg, inputs)
    return out
```

```python
import contextlib
import numpy as np
import concourse.bass as bass
import concourse.mybir as mybir
from concourse.bass_utils import run_bass_kernel_spmd

F32 = mybir.dt.float32
BF16 = mybir.dt.bfloat16
AF = mybir.ActivationFunctionType
ALU = mybir.AluOpType

EPS = 1e-6
NEG = -30000.0
IN_W = 11520
A_GROUPS = ((128, 1), (512, 4), (2048, 16))
C_QA, C_KA, C_VA = 0, 18, 36
C_QB, C_KB, C_VB = 54, 62, 63
C_QC, C_KC, C_VC, C_GC, C_GL = 64, 68, 72, 80, 88
N_CH = 90


class Cfg:
    def __init__(self, D=4096, DFF=16384, SEQ=8192, DEPTH=4, NSEQ=2, P=4):
        self.D, self.DFF, self.DEPTH = D, DFF, DEPTH
        self.SEQ, self.NSEQ, self.P = SEQ, NSEQ, P
        self.S = SEQ // P
        self.NCORE = NSEQ * P
        self.groups = [[q * P + i for i in range(P)] for q in range(NSEQ)]
        self.KC = D // 128
        self.T = 512
        self.NTT = self.S // 512
        self.FB = min(2048, DFF)
        self.SB = 2048


def halo_item_a(g, kind, h):
    idx = 2 * h + kind
    if g == 2:
        return idx // 2, (idx % 2) * 2048, 2048
    if g == 1:
        return 6 + idx // 8, (idx % 8) * 512, 512
    return 8, idx * 128, 128


def halo_item_b(i):
    return 8, 1536 + i * 128, 128


N_PAGES = 9


def _const_layout():
    names = [("ident", 128), ("ones", 128), ("blk64", 128), ("rmatT", 128),
             ("e0", 128), ("e1", 128), ("e2", 128), ("e3", 128),
             ("ones_lo", 128), ("ones_hi", 128),
             ("maskA", 256), ("maskAf", 256), ("maskB", 512), ("maskBf", 512),
             ("decay0", 128), ("decay1", 128), ("decay2", 128), ("decay3", 128),
             ("qd0", 128), ("qd1", 128), ("qd2", 128), ("qd3", 128), ("kd", 4)]
    off = {}
    o = 0
    for n, w in names:
        off[n] = (o, w)
        o += w
    return off, o


CL, NCONST = _const_layout()
NBF = CL["decay0"][0]


def make_consts():
    c = np.zeros((128, NCONST), np.float32)

    def put(n, a):
        o, w = CL[n]
        c[:, o:o + w] = a

    i = np.arange(128)
    put("ident", np.eye(128, dtype=np.float32))
    put("ones", np.ones((128, 128), np.float32))
    blk = np.zeros((128, 128), np.float32)
    blk[:64, :64] = 1
    blk[64:, 64:] = 1
    put("blk64", blk)
    R = np.zeros((128, 128), np.float32)
    for a in range(64):
        R[2 * a, 2 * a + 1] = -1.0
        R[2 * a + 1, 2 * a] = 1.0
    put("rmatT", R.T.copy())
    e0 = np.zeros((128, 128), np.float32)
    e1 = np.zeros((128, 128), np.float32)
    e2 = np.zeros((128, 128), np.float32)
    e3 = np.zeros((128, 128), np.float32)
    for m in range(64):
        e0[m, m] = 1
        e1[m, m + 64] = 1
        e2[m + 64, m] = 1
        e3[m + 64, m + 64] = 1
    put("e0", e0), put("e1", e1), put("e2", e2), put("e3", e3)
    lo = np.zeros((128, 128), np.float32)
    lo[:, :64] = 1
    hi = np.zeros((128, 128), np.float32)
    hi[:, 64:] = 1
    put("ones_lo", lo), put("ones_hi", hi)
    j = i[:, None]
    q = i[None, :]
    cur = np.where(j <= q, 0.0, NEG).astype(np.float32)
    prevA = np.where(j >= q, 0.0, NEG).astype(np.float32)
    prevB = np.where(j >= q + 1, 0.0, NEG).astype(np.float32)
    negs = np.full((128, 128), NEG, np.float32)
    put("maskA", np.concatenate([prevA, cur], 1))
    put("maskAf", np.concatenate([negs, cur], 1))
    put("maskB", np.concatenate([prevB, cur, prevB, cur], 1))
    put("maskBf", np.concatenate([negs, cur, negs, cur], 1))
    for h in range(4):
        lg = np.log1p(-(2.0 ** (-5.0 - h)))
        rel = (q - j).astype(np.float64)
        dec = np.where(rel >= 0, np.exp(lg * np.maximum(rel, 0)), 0.0)
        put(f"decay{h}", dec.astype(np.float32))
        put(f"qd{h}", np.broadcast_to(np.exp(lg * (i + 1.0))[None, :], (128, 128)).astype(np.float32))
        c[:, CL["kd"][0] + h] = np.exp(lg * (127.0 - i)).astype(np.float32)
    return c


def chunk_decay(h):
    return float(np.exp(np.log1p(-(2.0 ** (-5.0 - h))) * 128.0))


def make_rot(S):
    inv = (1.0 / (np.float32(10000.0) ** np.linspace(0.0, 1.0, 64, dtype=np.float32))).astype(np.float32)
    pos = np.arange(S, dtype=np.float32)
    ang = (pos[:, None] * np.repeat(inv, 2)[None, :]).astype(np.float32)
    cos = np.cos(ang.astype(np.float64)).astype(np.float32).T
    sin = np.sin(ang.astype(np.float64)).astype(np.float32).T
    ksc = np.float32(128 ** -0.5)
    return np.ascontiguousarray(np.stack([cos, sin, cos * ksc, sin * ksc], 0))


def layout_params(cfg, p):
    L, KC, D = cfg.DEPTH, cfg.KC, cfg.D
    g1 = p["norm1_g"].reshape(L, KC, 128).transpose(2, 0, 1).reshape(128, L * KC)
    g2 = p["norm2_g"].reshape(L, KC, 128).transpose(2, 0, 1).reshape(128, L * KC)
    qka = np.stack([p["qn_a"], p["kn_a"]], 1).transpose(2, 0, 1).reshape(128, L * 2)
    qb2 = np.concatenate([p["qn_b"], p["qn_b"]], 1)
    kb2 = np.concatenate([p["kn_b"], p["kn_b"]], 1)
    qkb = np.stack([qb2, kb2], 1).transpose(2, 0, 1).reshape(128, L * 2)
    sk = p["sinks"].reshape(L, 8, 2)
    sinkt = np.repeat(sk.transpose(2, 0, 1), 64, axis=0).reshape(128, L * 8)
    bg = p["b_gate"].reshape(L, 3, KC, 128).transpose(3, 0, 1, 2).reshape(128, L * 3 * KC)
    tab = np.concatenate([g1, g2, qka, qkb, sinkt, bg], 1).astype(np.float32)
    return np.ascontiguousarray(tab)


def ptab_offsets(cfg):
    L, KC = cfg.DEPTH, cfg.KC
    o = {}
    o["g1"] = 0
    o["g2"] = L * KC
    o["qka"] = 2 * L * KC
    o["qkb"] = o["qka"] + 2 * L
    o["sink"] = o["qkb"] + 2 * L
    o["bg"] = o["sink"] + 8 * L
    o["n"] = o["bg"] + 3 * KC * L
    return o


class Eng:
    def __init__(self, name, obj, sem):
        self.name, self.o, self.sem = name, obj, sem
        self.cnt = 0
        self.waited = {}


class TT:
    __slots__ = ("ap", "w", "r", "sem", "dcnt", "name", "excl")

    def __init__(self, ap, name, sem=None):
        self.ap, self.name, self.sem = ap, name, sem
        self.w = None
        self.r = []
        self.dcnt = 0
        self.excl = False


class Kern:
    def __init__(self, nc, cfg):
        self.nc, self.cfg = nc, cfg
        self.E = {}
        for n, o in (("pe", nc.tensor), ("act", nc.scalar), ("dve", nc.vector),
                     ("pool", nc.gpsimd), ("sp", nc.sync)):
            self.E[n] = Eng(n, o, nc.alloc_semaphore(name="sem_" + n))
        self.rec = {n: [] for n in self.E}
        self.pend = []
        self.csem = nc.alloc_semaphore(name="csem")
        self.ccnt = 0
        self.dsems = []
        self.free_recs = {}
        self.phase_recs = []
        self.nsem = 0
        self.alt = 0

    def dsem(self):
        s = self.nc.alloc_semaphore(name=f"dsem{self.nsem}")
        self.nsem += 1
        return s

    def sb(self, es, name, shape, dt, dma=False):
        self.uid = getattr(self, "uid", 0) + 1
        name = f"{name}_u{self.uid}"
        h = es.enter_context(self.nc.sbuf_tensor(name, list(shape), dt))
        t = TT(h[:], name)
        if dma:
            self.give_sem(t)
        return t

    def ps(self, es, name, shape, dt=F32):
        self.uid = getattr(self, "uid", 0) + 1
        name = f"{name}_u{self.uid}"
        full = 512 if dt == F32 else 1024
        h = es.enter_context(self.nc.psum_tensor(name, [128, full], dt))
        n = 1
        for d in shape[1:]:
            n *= d
        ap = h[:, 0:n]
        if len(shape) == 3:
            ap = ap.rearrange("p (a b) -> p a b", a=shape[1])
        t = TT(ap, name)
        t.excl = True
        return t

    def give_sem(self, t):
        t.sem = "lazy"
        return t

    def _bind_sem(self, t, qn):
        fl = self.free_recs.setdefault(qn, [])
        if fl:
            rec = fl.pop()
        else:
            rec = [self.dsem(), 0, qn]
            self.dsems.append(rec)
        self.phase_recs.append(rec)
        t.sem = rec

    def recycle_sems(self):
        for rec in self.phase_recs:
            self.free_recs.setdefault(rec[2], []).append(rec)
        self.phase_recs = []

    def sub(self, t, ap, name=None):
        s = TT(ap, name or t.name)
        s.sem = t.sem
        return s

    def _wait(self, eng, ev):
        key, val, h = ev
        if eng.waited.get(key, 0) >= val:
            return
        if key == "pe" and eng.name == "pe":
            return
        self.pend.append((h, val))
        eng.waited[key] = val

    def _deps(self, eng, R, W):
        for t in R:
            if t.w is not None:
                self._wait(eng, t.w)
            if t.excl:
                for ev in t.r:
                    if ev[0] != eng.name:
                        self._wait(eng, ev)
        for t in W:
            if t.w is not None:
                self._wait(eng, t.w)
            for ev in t.r:
                self._wait(eng, ev)

    def _commit(self, ev, R, W):
        for t in R:
            if len(t.r) > 24:
                last = {}
                for e in t.r:
                    if e[0] not in last or last[e[0]][1] < e[1]:
                        last[e[0]] = e
                t.r = list(last.values())
            t.r.append(ev)
        for t in W:
            t.w = ev
            t.r = []

    def op(self, en, fn, R=(), W=(), inc=True):
        eng = self.E[en]
        self.pend = []
        self._deps(eng, R, W)
        waits = self.pend
        ev = (en, eng.cnt + 1, eng.sem)
        if inc:
            eng.cnt += 1

        def emit(waits=waits, fn=fn, inc=inc, o=eng.o, sem=eng.sem):
            for h, v in waits:
                o.wait_ge(h, v)
            ins = fn(o)
            if inc:
                ins.then_inc(sem, 1)

        self.rec[en].append(emit)
        self._commit(ev, R, W)

    def flush(self):
        rec = self.rec
        self.rec = {n: [] for n in rec}
        if not any(rec.values()):
            return
        with self.nc.Block() as block:
            @block.tensor
            def _(e):
                for f in rec["pe"]:
                    f()

            @block.scalar
            def _(e):
                for f in rec["act"]:
                    f()

            @block.vector
            def _(e):
                for f in rec["dve"]:
                    f()

            @block.gpsimd
            def _(e):
                for f in rec["pool"]:
                    f()

            @block.sync
            def _(e):
                for f in rec["sp"]:
                    f()

    def allgather(self, in_ap, out_ap):
        self.ccnt += 1
        nc, csem, groups = self.nc, self.csem, self.cfg.groups

        def emit():
            nc.gpsimd.collective_compute("AllGather", ALU.bypass, replica_groups=groups,
                                         ins=[in_ap], outs=[out_ap]).then_inc(csem)

        self.rec["pool"].append(emit)

    def dma(self, qn, out_ap, in_ap, R=(), W=(), semt=None, split=None):
        eng = self.E[qn]
        self.pend = []
        self._deps(eng, R, W)
        if semt.sem == "lazy":
            self._bind_sem(semt, qn)
        rec = semt.sem
        assert rec[2] == qn, (semt.name, rec[2], qn)
        pieces = [(out_ap, in_ap)]
        if split is not None:
            n_tot, step = split
            if n_tot > step:
                pieces = [(out_ap[:, a:min(a + step, n_tot)], in_ap[:, a:min(a + step, n_tot)])
                          for a in range(0, n_tot, step)]
        rec[1] += len(pieces)
        waits = self.pend

        def emit(waits=waits, pieces=pieces, o=eng.o, sem=rec[0]):
            for h, v in waits:
                o.wait_ge(h, v)
            for o_, i_ in pieces:
                o.dma_start(out=o_, in_=i_).then_inc(sem, 16)

        self.rec[qn].append(emit)
        ev = (id(rec), 16 * rec[1], rec[0])
        self._commit(ev, R, W)

    def barrier(self):
        for eng in self.E.values():
            self.pend = []
            for e2 in self.E.values():
                if e2.name == "sp" or e2.cnt == 0:
                    continue
                self._wait(eng, (e2.name, e2.cnt, e2.sem))
            for rec in self.dsems:
                if rec[1]:
                    self._wait(eng, (id(rec), 16 * rec[1], rec[0]))
            if self.ccnt:
                self._wait(eng, ("csem", self.ccnt, self.csem))
            waits = self.pend

            def emit(waits=waits, o=eng.o):
                for h, v in waits:
                    o.wait_ge(h, v)

            if waits:
                self.rec[eng.name].append(emit)

    def alt_eng(self, choices=("dve", "act")):
        self.alt += 1
        return choices[self.alt % len(choices)]

    def copy(self, en, out_t, out_ap, in_t, in_ap):
        if en == "act":
            self.op("act", lambda e: e.activation(out=out_ap, in_=in_ap, func=AF.Copy), R=[in_t], W=[out_t])
        else:
            self.op(en, lambda e: e.tensor_copy(out=out_ap, in_=in_ap), R=[in_t], W=[out_t])

    def rstd_from_ssq(self, out_t, out_ap, ps_t, ps_ap, n):
        self.op("act", lambda e: e.activation(out=out_ap, in_=ps_ap, func=AF.Sqrt, scale=1.0 / n,
                                              bias=self.epsb.ap), R=[ps_t, self.epsb], W=[out_t])
        self.op("dve", lambda e: e.reciprocal(out=out_ap, in_=out_ap), R=[out_t], W=[out_t])

    def run(self, io):
        nc, cfg = self.nc, self.cfg
        self.io = io
        with contextlib.ExitStack() as es:
            self.cf = self.sb(es, "cf", [128, NCONST], F32, dma=True)
            self.cb = self.sb(es, "cb", [128, NBF], BF16)
            po = ptab_offsets(cfg)
            self.po = po
            self.pt = self.sb(es, "pt", [128, po["n"]], F32, dma=True)
            self.epsb = self.sb(es, "epsb", [128, 1], F32)
            self.op("pool", lambda e: e.memset(self.epsb.ap, EPS), W=[self.epsb])
            self.dma("sp", self.cf.ap, io["consts"], W=[self.cf], semt=self.cf)
            self.dma("sp", self.pt.ap, io["ptab"], W=[self.pt], semt=self.pt)
            self.op("dve", lambda e: e.tensor_copy(out=self.cb.ap, in_=self.cf.ap[:, 0:NBF]),
                    R=[self.cf], W=[self.cb])
            self.cmf = self.sb(es, "cmf", [128, 768], F32, dma=True)
            self.cmb = self.sb(es, "cmb", [128, 768], BF16)
            self.cflag = self.sb(es, "cflag", [128, 16], F32, dma=True)
            self.dma("sp", self.cmf.ap, io["cmask"], W=[self.cmf], semt=self.cmf)
            self.dma("sp", self.cflag.ap, io["cflag"], W=[self.cflag], semt=self.cflag)
            self.op("dve", lambda e: e.tensor_copy(out=self.cmb.ap, in_=self.cmf.ap), R=[self.cmf], W=[self.cmb])
            if cfg.P > 1:
                zt = self.sb(es, "zt", [128, 2048], BF16, dma=True)
                self.op("pool", lambda e: e.memset(zt.ap, 0.0), W=[zt])
                self.dma("pool", io["HB"][7][:, 2048:4096], zt.ap, R=[zt], semt=zt)
                self.dma("pool", io["HB"][8][:, 2176:4096], zt.ap[:, 0:1920], R=[zt], semt=zt)
            import os
            stop = os.environ.get("KSTOP", "")
            steps = [("w", self.phase_weights), ("xin", self.phase_xin)]
            for l in range(cfg.DEPTH):
                steps += [(f"p1_{l}", lambda l=l: self.phase1(l)), (f"h_{l}", lambda l=l: self.halo_exchange(l)),
                          (f"a_{l}", lambda l=l: self.attn_a(l)),
                          (f"b_{l}", lambda l=l: self.attn_b(l)), (f"c_{l}", lambda l=l: self.attn_c(l)),
                          (f"p34_{l}", lambda l=l: self.phase34(l))]
            steps += [("xout", self.phase_xout)]
            skip = os.environ.get("KSKIP", "").split(",")
            self.barrier()
            self.phase_recs = []
            for name, fn in steps:
                if name not in skip:
                    fn()
                    self.barrier()
                    self.recycle_sems()
                if name == stop:
                    break
            self.flush()

    def cbf(self, n):
        o, w = CL[n]
        return self.cb.ap[:, o:o + w]

    def cff(self, n):
        o, w = CL[n]
        return self.cf.ap[:, o:o + w]

    def phase_weights(self):
        cfg, io = self.cfg, self.io
        NK, PW = 4, 512
        with contextlib.ExitStack() as es:
            sf = [self.sb(es, f"wsf{i}", [128, NK, PW], F32, dma=True) for i in range(3)]
            sbf = [self.sb(es, f"wsb{i}", [128, NK, PW], BF16, dma=True) for i in range(3)]
            n = 0
            for l in range(cfg.DEPTH):
                for name in ("w_in", "w_branch", "w_gate_up", "w_out", "w_ff1", "w_ff2"):
                    W = io[name]
                    Wb = io["wb_" + name][l]
                    K, E = W.shape[1], W.shape[2]
                    KCn = K // 128
                    for k0 in range(0, KCn, NK):
                        nk = min(NK, KCn - k0)
                        for c0 in range(0, E, PW):
                            pw = min(PW, E - c0)
                            a, b = sf[n % 3], sbf[n % 3]
                            self.dma("sp", a.ap[:, 0:nk, 0:pw],
                                     W[l, k0 * 128:(k0 + nk) * 128, c0:c0 + pw].rearrange("(k p) c -> p k c", p=128),
                                     W=[a], semt=a)
                            en = ("dve", "act", "pool")[n % 3]
                            self.copy(en, b, b.ap[:, 0:nk, 0:pw], a, a.ap[:, 0:nk, 0:pw])
                            g0, g1 = c0 // 256, (c0 + pw) // 256
                            for gi in range(g0, g1):
                                self.dma("pool", Wb[gi, :, k0:k0 + nk, :],
                                         b.ap[:, 0:nk, (gi - g0) * 256:(gi - g0 + 1) * 256], R=[b], semt=b)
                            n += 1
            self.flush()

    def phase_xin(self):
        cfg, io = self.cfg, self.io
        KC = cfg.KC
        xTv = io["xT"].rearrange("(c p) s -> p c s", p=128)
        with contextlib.ExitStack() as es:
            xin = [self.sb(es, f"xin{i}", [128, cfg.D], F32, dma=True) for i in range(4)]
            xt = self.sb(es, "xt", [128, KC, 512], F32, dma=True)
            pss = [self.ps(es, f"psx{i}", [128, 512]) for i in range(4)]
            idf = self.cff("ident")
            for tt in range(cfg.NTT):
                for i in range(4):
                    r0 = tt * 512 + i * 128
                    self.dma("pool", xin[i].ap, io["x"][r0:r0 + 128, :], W=[xin[i]], semt=xin[i])
                for c in range(KC):
                    bank = pss[c % 4]
                    for i in range(4):
                        self.op("pe", lambda e, i=i, c=c, bank=bank: e.transpose(
                            bank.ap[:, i * 128:(i + 1) * 128], xin[i].ap[:, c * 128:(c + 1) * 128], idf),
                            R=[xin[i], self.cf], W=[bank], inc=(i == 3))
                    self.copy(self.alt_eng(), xt, xt.ap[:, c, :], bank, bank.ap)
                self.dma("sp", xTv[:, :, tt * 512:(tt + 1) * 512], xt.ap, R=[xt], semt=xt, split=(KC, 8))
            self.flush()

    def phase_xout(self):
        cfg, io = self.cfg, self.io
        KC = cfg.KC
        xTv = io["xT"].rearrange("(c p) s -> p c s", p=128)
        with contextlib.ExitStack() as es:
            xo = [self.sb(es, f"xo{i}", [128, cfg.D], F32, dma=True) for i in range(4)]
            xt = self.sb(es, "xt", [128, KC, 512], F32, dma=True)
            pss = [self.ps(es, f"psx{i}", [128, 512]) for i in range(4)]
            idf = self.cff("ident")
            n = 0
            for tt in range(cfg.NTT):
                self.dma("pool", xt.ap, xTv[:, :, tt * 512:(tt + 1) * 512], W=[xt], semt=xt, split=(KC, 8))
                for i in range(4):
                    for c4 in range(0, KC, 4):
                        nn = min(4, KC - c4)
                        bank = pss[n % 4]
                        n += 1
                        for cc in range(nn):
                            c = c4 + cc
                            self.op("pe", lambda e, i=i, c=c, cc=cc, bank=bank: e.transpose(
                                bank.ap[:, cc * 128:(cc + 1) * 128], xt.ap[:, c, i * 128:(i + 1) * 128], idf),
                                R=[xt, self.cf], W=[bank], inc=(cc == nn - 1))
                        self.copy(self.alt_eng(), xo[i], xo[i].ap[:, c4 * 128:(c4 + nn) * 128], bank,
                                  bank.ap[:, 0:nn * 128])
                    r0 = tt * 512 + i * 128
                    self.dma("sp", io["out"][r0:r0 + 128, :], xo[i].ap, R=[xo[i]], semt=xo[i])
            self.flush()

    def gemm_setup(self, es, nbuf=3):
        self.wbufs = [self.sb(es, f"wbuf{i}", [128, 32, 256], BF16, dma=True) for i in range(nbuf)]
        self.wn = 0
        self.pgroups = [[self.ps(es, f"pg{g}_{j}", [128, 512]) for j in range(2)] for g in range(3)]
        self.pgn = 0
        self.pending = None

    def loadw(self, Wb, g, kc0, kc1):
        wt = self.wbufs[self.wn % len(self.wbufs)]
        self.wn += 1
        nk = kc1 - kc0
        self.dma("sp", wt.ap[:, 0:nk, :], Wb[g, :, kc0:kc1, :], W=[wt], semt=wt)
        return wt

    def gemm_group(self, Wb, g, kc0, kc1, rhs, epi):
        wt = self.loadw(Wb, g, kc0, kc1)
        pg = self.pgroups[self.pgn % 3]
        self.pgn += 1
        nk = kc1 - kc0
        for ki in range(nk):
            rt, rap = rhs[ki]
            for j in range(2):
                self.op("pe", lambda e, j=j, ki=ki, rap=rap: e.matmul(
                    pg[j].ap, wt.ap[:, ki, j * 128:(j + 1) * 128], rap,
                    start=(ki == 0), stop=(ki == nk - 1)),
                    R=[wt, rt], W=[pg[j]], inc=(ki == nk - 1 and j == 1))
        self.flush_epi()
        gen = epi(pg)
        next(gen, None)
        self.pending = gen

    def flush_epi(self):
        if self.pending is not None:
            for _ in self.pending:
                pass
            self.pending = None

    def rmsnorm_tile(self, xa_chunks, xa_ap, out_tiles, gcol0, sq, pss, rstd):
        cfg = self.cfg
        KC = cfg.KC
        ones = self.cbf("ones")
        for c in range(KC):
            s = sq[c % 2]
            self.op("act", lambda e, c=c, s=s: e.activation(out=s.ap, in_=xa_ap[:, c, :], func=AF.Square),
                    R=[xa_chunks[c]], W=[s])
            self.op("pe", lambda e, c=c, s=s: e.matmul(pss.ap, ones, s.ap, start=(c == 0), stop=(c == KC - 1)),
                    R=[s, self.cb], W=[pss], inc=True)
        self.rstd_from_ssq(rstd, rstd.ap, pss, pss.ap, float(cfg.D))
        for c in range(KC):
            o = out_tiles[c]
            self.op("dve", lambda e, c=c, o=o: e.scalar_tensor_tensor(
                out=o.ap, in0=xa_ap[:, c, :], scalar=self.pt.ap[:, gcol0 + c:gcol0 + c + 1], in1=rstd.ap,
                op0=ALU.mult, op1=ALU.mult), R=[xa_chunks[c], rstd, self.pt], W=[o])

    def phase1(self, l):
        cfg, io = self.cfg, self.io
        KC, T = cfg.KC, cfg.T
        po = self.po
        xTv = io["xT"].rearrange("(c p) s -> p c s", p=128)
        projv = io["projT"].rearrange("(c p) s -> c p s", p=128)
        kbxv = io["kbx"].rearrange("(c p) s -> c p s", p=128)
        sgcv = io["sgc"].rearrange("(c p) s -> c p s", p=128)
        Wb = io["wb_w_in"][l]
        with contextlib.ExitStack() as es:
            self.gemm_setup(es)
            xacc = self.sb(es, "xacc", [128, KC, T], F32, dma=True)
            xch = [self.sub(xacc, xacc.ap[:, c, :], f"xacc{c}") for c in range(KC)]
            xnm = self.sb(es, "xn", [128, KC, T], BF16)
            xn = [self.sub(xnm, xnm.ap[:, c, :], f"xn{c}") for c in range(KC)]
            sq = [self.sb(es, f"sq{i}", [128, T], BF16) for i in range(2)]
            rstd = self.sb(es, "rstd", [128, T], F32)
            ps_x = self.ps(es, "ps_x", [128, 512])
            ps_y = self.ps(es, "ps_y", [128, 512])
            rot = self.sb(es, "rot", [128, 4, T], F32, dma=True)
            ob = [self.sb(es, f"ob{i}", [128, T], BF16, dma=True) for i in range(6)]
            of = [self.sb(es, f"of{i}", [128, T], F32, dma=True) for i in range(2)]
            tf = [self.sb(es, f"tf{i}", [128, T], F32) for i in range(2)]
            r2 = [self.sb(es, f"r2{i}", [128, T], F32) for i in range(2)]
            st = {"ob": 0, "of": 0, "tf": 0, "r2": 0, "sq": 0}

            def nxt(lst, k):
                st[k] += 1
                return lst[st[k] % len(lst)]

            for tt in range(cfg.NTT):
                ts = slice(tt * T, (tt + 1) * T)
                self.dma("pool", xacc.ap, xTv[:, :, ts], W=xch, semt=xacc, split=(KC, 8))
                self.dma("pool", rot.ap, io["rot"][:, :, ts].rearrange("f p s -> p f s"), W=[rot], semt=rot)
                self.rmsnorm_tile(xch, xacc.ap, xn, po["g1"] + l * KC, sq, ps_x, rstd)
                rhs = [(xn[c], xn[c].ap) for c in range(KC)]

                def epi(pg, g, ts=ts):
                    todo = []
                    for j in range(2):
                        ch = 2 * g + j
                        p = pg[j]
                        if ch < C_VA or C_QB <= ch < C_VB:
                            s = nxt(sq, "sq")
                            self.op("act", lambda e, s=s, p=p: e.activation(out=s.ap, in_=p.ap, func=AF.Square),
                                    R=[p], W=[s])
                            todo.append(("norm", ch, p, s))
                        elif C_QC <= ch < C_VC:
                            o = nxt(ob, "ob")
                            self.copy("act", o, o.ap, p, p.ap)
                            todo.append(("rot", ch, p, o))
                        elif C_GC <= ch < C_GL:
                            o = nxt(of, "of")
                            self.op("act", lambda e, o=o, p=p: e.activation(out=o.ap, in_=p.ap, func=AF.Silu),
                                    R=[p], W=[o])
                            self.dma("pool", sgcv[ch - C_GC][:, ts], o.ap, R=[o], semt=o)
                        else:
                            o = nxt(ob, "ob")
                            self.copy(self.alt_eng(), o, o.ap, p, p.ap)
                            self.dma("pool", projv[ch][:, ts], o.ap, R=[o], semt=o)
                    yield
                    for kind, ch, p, s in todo:
                        if kind == "norm":
                            isb = ch >= C_QB
                            red = self.cbf("blk64") if isb else self.cbf("ones")
                            self.op("pe", lambda e, s=s, red=red: e.matmul(ps_y.ap, red, s.ap, start=True, stop=True),
                                    R=[s, self.cb], W=[ps_y])
                            r = nxt(r2, "r2")
                            self.rstd_from_ssq(r, r.ap, ps_y, ps_y.ap, 64.0 if isb else 128.0)
                            if isb:
                                gc = po["qkb"] + 2 * l + (1 if ch == C_KB else 0)
                            else:
                                gc = po["qka"] + 2 * l + (1 if ch >= C_KA else 0)
                            o = nxt(ob, "ob")
                            self.op("dve", lambda e, o=o, p=p, r=r, gc=gc: e.scalar_tensor_tensor(
                                out=o.ap, in0=p.ap, scalar=self.pt.ap[:, gc:gc + 1], in1=r.ap,
                                op0=ALU.mult, op1=ALU.mult), R=[p, r, self.pt], W=[o])
                            if ch == C_KB:
                                for q4 in range(4):
                                    self.op("pe", lambda e, q4=q4, o=o: e.matmul(
                                        ps_y.ap, self.cbf(f"e{q4}"), o.ap, start=True, stop=True),
                                        R=[o, self.cb], W=[ps_y])
                                    o2 = nxt(ob, "ob")
                                    self.copy(self.alt_eng(), o2, o2.ap, ps_y, ps_y.ap)
                                    self.dma("pool", kbxv[q4][:, ts], o2.ap, R=[o2], semt=o2)
                            else:
                                self.dma("pool", projv[ch][:, ts], o.ap, R=[o], semt=o)
                        else:
                            isk = ch >= C_KC
                            self.op("pe", lambda e, s=s: e.matmul(ps_y.ap, self.cbf("rmatT"), s.ap, start=True, stop=True),
                                    R=[s, self.cb], W=[ps_y])
                            t1, t2 = nxt(tf, "tf"), nxt(tf, "tf")
                            ci, si = (2, 3) if isk else (0, 1)
                            self.op("dve", lambda e, t1=t1, p=p, ci=ci: e.tensor_tensor(
                                out=t1.ap, in0=p.ap, in1=rot.ap[:, ci, :], op=ALU.mult), R=[p, rot], W=[t1])
                            self.op("dve", lambda e, t2=t2, si=si: e.tensor_tensor(
                                out=t2.ap, in0=ps_y.ap, in1=rot.ap[:, si, :], op=ALU.mult), R=[ps_y, rot], W=[t2])
                            o = nxt(ob, "ob")
                            self.op("pool", lambda e, o=o, t1=t1, t2=t2: e.tensor_tensor(
                                out=o.ap, in0=t1.ap, in1=t2.ap, op=ALU.add), R=[t1, t2], W=[o])
                            self.dma("pool", projv[ch][:, ts], o.ap, R=[o], semt=o)

                for g in range(N_CH // 2):
                    self.gemm_group(Wb, g, 0, KC, rhs, lambda pg, g=g: epi(pg, g))
                self.flush_epi()
            self.flush()

    def phase2(self, l):
        self.attn_a(l)
        self.barrier()
        self.attn_b(l)
        self.barrier()
        self.attn_c(l)

    def halo_select(self, tl, out_ap, gb, c0, halo, hst):
        ncand = self.cfg.P - 1
        for j in range(ncand):
            self.hsn = getattr(self, "hsn", 0) + 1
            st = hst[self.hsn % len(hst)]
            self.dma("pool", st.ap[:, 0:halo], gb[j * 128:(j + 1) * 128, c0:c0 + halo], W=[st], semt=st)
            en = "dve"
            sc = self.cflag.ap[:, j:j + 1]
            if j == 0:
                self.op(en, lambda e, st=st, sc=sc: e.tensor_scalar(
                    out=out_ap, in0=st.ap[:, 0:halo], scalar1=sc, scalar2=None, op0=ALU.mult),
                    R=[st, self.cflag], W=[tl])
            else:
                self.op(en, lambda e, st=st, sc=sc: e.scalar_tensor_tensor(
                    out=out_ap, in0=st.ap[:, 0:halo], scalar=sc, in1=out_ap, op0=ALU.mult, op1=ALU.add),
                    R=[st, self.cflag, tl], W=[tl])

    def halo_exchange(self, l):
        cfg, io = self.cfg, self.io
        if cfg.P == 1:
            return
        S = cfg.S
        projv = io["projT"].rearrange("(c p) s -> c p s", p=128)
        kbxv = io["kbx"].rearrange("(c p) s -> c p s", p=128)
        hs = TT(None, "hsem")
        self.give_sem(hs)
        for g in range(3):
            for h in range(6):
                for kind, base in ((0, C_KA), (1, C_VA)):
                    pg, c0, halo = halo_item_a(g, kind, h)
                    self.dma("pool", io["HB"][pg][:, c0:c0 + halo], projv[base + g * 6 + h][:, S - halo:S], semt=hs)
        for i in range(5):
            pg, c0, halo = halo_item_b(i)
            src = kbxv[i] if i < 4 else projv[C_VB]
            self.dma("pool", io["HB"][pg][:, c0:c0 + halo], src[:, S - halo:S], semt=hs)
        self.barrier()
        for pg in range(N_PAGES):
            self.allgather(io["HB"][pg], io["GB"][pg])
        self.barrier()
        self.flush()

    def attn_a(self, l):
        cfg, io = self.cfg, self.io
        S, SB = cfg.S, cfg.SB
        projv = io["projT"].rearrange("(c p) s -> c p s", p=128)
        oTv = io["oT"].rearrange("(c p) s -> c p s", p=128)
        ident, ones = self.cbf("ident"), self.cbf("ones")
        scale = 128 ** -0.5
        HM = cfg.P > 1
        with contextlib.ExitStack() as es:
            qt = [self.sb(es, f"qt{i}", [128, SB], BF16, dma=True) for i in range(2)]
            kt = [self.sb(es, f"kt{i}", [128, 2 * SB], BF16, dma=True) for i in range(2)]
            vt = [self.sb(es, f"vt{i}", [128, 2 * SB], BF16, dma=True) for i in range(2)]
            acc = [self.sb(es, f"acc{i}", [128, 2, SB], F32) for i in range(2)]
            ost = [self.sb(es, f"ost{i}", [128, SB], BF16, dma=True) for i in range(2)]
            vps = [self.ps(es, f"vps{i}", [128, 256], BF16) for i in range(2)]
            sps = [self.ps(es, f"sps{i}", [128, 256]) for i in range(2)]
            ops = [self.ps(es, f"ops{i}", [128, 2, 128]) for i in range(2)]
            vtok = [self.sb(es, f"vtok{i}", [128, 256], BF16) for i in range(3)]
            pT = [self.sb(es, f"pT{i}", [128, 256], BF16) for i in range(3)]
            hst = [self.sb(es, f"hst{i}", [128, SB], BF16, dma=True) for i in range(4)] if HM else None
            units = [(sbi, h, g) for sbi in range(S // SB) for h in range(6) for g in range(3)]

            def load(i):
                sbi, h, g = units[i]
                w, r = A_GROUPS[g]
                halo = 128 * r
                t0 = sbi * SB
                lo = max(0, t0 - halo)
                off = t0 - lo
                q_, k_, v_ = qt[i % 2], kt[i % 2], vt[i % 2]
                ch = g * 6 + h
                self.dma("pool", q_.ap, projv[C_QA + ch][:, t0:t0 + SB], W=[q_], semt=q_)
                if HM and sbi == 0:
                    for kind, base, tl in ((0, C_KA, k_), (1, C_VA, v_)):
                        pg, c0, _ = halo_item_a(g, kind, h)
                        self.halo_select(tl, tl.ap[:, 0:halo], io["GB"][pg], c0, halo, hst)
                        self.dma("pool", tl.ap[:, halo:halo + SB], projv[base + ch][:, 0:SB], W=[tl], semt=tl)
                else:
                    self.dma("pool", k_.ap[:, 0:off + SB], projv[C_KA + ch][:, lo:t0 + SB], W=[k_], semt=k_)
                    self.dma("pool", v_.ap[:, 0:off + SB], projv[C_VA + ch][:, lo:t0 + SB], W=[v_], semt=v_)

            nb = 0
            load(0)
            for i, (sbi, h, g) in enumerate(units):
                if i + 1 < len(units):
                    load(i + 1)
                w, r = A_GROUPS[g]
                halo = 128 * r
                t0 = sbi * SB
                off = halo if HM else t0 - max(0, t0 - halo)
                q_, k_, v_ = qt[i % 2], kt[i % 2], vt[i % 2]
                nh = sbi * 6 + h
                ac = acc[nh % 2]
                for u in range(SB // halo):
                    for rho in range(r):
                        bq = u * halo + rho
                        has_prev = HM or (t0 + u * halo) > 0
                        bnd = HM and (t0 + u * halo) == 0
                        ext = 127 * r + 1
                        qs = slice(bq, bq + ext, r)
                        cs = slice(off + bq, off + bq + ext, r)
                        prs = slice(off + bq - halo, off + bq - halo + ext, r)
                        vp, sp_, op_ = vps[nb % 2], sps[nb % 2], ops[nb % 2]
                        vk, p_ = vtok[nb % 3], pT[nb % 3]
                        nb += 1
                        if has_prev:
                            self.op("pe", lambda e, vp=vp, v_=v_, prs=prs: e.transpose(
                                vp.ap[:, 0:128], v_.ap[:, prs], ident), R=[v_, self.cb], W=[vp], inc=False)
                        self.op("pe", lambda e, vp=vp, v_=v_, cs=cs: e.transpose(
                            vp.ap[:, 128:256], v_.ap[:, cs], ident), R=[v_, self.cb], W=[vp])
                        c0 = 0 if has_prev else 128
                        self.copy("dve", vk, vk.ap[:, c0:256], vp, vp.ap[:, c0:256])
                        mk = self.cbf("maskA") if has_prev else self.cbf("maskAf")
                        if bnd:
                            mk = self.cmb.ap[:, 0:256]
                        self.op("pe", lambda e, sp_=sp_, mk=mk: e.matmul(sp_.ap, ident, mk, start=True, stop=False),
                                R=[self.cb, self.cmb], W=[sp_], inc=False)
                        if has_prev:
                            self.op("pe", lambda e, sp_=sp_, k_=k_, q_=q_, prs=prs, qs=qs: e.matmul(
                                sp_.ap[:, 0:128], k_.ap[:, prs], q_.ap[:, qs], start=False, stop=False),
                                R=[k_, q_], W=[sp_], inc=False)
                        self.op("pe", lambda e, sp_=sp_, k_=k_, q_=q_, cs=cs, qs=qs: e.matmul(
                            sp_.ap[:, 128:256], k_.ap[:, cs], q_.ap[:, qs], start=False, stop=True),
                            R=[k_, q_], W=[sp_])
                        self.op("act", lambda e, p_=p_, sp_=sp_: e.activation(
                            out=p_.ap, in_=sp_.ap, func=AF.Exp, scale=scale), R=[sp_], W=[p_])
                        if has_prev:
                            self.op("pe", lambda e, op_=op_, vk=vk, p_=p_: e.matmul(
                                op_.ap[:, 0, :], vk.ap[:, 0:128], p_.ap[:, 0:128], start=True, stop=False),
                                R=[vk, p_], W=[op_], inc=False)
                        self.op("pe", lambda e, op_=op_, vk=vk, p_=p_, hp=has_prev: e.matmul(
                            op_.ap[:, 0, :], vk.ap[:, 128:256], p_.ap[:, 128:256], start=(not hp), stop=True),
                            R=[vk, p_], W=[op_], inc=False)
                        if has_prev:
                            self.op("pe", lambda e, op_=op_, p_=p_: e.matmul(
                                op_.ap[:, 1, :], ones, p_.ap[:, 0:128], start=True, stop=False),
                                R=[p_, self.cb], W=[op_], inc=False)
                        self.op("pe", lambda e, op_=op_, p_=p_, hp=has_prev: e.matmul(
                            op_.ap[:, 1, :], ones, p_.ap[:, 128:256], start=(not hp), stop=True),
                            R=[p_, self.cb], W=[op_])
                        if g == 0:
                            self.op("dve", lambda e, ac=ac, op_=op_, qs=qs: e.tensor_copy(
                                out=ac.ap[:, :, qs], in_=op_.ap), R=[op_], W=[ac])
                        else:
                            self.op("dve", lambda e, ac=ac, op_=op_, qs=qs: e.tensor_tensor(
                                out=ac.ap[:, :, qs], in0=ac.ap[:, :, qs], in1=op_.ap, op=ALU.add),
                                R=[op_, ac], W=[ac])
                if g == 2:
                    o_ = ost[nh % 2]
                    self.op("dve", lambda e, ac=ac: e.reciprocal(out=ac.ap[:, 1, :], in_=ac.ap[:, 1, :]), R=[ac], W=[ac])
                    self.op("pool", lambda e, o_=o_, ac=ac: e.tensor_tensor(
                        out=o_.ap, in0=ac.ap[:, 0, :], in1=ac.ap[:, 1, :], op=ALU.mult), R=[ac], W=[o_])
                    self.dma("pool", oTv[h][:, t0:t0 + SB], o_.ap, R=[o_], semt=o_)
            self.flush()

    def attn_b(self, l):
        cfg, io = self.cfg, self.io
        S, SB = cfg.S, cfg.SB
        po = self.po
        projv = io["projT"].rearrange("(c p) s -> c p s", p=128)
        kbxv = io["kbx"].rearrange("(c p) s -> c p s", p=128)
        oTv = io["oT"].rearrange("(c p) s -> c p s", p=128)
        ident = self.cbf("ident")
        scale = 64 ** -0.5
        HM = cfg.P > 1
        NBK = SB // 128
        with contextlib.ExitStack() as es:
            esink = self.sb(es, "esink", [128, 8], F32)
            hst = [self.sb(es, f"hst{i}", [128, 128], BF16, dma=True) for i in range(4)] if HM else None
            self.op("act", lambda e: e.activation(out=esink.ap, in_=self.pt.ap[:, po["sink"] + 8 * l:po["sink"] + 8 * l + 8],
                                                  func=AF.Exp), R=[self.pt], W=[esink])
            qt = [self.sb(es, f"qt{i}", [128, SB], BF16, dma=True) for i in range(2)]
            klo = [self.sb(es, f"klo{i}", [128, 128 + SB], BF16, dma=True) for i in range(2)]
            khi = [self.sb(es, f"khi{i}", [128, 128 + SB], BF16, dma=True) for i in range(2)]
            vt = [self.sb(es, f"vt{i}", [128, 128 + SB], BF16, dma=True) for i in range(2)]
            vlo = [self.sb(es, f"vlo{i}", [128, NBK + 1, 128], BF16) for i in range(2)]
            vhi = [self.sb(es, f"vhi{i}", [128, NBK + 1, 128], BF16) for i in range(2)]
            for t in vlo + vhi:
                self.op("pool", lambda e, t=t: e.memset(t.ap, 0.0), W=[t])
            ost = [self.sb(es, f"ost{i}", [128, SB], BF16, dma=True) for i in range(2)]
            vps = [self.ps(es, f"vps{i}", [128, 128], BF16) for i in range(2)]
            sps = [self.ps(es, f"sps{i}", [128, 512]) for i in range(2)]
            ops = [self.ps(es, f"ops{i}", [128, 2, 128]) for i in range(2)]
            pT = [self.sb(es, f"pT{i}", [128, 512], BF16) for i in range(3)]
            tden = [self.sb(es, f"tden{i}", [128, 128], F32) for i in range(2)]
            nkv = 0
            nq = 0
            nb = 0
            for sbi in range(S // SB):
                t0 = sbi * SB
                lo = max(0, t0 - 128)
                off = 128 if HM else t0 - lo
                for kv in range(2):
                    kl, kh, v_ = klo[nkv % 2], khi[nkv % 2], vt[nkv % 2]
                    vl, vh = vlo[nkv % 2], vhi[nkv % 2]
                    nkv += 1
                    if HM and sbi == 0:
                        for tl, it, src in ((kl, 2 * kv, kbxv[2 * kv]), (kh, 2 * kv + 1, kbxv[2 * kv + 1]),
                                            (v_, 4, projv[C_VB])):
                            pg, c0, _ = halo_item_b(it)
                            self.halo_select(tl, tl.ap[:, 0:128], io["GB"][pg], c0, 128, hst)
                            self.dma("pool", tl.ap[:, 128:128 + SB], src[:, 0:SB], W=[tl], semt=tl)
                    else:
                        self.dma("pool", kl.ap[:, 0:off + SB], kbxv[2 * kv][:, lo:t0 + SB], W=[kl], semt=kl)
                        self.dma("pool", kh.ap[:, 0:off + SB], kbxv[2 * kv + 1][:, lo:t0 + SB], W=[kh], semt=kh)
                        self.dma("pool", v_.ap[:, 0:off + SB], projv[C_VB][:, lo:t0 + SB], W=[v_], semt=v_)
                    nblk = (off + SB) // 128
                    for b in range(nblk):
                        vp = vps[b % 2]
                        self.op("pe", lambda e, vp=vp, v_=v_, b=b: e.transpose(
                            vp.ap, v_.ap[:, b * 128:(b + 1) * 128], ident), R=[v_, self.cb], W=[vp])
                        self.copy("dve", vl, vl.ap[:, b, 0:64], vp, vp.ap[:, kv * 64:(kv + 1) * 64])
                        self.copy("act", vh, vh.ap[:, b, 64:128], vp, vp.ap[:, kv * 64:(kv + 1) * 64])
                    for j in range(4):
                        jj = kv * 4 + j
                        q_ = qt[nq % 2]
                        o_ = ost[nq % 2]
                        nq += 1
                        self.dma("pool", q_.ap, projv[C_QB + jj][:, t0:t0 + SB], W=[q_], semt=q_)
                        for b in range(NBK):
                            has_prev = HM or (t0 + b * 128) > 0
                            bnd = HM and (t0 + b * 128) == 0
                            cb_ = off // 128 + b
                            qs = slice(b * 128, (b + 1) * 128)
                            cs = slice(cb_ * 128, (cb_ + 1) * 128)
                            prs = slice((cb_ - 1) * 128, cb_ * 128)
                            sp_, op_, p_ = sps[nb % 2], ops[nb % 2], pT[nb % 3]
                            td = tden[nb % 2]
                            nb += 1
                            mk = self.cbf("maskB") if has_prev else self.cbf("maskBf")
                            if bnd:
                                mk = self.cmb.ap[:, 256:768]
                            self.op("pe", lambda e, sp_=sp_, mk=mk: e.matmul(sp_.ap, ident, mk, start=True, stop=False),
                                    R=[self.cb, self.cmb], W=[sp_], inc=False)
                            for hh, kk in enumerate((kl, kh)):
                                if has_prev:
                                    self.op("pe", lambda e, sp_=sp_, kk=kk, hh=hh, q_=q_, prs=prs, qs=qs: e.matmul(
                                        sp_.ap[:, hh * 256:hh * 256 + 128], kk.ap[:, prs], q_.ap[:, qs],
                                        start=False, stop=False), R=[kk, q_], W=[sp_], inc=False)
                                self.op("pe", lambda e, sp_=sp_, kk=kk, hh=hh, q_=q_, cs=cs, qs=qs: e.matmul(
                                    sp_.ap[:, hh * 256 + 128:hh * 256 + 256], kk.ap[:, cs], q_.ap[:, qs],
                                    start=False, stop=(hh == 1)), R=[kk, q_], W=[sp_], inc=(hh == 1))
                            self.op("act", lambda e, p_=p_, sp_=sp_: e.activation(
                                out=p_.ap, in_=sp_.ap, func=AF.Exp, scale=scale), R=[sp_], W=[p_])
                            terms = []
                            for hh, (vv, on) in enumerate(((vl, "ones_lo"), (vh, "ones_hi"))):
                                if has_prev:
                                    terms.append((vv, vv.ap[:, cb_ - 1, :], on, hh * 256))
                                terms.append((vv, vv.ap[:, cb_, :], on, hh * 256 + 128))
                            nt = len(terms)
                            for ti, (vv, vap, on, pc) in enumerate(terms):
                                self.op("pe", lambda e, op_=op_, vap=vap, p_=p_, pc=pc, ti=ti, nt=nt: e.matmul(
                                    op_.ap[:, 0, :], vap, p_.ap[:, pc:pc + 128], start=(ti == 0), stop=(ti == nt - 1)),
                                    R=[vv, p_], W=[op_], inc=False)
                            for ti, (vv, vap, on, pc) in enumerate(terms):
                                self.op("pe", lambda e, op_=op_, on=on, p_=p_, pc=pc, ti=ti, nt=nt: e.matmul(
                                    op_.ap[:, 1, :], self.cbf(on), p_.ap[:, pc:pc + 128], start=(ti == 0),
                                    stop=(ti == nt - 1)), R=[p_, self.cb], W=[op_], inc=(ti == nt - 1))
                            self.op("dve", lambda e, td=td, op_=op_, jj=jj: e.tensor_scalar(
                                out=td.ap, in0=op_.ap[:, 1, :], scalar1=esink.ap[:, jj:jj + 1], scalar2=None,
                                op0=ALU.add), R=[op_, esink], W=[td])
                            self.op("dve", lambda e, td=td: e.reciprocal(out=td.ap, in_=td.ap), R=[td], W=[td])
                            self.op("dve", lambda e, o_=o_, op_=op_, td=td, qs=qs: e.tensor_tensor(
                                out=o_.ap[:, qs], in0=op_.ap[:, 0, :], in1=td.ap, op=ALU.mult),
                                R=[op_, td], W=[o_])
                        self.dma("pool", oTv[6 + jj][:, t0:t0 + SB], o_.ap, R=[o_], semt=o_)
            self.flush()

    def attn_c(self, l):
        cfg, io = self.cfg, self.io
        S = cfg.S
        CS = 512
        projv = io["projT"].rearrange("(c p) s -> c p s", p=128)
        sgcv = io["sgc"].rearrange("(c p) s -> c p s", p=128)
        oTv = io["oT"].rearrange("(c p) s -> c p s", p=128)
        ident, ones = self.cbf("ident"), self.cbf("ones")
        kdo = CL["kd"][0]
        import os
        lvl = int(os.environ.get("KC_LEVEL", "9"))
        with contextlib.ExitStack() as es:
            qkv = [[self.sb(es, f"qkv{i}_{h}", [128, 4, CS], BF16, dma=True) for h in range(4)] for i in range(2)]
            qsub = [[(self.give_sem(self.sub(qkv[i][h], qkv[i][h].ap[:, 0, :], "cq")),
                      self.give_sem(self.sub(qkv[i][h], qkv[i][h].ap[:, 1, :], "ck")),
                      self.give_sem(self.sub(qkv[i][h], qkv[i][h].ap[:, 2:4, :], "cv"))) for h in range(4)]
                    for i in range(2)]
            gt = [[self.sb(es, f"gt{i}_{h}", [128, 2, CS], F32, dma=True) for h in range(4)] for i in range(2)]
            state = [self.sb(es, f"state{h}", [128, 256], F32) for h in range(4)]
            sbf = [self.sb(es, f"sbf{h}", [128, 256], BF16) for h in range(4)]
            for h in range(4):
                self.op("pool", lambda e, h=h: e.memset(state[h].ap, 0.0), W=[state[h]])
                self.op("pool", lambda e, h=h: e.memset(sbf[h].ap, 0.0), W=[sbf[h]])
            ost = [self.sb(es, f"ost{i}", [128, 2, CS], BF16, dma=True) for i in range(8)]
            pa = [self.ps(es, f"pa{i}", [128, 384], BF16) for i in range(2)]
            pb = [self.ps(es, f"pb{i}", [128, 384]) for i in range(2)]
            pc = [self.ps(es, f"pc{i}", [128, 512]) for i in range(2)]
            vtok = [self.sb(es, f"vtok{i}", [128, 256], BF16) for i in range(2)]
            kdec = [self.sb(es, f"kdec{i}", [128, 128], BF16) for i in range(2)]
            inb = [self.sb(es, f"inb{i}", [128, 128], BF16) for i in range(2)]
            qdec = [self.sb(es, f"qdec{i}", [128, 128], BF16) for i in range(2)]
            sqc = [self.sb(es, f"sqc{i}", [128, 256], BF16) for i in range(2)]
            rs = [self.sb(es, f"rs{i}", [128, 256], F32) for i in range(2)]
            tm = [self.sb(es, f"tm{i}", [128, 256], F32) for i in range(2)]
            if cfg.P > 1:
                n1 = 0
                for ci in range(S // CS):
                    t0 = ci * CS
                    for h in range(4):
                        _, tk, tv = qsub[ci % 2][h]
                        self.dma("pool", tk.ap, projv[C_KC + h][:, t0:t0 + CS], W=[tk], semt=tk)
                        self.dma("pool", tv.ap, io["projT"][(C_VC + 2 * h) * 128:(C_VC + 2 * h + 2) * 128, t0:t0 + CS]
                                 .rearrange("(e p) s -> p e s", p=128), W=[tv], semt=tv)
                    for c in range(CS // 128):
                        cs = slice(c * 128, (c + 1) * 128)
                        for h in range(4):
                            b = qkv[ci % 2][h]
                            _, tk, tv = qsub[ci % 2][h]
                            A, C = pa[n1 % 2], pc[n1 % 2]
                            vk, kd_ = vtok[n1 % 2], kdec[n1 % 2]
                            n1 += 1
                            gam = chunk_decay(h)
                            for e_ in range(2):
                                self.op("pe", lambda e, A=A, b=b, e_=e_, cs=cs: e.transpose(
                                    A.ap[:, e_ * 128:(e_ + 1) * 128], b.ap[:, 2 + e_, cs], ident),
                                    R=[tv, self.cb], W=[A], inc=False)
                            self.op("pe", lambda e, A=A, b=b, cs=cs: e.transpose(A.ap[:, 256:384], b.ap[:, 1, cs], ident),
                                    R=[tk, self.cb], W=[A])
                            self.copy("act", vk, vk.ap, A, A.ap[:, 0:256])
                            self.op("dve", lambda e, kd_=kd_, A=A, h=h: e.tensor_scalar(
                                out=kd_.ap, in0=A.ap[:, 256:384], scalar1=self.cf.ap[:, kdo + h:kdo + h + 1],
                                scalar2=None, op0=ALU.mult), R=[A, self.cf], W=[kd_])
                            self.op("pe", lambda e, C=C, kd_=kd_, vk=vk: e.matmul(C.ap[:, 256:512], kd_.ap, vk.ap,
                                                                               start=True, stop=True), R=[kd_, vk], W=[C])
                            self.op("dve", lambda e, h=h, C=C, gam=gam: e.scalar_tensor_tensor(
                                out=state[h].ap, in0=state[h].ap, scalar=gam, in1=C.ap[:, 256:512],
                                op0=ALU.mult, op1=ALU.add), R=[state[h], C], W=[state[h]])
                stx = self.sb(es, "stx", [128, 1024], F32, dma=True)
                for h in range(4):
                    self.copy("dve", stx, stx.ap[:, h * 256:(h + 1) * 256], state[h], state[h].ap)
                self.dma("pool", io["SBf"], stx.ap, R=[stx], semt=stx)
                self.barrier()
                self.allgather(io["SBf"], io["SG"])
                self.barrier()
                for j in range(cfg.P - 1):
                    self.dma("pool", stx.ap, io["SG"][j * 128:(j + 1) * 128, :], W=[stx], semt=stx)
                    for h in range(4):
                        sc = self.cflag.ap[:, 4 + h * 3 + j:5 + h * 3 + j]
                        if j == 0:
                            self.op("dve", lambda e, h=h, sc=sc: e.tensor_scalar(
                                out=state[h].ap, in0=stx.ap[:, h * 256:(h + 1) * 256], scalar1=sc,
                                scalar2=None, op0=ALU.mult), R=[stx, self.cflag], W=[state[h]])
                        else:
                            self.op("dve", lambda e, h=h, sc=sc: e.scalar_tensor_tensor(
                                out=state[h].ap, in0=stx.ap[:, h * 256:(h + 1) * 256], scalar=sc, in1=state[h].ap,
                                op0=ALU.mult, op1=ALU.add), R=[stx, self.cflag, state[h]], W=[state[h]])
                for h in range(4):
                    self.copy("act", sbf[h], sbf[h].ap, state[h], state[h].ap)
            n = 0
            for ci in range(S // CS):
                t0 = ci * CS
                bufs = qkv[ci % 2]
                gts = gt[ci % 2]
                for h in range(4):
                    b = bufs[h]
                    tq, tk, tv = qsub[ci % 2][h]
                    self.dma("pool", tq.ap, projv[C_QC + h][:, t0:t0 + CS], W=[tq], semt=tq)
                    self.dma("pool", tk.ap, projv[C_KC + h][:, t0:t0 + CS], W=[tk], semt=tk)
                    self.dma("pool", tv.ap, io["projT"][(C_VC + 2 * h) * 128:(C_VC + 2 * h + 2) * 128, t0:t0 + CS]
                             .rearrange("(e p) s -> p e s", p=128), W=[tv], semt=tv)
                    self.dma("pool", gts[h].ap, io["sgc"][2 * h * 128:(2 * h + 2) * 128, t0:t0 + CS]
                             .rearrange("(e p) s -> p e s", p=128), W=[gts[h]], semt=gts[h])
                outs = [ost[(ci % 2) * 4 + h] for h in range(4)]
                for c in range(CS // 128):
                    cs = slice(c * 128, (c + 1) * 128)
                    for h in range(4):
                        b = bufs[h]
                        tq, tk, tv = qsub[ci % 2][h]
                        A, B, C = pa[n % 2], pb[n % 2], pc[n % 2]
                        vk, kd_, ib, qd_, sq_, r_, t_ = (vtok[n % 2], kdec[n % 2], inb[n % 2], qdec[n % 2],
                                                        sqc[n % 2], rs[n % 2], tm[n % 2])
                        n += 1
                        gam = chunk_decay(h)
                        if lvl < 1:
                            continue
                        self.op("pool", lambda e, qd_=qd_, b=b, cs=cs, h=h: e.tensor_tensor(
                            out=qd_.ap, in0=b.ap[:, 0, cs], in1=self.cff(f"qd{h}"), op=ALU.mult),
                            R=[tq, self.cf], W=[qd_])
                        if lvl < 2:
                            continue
                        for e_ in range(2):
                            self.op("pe", lambda e, A=A, b=b, e_=e_, cs=cs: e.transpose(
                                A.ap[:, e_ * 128:(e_ + 1) * 128], b.ap[:, 2 + e_, cs], ident),
                                R=[tv, self.cb], W=[A], inc=False)
                        self.op("pe", lambda e, A=A, b=b, cs=cs: e.transpose(A.ap[:, 256:384], b.ap[:, 1, cs], ident),
                                R=[tk, self.cb], W=[A])
                        self.copy("act", vk, vk.ap, A, A.ap[:, 0:256])
                        self.op("dve", lambda e, kd_=kd_, A=A, h=h: e.tensor_scalar(
                            out=kd_.ap, in0=A.ap[:, 256:384], scalar1=self.cf.ap[:, kdo + h:kdo + h + 1], scalar2=None,
                            op0=ALU.mult), R=[A, self.cf], W=[kd_])
                        if lvl < 3:
                            continue
                        self.op("pe", lambda e, B=B, b=b, cs=cs: e.matmul(B.ap[:, 0:128], b.ap[:, 1, cs], b.ap[:, 0, cs],
                                                                       start=True, stop=True), R=[tq, tk], W=[B])
                        self.op("dve", lambda e, ib=ib, B=B, h=h: e.tensor_tensor(
                            out=ib.ap, in0=B.ap[:, 0:128], in1=self.cff(f"decay{h}"), op=ALU.mult),
                            R=[B, self.cf], W=[ib])
                        if lvl < 4:
                            continue
                        for e_ in range(2):
                            self.op("pe", lambda e, C=C, vk=vk, ib=ib, e_=e_: e.matmul(
                                C.ap[:, e_ * 128:(e_ + 1) * 128], vk.ap[:, e_ * 128:(e_ + 1) * 128], ib.ap,
                                start=True, stop=False), R=[vk, ib], W=[C], inc=False)
                            self.op("pe", lambda e, C=C, qd_=qd_, e_=e_, h=h: e.matmul(
                                C.ap[:, e_ * 128:(e_ + 1) * 128], sbf[h].ap[:, e_ * 128:(e_ + 1) * 128], qd_.ap,
                                start=False, stop=True), R=[sbf[h], qd_], W=[C], inc=False)
                        if lvl < 5:
                            continue
                        self.op("pe", lambda e, C=C, kd_=kd_, vk=vk: e.matmul(C.ap[:, 256:512], kd_.ap, vk.ap,
                                                                           start=True, stop=True), R=[kd_, vk], W=[C])
                        self.op("dve", lambda e, h=h, C=C, gam=gam: e.scalar_tensor_tensor(
                            out=state[h].ap, in0=state[h].ap, scalar=gam, in1=C.ap[:, 256:512],
                            op0=ALU.mult, op1=ALU.add), R=[state[h], C], W=[state[h]])
                        self.copy("act", sbf[h], sbf[h].ap, state[h], state[h].ap)
                        if lvl < 6:
                            continue
                        self.op("act", lambda e, sq_=sq_, C=C: e.activation(out=sq_.ap, in_=C.ap[:, 0:256], func=AF.Square),
                                R=[C], W=[sq_])
                        for half in range(2):
                            for e_ in range(2):
                                self.op("pe", lambda e, B=B, sq_=sq_, half=half, e_=e_: e.matmul(
                                    B.ap[:, 128 + half * 128:256 + half * 128], ones, sq_.ap[:, e_ * 128:(e_ + 1) * 128],
                                    start=(e_ == 0), stop=(e_ == 1)), R=[sq_, self.cb], W=[B],
                                    inc=(half == 1 and e_ == 1))
                        self.rstd_from_ssq(r_, r_.ap, B, B.ap[:, 128:384], 256.0)
                        self.op("dve", lambda e, t_=t_, C=C, r_=r_: e.tensor_tensor(
                            out=t_.ap, in0=C.ap[:, 0:256], in1=r_.ap, op=ALU.mult), R=[C, r_], W=[t_])
                        o_ = outs[h]
                        self.op("pool", lambda e, o_=o_, t_=t_, cs=cs, h=h, gts=gts: e.tensor_tensor(
                            out=o_.ap[:, :, cs], in0=t_.ap.rearrange("p (e s) -> p e s", e=2), in1=gts[h].ap[:, :, cs],
                            op=ALU.mult), R=[t_, gts[h]], W=[o_])
                for h in range(4):
                    self.dma("pool", io["oT"][(14 + 2 * h) * 128:(16 + 2 * h) * 128, t0:t0 + CS]
                             .rearrange("(e p) s -> p e s", p=128), outs[h].ap, R=[outs[h]], semt=outs[h])
            self.flush()

    def phase34(self, l):
        cfg, io = self.cfg, self.io
        KC, T, D, DFF, FB = cfg.KC, cfg.T, cfg.D, cfg.DFF, cfg.FB
        po = self.po
        xTv = io["xT"].rearrange("(c p) s -> p c s", p=128)
        oTv = io["oT"].rearrange("(c p) s -> p c s", p=128)
        glv = io["projT"][C_GL * 128:(C_GL + 2) * 128, :].rearrange("(c p) s -> p c s", p=128)
        NG = D // 256
        FC = FB // 128
        NSLOT = max(22 + 2 + KC, KC + FC)
        br_kc = ((0, 6), (6, 14), (14, 22))
        with contextlib.ExitStack() as es:
            self.gemm_setup(es, 2)
            xacc = self.sb(es, "xacc", [128, KC, T], F32, dma=True)
            xch = [self.sub(xacc, xacc.ap[:, c, :], f"xacc{c}") for c in range(KC)]
            slotm = self.sb(es, "slots", [128, NSLOT, T], BF16, dma=True)
            slots = [self.sub(slotm, slotm.ap[:, i, :], f"slot{i}") for i in range(NSLOT)]
            o_sl, gl_sl, mix_sl = slots[0:22], slots[22:24], slots[24:24 + KC]
            glsem = self.give_sem(TT(None, "glsem"))
            xn_sl, hid_sl = slots[0:KC], slots[KC:KC + FC]
            sq = [self.sb(es, f"sq{i}", [128, T], BF16) for i in range(2)]
            rstd = self.sb(es, "rstd", [128, T], F32)
            ps_x = self.ps(es, "ps_x", [128, 512])
            gsb = [self.sb(es, f"gsb{i}", [128, T], F32) for i in range(2)]
            tmp = [self.sb(es, f"tmp{i}", [128, T], F32) for i in range(3)]
            mixf = [self.sb(es, f"mixf{i}", [128, T], F32) for i in range(2)]
            st = {"g": 0, "t": 0}

            for tt in range(cfg.NTT):
                ts = slice(tt * T, (tt + 1) * T)
                self.dma("pool", slotm.ap[:, 0:22, :], oTv[:, :, ts], W=o_sl, semt=slotm, split=(22, 8))
                self.dma("pool", slotm.ap[:, 22:24, :], glv[:, :, ts], W=gl_sl, semt=glsem)
                self.dma("pool", xacc.ap, xTv[:, :, ts], W=xch, semt=xacc, split=(KC, 8))
                rhs_gl = [(gl_sl[c], gl_sl[c].ap) for c in range(2)]
                rhs_o = [(o_sl[c], o_sl[c].ap) for c in range(22)]
                for dg in range(NG):
                    for i in range(3):
                        gts = []

                        def epi_gate(pg, i=i, dg=dg, gts=gts):
                            for j in range(2):
                                st["g"] += 1
                                g_ = gsb[st["g"] % 2]
                                bc = po["bg"] + (l * 3 + i) * KC + 2 * dg + j
                                self.op("act", lambda e, g_=g_, p=pg[j], bc=bc: e.activation(
                                    out=g_.ap, in_=p.ap, func=AF.Sigmoid, bias=self.pt.ap[:, bc:bc + 1]),
                                    R=[pg[j], self.pt], W=[g_])
                                gts.append(g_)
                            yield

                        self.gemm_group(io["wb_w_gate_up"][l], i * NG + dg, 0, 2, rhs_gl, epi_gate)

                        def epi_br(pg, i=i, dg=dg, gts=gts):
                            for j in range(2):
                                g_ = gts[j]
                                mf = mixf[j]
                                p = pg[j]
                                if i == 0:
                                    self.op("dve", lambda e, mf=mf, p=p, g_=g_: e.tensor_tensor(
                                        out=mf.ap, in0=p.ap, in1=g_.ap, op=ALU.mult), R=[p, g_], W=[mf])
                                else:
                                    st["t"] += 1
                                    t_ = tmp[st["t"] % 3]
                                    self.op("dve", lambda e, t_=t_, p=p, g_=g_: e.tensor_tensor(
                                        out=t_.ap, in0=p.ap, in1=g_.ap, op=ALU.mult), R=[p, g_], W=[t_])
                                    if i == 1:
                                        self.op("pool", lambda e, mf=mf, t_=t_: e.tensor_tensor(
                                            out=mf.ap, in0=mf.ap, in1=t_.ap, op=ALU.add), R=[mf, t_], W=[mf])
                                    else:
                                        ms = mix_sl[2 * dg + j]
                                        self.op("pool", lambda e, ms=ms, mf=mf, t_=t_: e.tensor_tensor(
                                            out=ms.ap, in0=mf.ap, in1=t_.ap, op=ALU.add), R=[mf, t_], W=[ms])
                            yield

                        k0, k1 = br_kc[i]
                        self.gemm_group(io["wb_w_branch"][l], dg, k0, k1, rhs_o[k0:k1], epi_br)
                self.flush_epi()
                rhs_m = [(mix_sl[c], mix_sl[c].ap) for c in range(KC)]

                def epi_acc(pg, dg):
                    for j in range(2):
                        xc = xch[2 * dg + j]
                        self.op("dve", lambda e, xc=xc, p=pg[j]: e.tensor_tensor(
                            out=xc.ap, in0=p.ap, in1=xc.ap, op=ALU.add), R=[pg[j], xc], W=[xc])
                    yield

                for dg in range(NG):
                    self.gemm_group(io["wb_w_out"][l], dg, 0, KC, rhs_m, lambda pg, dg=dg: epi_acc(pg, dg))
                self.flush_epi()
                self.rmsnorm_tile(xch, xacc.ap, xn_sl, po["g2"] + l * KC, sq, ps_x, rstd)
                rhs_x = [(xn_sl[c], xn_sl[c].ap) for c in range(KC)]
                rhs_h = [(hid_sl[c], hid_sl[c].ap) for c in range(FC)]
                for fb in range(DFF // FB):

                    def epi_h(pg, fg):
                        for j in range(2):
                            st["t"] += 1
                            t_ = tmp[st["t"] % 3]
                            hs = hid_sl[2 * fg + j]
                            self.op("act", lambda e, t_=t_, p=pg[j]: e.activation(out=t_.ap, in_=p.ap, func=AF.Relu),
                                    R=[pg[j]], W=[t_])
                            en = self.alt_eng(("dve", "pool"))
                            self.op(en, lambda e, hs=hs, t_=t_: e.tensor_tensor(
                                out=hs.ap, in0=t_.ap, in1=t_.ap, op=ALU.mult), R=[t_], W=[hs])
                        yield

                    for fg in range(FB // 256):
                        self.gemm_group(io["wb_w_ff1"][l], fb * (FB // 256) + fg, 0, KC, rhs_x,
                                        lambda pg, fg=fg: epi_h(pg, fg))
                    self.flush_epi()
                    for dg in range(NG):
                        self.gemm_group(io["wb_w_ff2"][l], dg, fb * FC, (fb + 1) * FC, rhs_h,
                                        lambda pg, dg=dg: epi_acc(pg, dg))
                    self.flush_epi()
                self.dma("pool", xTv[:, :, ts], xacc.ap, R=xch, semt=xacc, split=(KC, 8))
            self.flush()


def build_program(cfg):
    nc = bass.Bass("TRN2", target_bir_lowering=False)
    D, DFF, S, L = cfg.D, cfg.DFF, cfg.S, cfg.DEPTH
    io = {}

    def ext(name, shape, kind="ExternalInput", dt=F32):
        io[name] = nc.dram_tensor(name, list(shape), dt, kind=kind).ap()

    ext("x", [S, D])
    ext("w_in", [L, D, IN_W])
    ext("w_branch", [L, 2816, D])
    ext("w_gate_up", [L, 256, 3 * D])
    ext("w_out", [L, D, D])
    ext("w_ff1", [L, D, DFF])
    ext("w_ff2", [L, DFF, D])
    ext("consts", [128, NCONST])
    ext("ptab", [128, ptab_offsets(cfg)["n"]])
    ext("rot", [4, 128, S])
    ext("cmask", [128, 768])
    ext("cflag", [128, 16])
    ext("out", [S, D], kind="ExternalOutput")

    def scratch(name, shape, dt):
        io[name] = nc.dram_tensor(name, list(shape), dt).ap()

    scratch("xT", [D, S], F32)
    scratch("projT", [IN_W, S], BF16)
    scratch("kbx", [512, S], BF16)
    scratch("sgc", [1024, S], F32)
    scratch("oT", [2816, S], BF16)
    io["HB"] = [nc.dram_tensor(f"HB{i}", [128, 4096], BF16).ap() for i in range(N_PAGES)]
    io["GB"] = [nc.dram_tensor(f"GB{i}", [cfg.P * 128, 4096], BF16).ap() for i in range(N_PAGES)]
    scratch("SBf", [128, 1024], F32)
    scratch("SG", [cfg.P * 128, 1024], F32)
    for name, K, E in (("w_in", D, IN_W), ("w_branch", 2816, D), ("w_gate_up", 256, 3 * D),
                       ("w_out", D, D), ("w_ff1", D, DFF), ("w_ff2", DFF, D)):
        io["wb_" + name] = [nc.dram_tensor(f"wb_{name}_{l}", [E // 256, 128, K // 128, 256], BF16).ap()
                            for l in range(L)]
    k = Kern(nc, cfg)
    k.run(io)
    return nc


def make_in_maps(cfg, inputs):
    consts = make_consts()
    rot = make_rot(cfg.SEQ)
    ptab = layout_params(cfg, {k: np.asarray(v, np.float32) for k, v in inputs.items()
                               if k in ("norm1_g", "norm2_g", "qn_a", "kn_a", "qn_b", "kn_b", "sinks", "b_gate")})

    def cc(n):
        o, w = CL[n]
        return consts[:, o:o + w]

    S = cfg.S
    maps = []
    for c in range(cfg.NCORE):
        q, pos = c // cfg.P, c % cfg.P
        first = pos == 0
        csel = np.zeros((128, 16), np.float32)
        for j in range(cfg.P - 1):
            if j == pos - 1:
                csel[:, j] = 1.0
            for h in range(4):
                if j < pos:
                    lg = np.log1p(-(2.0 ** (-5.0 - h)))
                    csel[:, 4 + h * 3 + j] = np.float32(np.exp(lg * float(S) * (pos - 1 - j)))
        cmask = np.concatenate([cc("maskAf") if first else cc("maskA"),
                                cc("maskBf") if first else cc("maskB")], 1).astype(np.float32)
        m = {"x": np.ascontiguousarray(np.asarray(inputs["x"][q, pos * S:(pos + 1) * S], np.float32)),
             "consts": consts, "ptab": ptab,
             "rot": np.ascontiguousarray(rot[:, :, pos * S:(pos + 1) * S]),
             "cmask": np.ascontiguousarray(cmask),
             "cflag": csel}
        for n in ("w_in", "w_branch", "w_gate_up", "w_out", "w_ff1", "w_ff2"):
            m[n] = np.asarray(inputs[n], np.float32)
        maps.append(m)
    return maps


def run_cfg(cfg, inputs, trace=False):
    nc = build_program(cfg)
    maps = make_in_maps(cfg, inputs)
    res = run_bass_kernel_spmd(nc, maps, core_ids=list(range(cfg.NCORE)), trace=trace)
    out = np.zeros((cfg.NSEQ, cfg.SEQ, cfg.D), np.float32)
    for c in range(cfg.NCORE):
        q, pos = c // cfg.P, c % cfg.P
        out[q, pos * cfg.S:(pos + 1) * cfg.S] = np.asarray(res.results[c]["out"])
    return out, res


def kernel(**inputs):
    cfg = Cfg()
    out, _ = run_cfg(cfg, inputs)
    return out
```
